# Optimizing a Trainium2 kernel written in Bass

```python
import jax, jax.numpy as jnp
from jax import lax
import numpy as np

D_MODEL = 2048
BATCH = 4
SEQ = 4096
DEPTH = 4

CTX_LEN = 256
GRID_W = 64
HEAD_DIM = D_MODEL // 16
ATTN_HEADS = 8
ATTN_KV_HEADS = 2
ATTN_GROUP = ATTN_HEADS // ATTN_KV_HEADS
FOURIER_GROUPS = 4
RET_HEADS = 4
ATTN_WIDTH = ATTN_HEADS * HEAD_DIM
KV_WIDTH = ATTN_KV_HEADS * HEAD_DIM
FOURIER_WIDTH = FOURIER_GROUPS * HEAD_DIM
RET_WIDTH = RET_HEADS * HEAD_DIM
MIX_WIDTH = ATTN_WIDTH + FOURIER_WIDTH + RET_WIDTH
SPLIT_POINTS = tuple(int(s) for s in np.cumsum(
    [ATTN_WIDTH, KV_WIDTH, KV_WIDTH, FOURIER_WIDTH, RET_WIDTH, RET_WIDTH, RET_WIDTH, RET_WIDTH]))
IN_WIDTH = ATTN_WIDTH + 2 * KV_WIDTH + FOURIER_WIDTH + 5 * RET_WIDTH
Q_BLOCK = 128
RET_CHUNK = 128
N_EXPERTS = 16
EXPERT_FF = D_MODEL // 2
CAPACITY_FACTOR = 2
ROPE_BASE = 10000.0
EPS = 1e-6

kernel_name = "hymba_style_attn_fourier_retention_ecmoe"


def rmsnorm(x, g):
    xf = x.astype(jnp.float32)
    y = xf * lax.rsqrt(jnp.mean(xf * xf, axis=-1, keepdims=True) + EPS)
    return (y * g.astype(jnp.float32)).astype(x.dtype)


def head_norm(o):
    of = o.astype(jnp.float32)
    mu = jnp.mean(of, axis=-1, keepdims=True)
    var = jnp.mean(jnp.square(of - mu), axis=-1, keepdims=True)
    return ((of - mu) * lax.rsqrt(var + EPS)).astype(o.dtype)


def adaln(cond, w, b):
    m = jax.nn.silu(cond) @ w + b
    m = m.reshape(m.shape[:-1] + (6, D_MODEL))
    return [m[..., i, :][..., None, :] for i in range(6)]


def modulate(h, shift, scale):
    return h * (1 + scale) + shift


def _rope(x, pos):
    half = x.shape[-1] // 2
    freqs = ROPE_BASE ** (-jnp.arange(half, dtype=jnp.float32) / half)
    ang = pos.astype(jnp.float32)[:, None] * freqs[None, :]
    cos = jnp.cos(ang)[:, None, :].astype(x.dtype)
    sin = jnp.sin(ang)[:, None, :].astype(x.dtype)
    x1, x2 = x[..., :half], x[..., half:]
    return jnp.concatenate([x1 * cos - x2 * sin, x1 * sin + x2 * cos], axis=-1)


def axial_rope(x):
    L, d = x.shape[1], x.shape[-1]
    n_rows = L // GRID_W
    rows = jnp.repeat(jnp.arange(n_rows), GRID_W)
    cols = jnp.tile(jnp.arange(GRID_W), n_rows)
    half = d // 2
    return jnp.concatenate([_rope(x[..., :half], rows), _rope(x[..., half:], cols)], axis=-1)


def _attend(q, k, v):
    B, Lq, H, dh = q.shape
    qg = q.reshape(B, Lq, ATTN_KV_HEADS, ATTN_GROUP, dh)
    s = jnp.einsum('bqkgd,bskd->bkgqs', qg, k).astype(jnp.float32) * (dh ** -0.5)
    p = jax.nn.softmax(s, axis=-1).astype(v.dtype)
    o = jnp.einsum('bkgqs,bskd->bqkgd', p, v)
    return o.reshape(B, Lq, H * dh)


def attend_blocks(q, k, v):
    B, S, H, dh = q.shape
    nb = S // Q_BLOCK
    qb = q.reshape(B, nb, Q_BLOCK, H, dh).transpose(1, 0, 2, 3, 4)
    o = lax.map(lambda qblk: _attend(qblk, k, v), qb)
    return o.transpose(1, 0, 2, 3).reshape(B, S, H * dh)


def fourier_mix(f):
    B, L, _ = f.shape
    fg = f.reshape(B, L, FOURIER_GROUPS, HEAD_DIM).astype(jnp.float32)
    y = jnp.fft.fft2(fg, axes=(1, 3), norm='ortho').real
    return y.reshape(B, L, FOURIER_WIDTH).astype(f.dtype)


def retention_chunkwise(q, k, v, log_decay, state0):
    B, L, H, dk = q.shape
    dt = q.dtype
    n = L // RET_CHUNK
    lg = log_decay.astype(jnp.float32)
    i = jnp.arange(RET_CHUNK, dtype=jnp.float32)
    diff = i[:, None] - i[None, :]
    intra = jnp.where(diff[None] >= 0,
                      jnp.exp(lg[:, None, None] * jnp.maximum(diff, 0.0)[None]), 0.0).astype(dt)
    q_dec = jnp.exp(lg[None, :] * (i[:, None] + 1.0)).astype(dt)
    k_dec = jnp.exp(lg[None, :] * (RET_CHUNK - 1.0 - i)[:, None]).astype(dt)
    c_dec = jnp.exp(lg * RET_CHUNK).astype(dt)

    def chunks(a):
        return a.reshape(B, n, RET_CHUNK, H, a.shape[-1]).transpose(1, 0, 2, 3, 4)

    def step(state, inp):
        qc, kc, vc = inp
        sc = jnp.einsum('bihd,bjhd->bhij', qc, kc) * intra[None]
        o = (jnp.einsum('bhij,bjhe->bihe', sc, vc)
             + jnp.einsum('bihd,bhde->bihe', qc, state) * q_dec[None, :, :, None])
        state = (state * c_dec[None, :, None, None]
                 + jnp.einsum('bjhd,bjhe->bhde', kc * k_dec[None, :, :, None], vc))
        return state, o

    state, o = lax.scan(step, state0, (chunks(q), chunks(k), chunks(v)))
    return o.transpose(1, 0, 2, 3, 4).reshape(B, L, H, v.shape[-1]), state


def retention_direction(q_c, k_c, v_c, q_l, k_l, v_l, log_decay):
    B, Lc, H, dk = q_c.shape
    Ls = q_l.shape[1]
    pos_c = jnp.arange(Lc)
    pos_l = Lc + jnp.arange(Ls)
    state0 = jnp.zeros((B, H, dk, v_c.shape[-1]), q_c.dtype)
    o_c, state = retention_chunkwise(_rope(q_c, pos_c), _rope(k_c, pos_c), v_c, log_decay, state0)
    o_l, _ = retention_chunkwise(_rope(q_l, pos_l), _rope(k_l, pos_l), v_l, log_decay, state)
    return o_c, o_l


def ret_combine(o_f, o_b, g_f, g_b):
    B, L = g_f.shape[:2]
    gn = lambda o: head_norm(o).reshape(B, L, RET_WIDTH)
    return jax.nn.silu(g_f) * gn(o_f) + jax.nn.silu(g_b) * gn(o_b)


def in_project(h, w_in_l, q_gain, k_gain):
    B, L, _ = h.shape
    p = jnp.einsum('bld,de->ble', h, w_in_l)
    q, k, v, f, rq, rk, rv, gf, gb = jnp.split(p, SPLIT_POINTS, axis=-1)
    heads = lambda a, nh: a.reshape(B, L, nh, HEAD_DIM)
    q = rmsnorm(heads(q, ATTN_HEADS), q_gain)
    k = rmsnorm(heads(k, ATTN_KV_HEADS), k_gain)
    v = heads(v, ATTN_KV_HEADS)
    rq = heads(rq, RET_HEADS)
    rk = heads(rk, RET_HEADS) * (HEAD_DIM ** -0.5)
    rv = heads(rv, RET_HEADS)
    return q, k, v, f, rq, rk, rv, gf, gb


def ec_moe(h, w_router_l, w_gate_l, w_up_l, w_down_l):
    B, N, D = h.shape
    cap = CAPACITY_FACTOR * N // N_EXPERTS
    logits = jnp.einsum('bnd,de->bne', h, w_router_l).astype(jnp.float32)
    aff = jax.nn.softmax(logits, axis=-1)
    gate, idx = lax.top_k(aff.transpose(0, 2, 1), cap)
    xg = jax.vmap(lambda hb, ib: hb[ib])(h, idx)
    a = jnp.einsum('becd,edf->becf', xg, w_gate_l)
    u = jnp.einsum('becd,edf->becf', xg, w_up_l)
    y = jnp.einsum('becf,efd->becd', jax.nn.silu(a) * u, w_down_l)
    y = y * gate[..., None].astype(y.dtype)
    return jax.vmap(lambda ib, yb: jnp.zeros((N, D), yb.dtype).at[ib.reshape(-1)].add(
        yb.reshape(-1, D)))(idx, y)


def layer(x, xc, c, c_ctx, w_ada_l, b_ada_l, g_mix, g_ffn, w_in_l, q_gain, k_gain,
          log_decay_l, w_out_l, w_router_l, w_gate_l, w_up_l, w_down_l, ctx_out):
    sh1, sc1, gt1, sh2, sc2, gt2 = adaln(c, w_ada_l, b_ada_l)
    sh1c, sc1c, gt1c, sh2c, sc2c, gt2c = adaln(c_ctx, w_ada_l, b_ada_l)

    h = modulate(rmsnorm(x, g_mix), sh1, sc1)
    hc = modulate(rmsnorm(xc, g_mix), sh1c, sc1c)
    q, k, v, f, rq, rk, rv, gf, gb = in_project(h, w_in_l, q_gain, k_gain)
    qc, kc, vc, fc, rqc, rkc, rvc, gfc, gbc = in_project(hc, w_in_l, q_gain, k_gain)

    q, k = axial_rope(q), axial_rope(k)
    a = attend_blocks(q, jnp.concatenate([kc, k], axis=1), jnp.concatenate([vc, v], axis=1))

    ocf, olf = retention_direction(rqc, rkc, rvc, rq, rk, rv, log_decay_l[0])
    flip = lambda t: jnp.flip(t, axis=1)
    ocb, olb = retention_direction(flip(rqc), flip(rkc), flip(rvc), flip(rq), flip(rk), flip(rv),
                                   log_decay_l[1])
    ocb, olb = flip(ocb), flip(olb)
    r = ret_combine(olf, olb, gf, gb)

    y = jnp.concatenate([a, fourier_mix(f), r], axis=-1) @ w_out_l
    x = x + gt1 * y
    x = x + gt2 * ec_moe(modulate(rmsnorm(x, g_ffn), sh2, sc2), w_router_l, w_gate_l, w_up_l, w_down_l)

    if ctx_out:
        ac = _attend(qc, kc, vc)
        rc = ret_combine(ocf, ocb, gfc, gbc)
        yc = jnp.concatenate([ac, fourier_mix(fc), rc], axis=-1) @ w_out_l
        xc = xc + gt1c * yc
        xc = xc + gt2c * ec_moe(modulate(rmsnorm(xc, g_ffn), sh2c, sc2c),
                                w_router_l, w_gate_l, w_up_l, w_down_l)
    return x, xc


def setup_inputs(seed: int = 0) -> dict:
    key = jax.random.key(seed)
    ks = jax.random.split(key, 17)
    f32 = jnp.float32
    nrm = lambda k, shape, s: jax.random.normal(k, shape, f32) * s
    base_decay = jnp.log(1.0 - 2.0 ** (-5.0 - jnp.arange(RET_HEADS, dtype=f32)))
    return {
        "x": nrm(ks[0], (BATCH, SEQ, D_MODEL), 1.0),
        "c": nrm(ks[1], (BATCH, D_MODEL), 1.0),
        "ctx": nrm(ks[2], (BATCH, CTX_LEN, D_MODEL), 1.0),
        "c_ctx": nrm(ks[3], (D_MODEL,), 1.0),
        "w_ada": nrm(ks[4], (DEPTH, D_MODEL, 6 * D_MODEL), 0.5 * D_MODEL ** -0.5),
        "b_ada": nrm(ks[5], (DEPTH, 6 * D_MODEL), 0.02),
        "norm_mix": 1.0 + nrm(ks[6], (DEPTH, D_MODEL), 0.02),
        "norm_ffn": 1.0 + nrm(ks[7], (DEPTH, D_MODEL), 0.02),
        "w_in": nrm(ks[8], (DEPTH, D_MODEL, IN_WIDTH), D_MODEL ** -0.5),
        "q_norm": 1.0 + nrm(ks[9], (DEPTH, HEAD_DIM), 0.02),
        "k_norm": 1.0 + nrm(ks[10], (DEPTH, HEAD_DIM), 0.02),
        "ret_log_decay": base_decay * (1.0 + nrm(ks[11], (DEPTH, 2, RET_HEADS), 0.05)),
        "w_out": nrm(ks[12], (DEPTH, MIX_WIDTH, D_MODEL), MIX_WIDTH ** -0.5),
        "w_router": nrm(ks[13], (DEPTH, D_MODEL, N_EXPERTS), D_MODEL ** -0.5),
        "w_gate": nrm(ks[14], (DEPTH, N_EXPERTS, D_MODEL, EXPERT_FF), D_MODEL ** -0.5),
        "w_up": nrm(ks[15], (DEPTH, N_EXPERTS, D_MODEL, EXPERT_FF), D_MODEL ** -0.5),
        "w_down": nrm(ks[16], (DEPTH, N_EXPERTS, EXPERT_FF, D_MODEL), EXPERT_FF ** -0.5),
    }


def reference(x, c, ctx, c_ctx, w_ada, b_ada, norm_mix, norm_ffn, w_in, q_norm, k_norm,
              ret_log_decay, w_out, w_router, w_gate, w_up, w_down):
    xc = ctx
    for l in range(DEPTH):
        x, xc = layer(x, xc, c, c_ctx, w_ada[l], b_ada[l], norm_mix[l], norm_ffn[l], w_in[l],
                      q_norm[l], k_norm[l], ret_log_decay[l], w_out[l], w_router[l],
                      w_gate[l], w_up[l], w_down[l], l < DEPTH - 1)
    return x
```

```python
import os
import numpy as np
import ml_dtypes
from contextlib import ExitStack
import concourse.bass as bass
import concourse.mybir as mybir
from concourse.bass_utils import run_bass_kernel_spmd

F32 = mybir.dt.float32
BF16 = mybir.dt.bfloat16
ALU = mybir.AluOpType
AF = mybir.ActivationFunctionType
AX = mybir.AxisListType

D = 2048
S = 4096
LC = 256
T = S + LC
NT = T // 128
NE = 16
FF = 1024
INW = 4608
CAP_L = 512
CAP_C = 32
NSL = CAP_L + CAP_C
EPS = 1e-6


class Buf:
    __slots__ = ("t", "w", "r", "name")

    def __init__(self, t, name=""):
        self.t = t
        self.w = None
        self.r = {}
        self.name = name

    def __getitem__(self, idx):
        return self.t[idx]


class TK:
    NDMA = 8

    def __init__(self, nc, es):
        self.nc = nc
        self.es = es
        self.eng = {"pe": nc.tensor, "act": nc.scalar, "dve": nc.vector, "pool": nc.gpsimd, "sp": nc.sync}
        self.csem = {k: es.enter_context(nc.semaphore("s_" + k)) for k in ("pe", "act", "dve", "pool")}
        self.ccnt = {k: 0 for k in self.csem}
        self.dsem = {q: [es.enter_context(nc.semaphore(f"d_{q}{i}")) for i in range(self.NDMA)] for q in ("sp", "pool")}
        self.dcnt = {q: [0] * self.NDMA for q in self.dsem}
        self.dnext = {q: 0 for q in self.dsem}
        self.waited = {}
        self.n_inst = 0
        self.uid = 0

    def sb(self, es, name, shape, dt):
        self.uid += 1
        return Buf(es.enter_context(self.nc.sbuf_tensor(f"{name}_{self.uid}", shape, dt)), name)

    def ps(self, es, name, shape, dt=F32):
        self.uid += 1
        return Buf(es.enter_context(self.nc.psum_tensor(f"{name}_{self.uid}", shape, dt)), name)

    def dram(self, name, shape, dt, kind="Internal"):
        return Buf(self.nc.dram_tensor(name, shape, dt, kind=kind), name)

    def _wait(self, e, tok):
        sem, val = tok[0], tok[1]
        key = (e, id(sem))
        if self.waited.get(key, 0) >= val:
            return
        self.waited[key] = val
        self.eng[e].wait_ge(sem, val)

    def _deps(self, e, reads, writes, skip_same):
        for b in list(reads) + list(writes):
            if b.w is not None and not (skip_same and b.w[2] == e):
                self._wait(e, b.w)
        for b in writes:
            for tok in b.r.values():
                if not (skip_same and tok[2] == e):
                    self._wait(e, tok)

    def _commit(self, tok, reads, writes):
        for b in writes:
            b.w = tok
            b.r = {}
        for b in reads:
            if b in writes:
                continue
            b.r[id(tok[0])] = tok

    def op(self, e, fn, reads=(), writes=(), skip_same=False):
        self._deps(e, reads, writes, skip_same)
        ins = fn()
        self.ccnt[e] += 1
        ins.then_inc(self.csem[e], 1)
        tok = (self.csem[e], self.ccnt[e], e)
        self._commit(tok, reads, writes)
        self.n_inst += 1
        return tok

    def dma(self, q, out, in_, reads=(), writes=(), **kw):
        i = self.dnext[q]
        self.dnext[q] = (i + 1) % self.NDMA
        sem = self.dsem[q][i]
        if self.dcnt[q][i] > 0:
            self._wait(q, (sem, 16 * self.dcnt[q][i]))
        self._deps(q, reads, writes, False)
        ins = self.eng[q].dma_start(out=out, in_=in_, **kw)
        self.dcnt[q][i] += 1
        ins.then_inc(sem, 16)
        tok = (sem, 16 * self.dcnt[q][i], "dma_" + q)
        self._commit(tok, reads, writes)
        self.n_inst += 1
        return tok

    def barrier(self):
        for e in ("pe", "act", "dve", "pool", "sp"):
            for k in self.csem:
                if self.ccnt[k] > 0:
                    self._wait(e, (self.csem[k], self.ccnt[k]))
            for q in self.dsem:
                for i in range(self.NDMA):
                    if self.dcnt[q][i] > 0:
                        self._wait(e, (self.dsem[q][i], 16 * self.dcnt[q][i]))


def build(depth):
    nc = bass.Bass("TRN2", target_bir_lowering=False)
    top = ExitStack()
    tk = TK(nc, top)
    V, A, PE = nc.vector, nc.scalar, nc.tensor

    def dve(fn, r, w):
        return tk.op("dve", fn, r, w)

    def act(fn, r, w):
        return tk.op("act", fn, r, w)

    def mm(ps_ap, lhsT, rhs, start, stop, r, w):
        return tk.op("pe", lambda: PE.matmul(ps_ap, lhsT, rhs, start=start, stop=stop), r, w, skip_same=True)

    flip = [0]

    def evac(out_ap, in_ap, r, w):
        flip[0] ^= 1
        if flip[0]:
            return dve(lambda: V.tensor_copy(out_ap, in_ap), r, w)
        return act(lambda: A.copy(out_ap, in_ap), r, w)

    EI = "ExternalInput"
    x_in = tk.dram("x", [S, D], F32, EI)
    ctx_in = tk.dram("ctx", [LC, D], F32, EI)
    cvec = tk.dram("cvec", [2, D], F32, EI)
    w_ada = tk.dram("w_ada", [depth * D, 6 * D], F32, EI)
    b_ada = tk.dram("b_ada", [depth, 6 * D], F32, EI)
    norm_mix = tk.dram("norm_mix", [depth, D], F32, EI)
    norm_ffn = tk.dram("norm_ffn", [depth, D], F32, EI)
    w_in = tk.dram("w_in", [depth * D, INW], F32, EI)
    q_norm = tk.dram("q_norm", [depth, 128], F32, EI)
    k_norm = tk.dram("k_norm", [depth, 128], F32, EI)
    rld = tk.dram("rld", [depth, 8], F32, EI)
    w_out = tk.dram("w_out", [depth * D, D], F32, EI)
    w_router = tk.dram("w_router", [depth * D, NE], F32, EI)
    w_gate = tk.dram("w_gate", [depth * NE * D, FF], F32, EI)
    w_up = tk.dram("w_up", [depth * NE * D, FF], F32, EI)
    w_down = tk.dram("w_down", [depth * NE * FF, D], F32, EI)
    t_ident = tk.dram("t_ident", [128, 128], F32, EI)
    t_ropeA = tk.dram("t_ropeA", [T, 256], F32, EI)
    t_ropeF = tk.dram("t_ropeF", [T, 256], F32, EI)
    t_ropeB = tk.dram("t_ropeB", [T, 256], F32, EI)
    t_CL = tk.dram("t_CL", [S, S], BF16, EI)
    t_SL = tk.dram("t_SL", [S, S], BF16, EI)
    t_CL2 = tk.dram("t_CL2", [LC, LC], BF16, EI)
    t_SL2 = tk.dram("t_SL2", [LC, LC], BF16, EI)
    t_CC = tk.dram("t_CC", [128, 256], BF16, EI)
    t_ret = tk.dram("t_ret", [128, 6 * 128 + 2], F32, EI)
    t_ustr = tk.dram("t_ustr", [128, 128], F32, EI)
    out_d = tk.dram("out", [S, D], F32, "ExternalOutput")

    xcur = tk.dram("xcur", [T, D], F32)
    mod_d = tk.dram("mod_d", [depth * 12 * 128, D], F32)
    hT_d = tk.dram("hT_d", [T, D], BF16)
    p_d = tk.dram("p_d", [T, INW], F32)
    qT_d = tk.dram("qT_d", [1024, T], BF16)
    kT_d = tk.dram("kT_d", [256, T], BF16)
    v_d = tk.dram("v_d", [T, 256], BF16)
    f_d = tk.dram("f_d", [T, 512], BF16)
    rT_d = tk.dram("rT_d", [4 * 512, T], BF16)
    rk_d = tk.dram("rk_d", [T, 1024], BF16)
    rv_d = tk.dram("rv_d", [T, 512], BF16)
    g_d = tk.dram("g_d", [T, 1024], F32)
    ro_d = tk.dram("ro_d", [T, 1024], F32)
    mixT_d = tk.dram("mixT_d", [D, T], BF16)
    h2_d = tk.dram("h2_d", [T, D], BF16)
    xg_d = tk.dram("xg_d", [NE * D, NSL], BF16)
    y_d = tk.dram("y_d", [NE * NSL, D], BF16)

    identf = tk.sb(top, "identf", [128, 128], F32)
    identb = tk.sb(top, "identb", [128, 128], BF16)
    onesf = tk.sb(top, "onesf", [128, 128], F32)
    onesb = tk.sb(top, "onesb", [128, 128], BF16)
    ustrb = tk.sb(top, "ustrb", [128, 128], BF16)
    iota5 = tk.sb(top, "iota5", [128, 512], F32)
    iotap = tk.sb(top, "iotap", [128, 8], F32)
    tk.dma("sp", identf[:, :], t_ident[:, :], [t_ident], [identf])
    tk.dma("pool", identb[:, :], t_ident[:, :], [t_ident], [identb])
    tk.dma("pool", ustrb[:, :], t_ustr[:, :], [t_ustr], [ustrb])
    tk.op("pool", lambda: nc.gpsimd.memset(onesf[:, :], 1.0), [], [onesf])
    tk.op("pool", lambda: nc.gpsimd.memset(onesb[:, :], 1.0), [], [onesb])
    tk.op("pool", lambda: nc.gpsimd.iota(iota5[:, :], [[1, 512]], base=0, channel_multiplier=0,
                                          allow_small_or_imprecise_dtypes=True), [], [iota5])
    tk.op("pool", lambda: nc.gpsimd.iota(iotap[:, :], [[128, 8]], base=0, channel_multiplier=1,
                                          allow_small_or_imprecise_dtypes=True), [], [iotap])
    aff_all = tk.sb(top, "aff_all", [128, NT, 16], F32)
    sidx_all = tk.sb(top, "sidx_all", [128, NT, 16], F32)
    gate_all = tk.sb(top, "gate_all", [128, NT, 16], F32)
    shl_d = tk.dram("shl_d", [48, T], BF16)
    gtb_d = tk.dram("gtb_d", [16, T], BF16)
    t_esel = tk.dram("t_esel", [48, 16 * 128], F32, "ExternalInput")

    def modrow(l, w, i):
        r0 = ((l * 12) + w * 6 + i) * 128
        return mod_d[r0:r0 + 128, :]

    def stage_adaln():
        with ExitStack() as es:
            condT = tk.sb(es, "condT", [128, 2, 16], F32)
            scb = [tk.sb(es, f"scb{w}", [128, 16, 128], F32) for w in range(2)]
            wts = [tk.sb(es, f"wada{i}", [128, 16, 512], F32) for i in range(2)]
            brow = [tk.sb(es, f"brow{i}", [1, 512], F32) for i in range(2)]
            pss = [tk.ps(es, f"psada{i}", [128, 512]) for i in range(2)]
            ot = [tk.sb(es, f"oada{i}", [128, 512], F32) for i in range(2)]
            tk.dma("sp", condT[:, :, :], cvec[:, :].rearrange("w (c p) -> p w c", p=128), [cvec], [condT],
                   allow_slow_non_contiguous=True)
            act(lambda: A.activation(condT[:, :, :], condT[:, :, :], AF.Silu), [condT], [condT])
            for w in range(2):
                for c in range(16):
                    dve(lambda w=w, c=c: V.tensor_scalar(scb[w][:, c, :], onesf[:, :], condT[:, w, c:c + 1], None, ALU.mult),
                        [onesf, condT], [scb[w]])
            k = 0
            for l in range(depth):
                for nb in range(24):
                    wt = wts[nb % 2]
                    br = brow[nb % 2]
                    tk.dma("sp", wt[:, :, :], w_ada[l * D:(l + 1) * D, nb * 512:(nb + 1) * 512].rearrange("(c p) n -> p c n", p=128),
                           [w_ada], [wt])
                    tk.dma("sp", br[:, :], b_ada[l:l + 1, nb * 512:(nb + 1) * 512], [b_ada], [br])
                    piece = nb // 4
                    for w in range(2):
                        ps = pss[k % 2]
                        o = ot[k % 2]
                        k += 1
                        for c in range(16):
                            mm(ps[:, :], scb[w][:, c, :], wt[:, c, :], c == 0, False, [scb[w], wt], [ps])
                        mm(ps[:, :], onesf[0:1, :], br[0:1, :], False, True, [onesf, br], [ps])
                        if piece in (1, 4):
                            dve(lambda ps=ps, o=o: V.tensor_scalar(o[:, :], ps[:, :], 1.0, None, ALU.add), [ps], [o])
                        else:
                            evac(o[:, :], ps[:, :], [ps], [o])
                        tk.dma("sp", modrow(l, w, piece)[:, (nb % 4) * 512:(nb % 4 + 1) * 512], o[:, :], [o], [mod_d])
        tk.barrier()

    def rms_rstd(es_bufs, xt, width):
        junk, st = es_bufs
        act(lambda: A.activation(junk[:, 0:width], xt[:, 0:width], AF.Square, accum_out=st[:, 0:1]), [xt], [junk, st])
        dve(lambda: V.tensor_scalar(st[:, 1:2], st[:, 0:1], 1.0 / width, EPS, ALU.mult, ALU.add), [st], [st])
        act(lambda: A.activation(st[:, 2:3], st[:, 1:2], AF.Sqrt), [st], [st])
        dve(lambda: V.reciprocal(st[:, 3:4], st[:, 2:3]), [st], [st])
        return st

    def stage_A(l):
        with ExitStack() as es:
            gm = tk.sb(es, "gm", [128, D], F32)
            G = [tk.sb(es, f"G{w}", [128, D], F32) for w in range(2)]
            SH = [tk.sb(es, f"SH{w}", [128, D], F32) for w in range(2)]
            xts = [tk.sb(es, f"xt{i}", [128, D], F32) for i in range(2)]
            junk = tk.sb(es, "junk", [128, D], BF16)
            tmp = tk.sb(es, "tmpA", [128, D], F32)
            hb = [tk.sb(es, f"hb{i}", [128, D], BF16) for i in range(2)]
            hTt = [tk.sb(es, f"hTt{i}", [128, 16, 128], BF16) for i in range(2)]
            sts = [tk.sb(es, f"stA{i}", [128, 4], F32) for i in range(2)]
            pst = [tk.ps(es, f"pstA{i}", [128, 512]) for i in range(4)]
            tk.dma("sp", gm[:, :], norm_mix[l, :].partition_broadcast(128), [norm_mix], [gm])
            for w in range(2):
                tk.dma("sp", G[w][:, :], modrow(l, w, 1), [mod_d], [G[w]])
                tk.dma("sp", SH[w][:, :], modrow(l, w, 0), [mod_d], [SH[w]])
                dve(lambda w=w: V.tensor_tensor(G[w][:, :], G[w][:, :], gm[:, :], ALU.mult), [G[w], gm], [G[w]])
            for t in range(NT):
                w = 1 if t < 2 else 0
                xt = xts[t % 2]
                if l == 0:
                    src = ctx_in[t * 128:(t + 1) * 128, :] if t < 2 else x_in[(t - 2) * 128:(t - 1) * 128, :]
                    tk.dma("sp", xt[:, :], src, [], [xt])
                    tk.dma("sp", xcur[t * 128:(t + 1) * 128, :], xt[:, :], [xt], [xcur])
                else:
                    tk.dma("sp", xt[:, :], xcur[t * 128:(t + 1) * 128, :], [xcur], [xt])
                st = rms_rstd((junk, sts[t % 2]), xt, D)
                h = hb[t % 2]
                dve(lambda xt=xt, st=st, w=w: V.scalar_tensor_tensor(tmp[:, :], xt[:, :], st[:, 3:4], G[w][:, :], ALU.mult, ALU.mult),
                    [xt, st, G[w]], [tmp])
                dve(lambda h=h, w=w: V.tensor_tensor(h[:, :], tmp[:, :], SH[w][:, :], ALU.add), [tmp, SH[w]], [h])
                ht = hTt[t % 2]
                for g4 in range(4):
                    ps = pst[g4]
                    for j in range(4):
                        c = g4 * 4 + j
                        mm(ps[:, j * 128:(j + 1) * 128], h[:, c * 128:(c + 1) * 128], identb[:, :], True, True, [h, identb], [ps])
                    evac(ht[:, g4 * 4:(g4 + 1) * 4, :], ps[:, :].rearrange("p (j n) -> p j n", j=4), [ps], [ht])
                tk.dma("sp", hT_d[t * 128:(t + 1) * 128, :], ht[:, :, :].rearrange("p c n -> p (c n)"), [ht], [hT_d])
        tk.barrier()

    def stage_B(l):
        with ExitStack() as es:
            hTall = tk.sb(es, "hTall", [128, NT, D], BF16)
            hts = [Buf(hTall.t, f"hTall{t}") for t in range(NT)]
            wg = [tk.sb(es, f"wgB{i}", [128, 16, 512], BF16) for i in range(2)]
            pss = [tk.ps(es, f"psB{i}", [128, 512]) for i in range(4)]
            ot = [tk.sb(es, f"oB{i}", [128, 512], F32) for i in range(4)]
            for t in range(NT):
                tk.dma("sp", hTall[:, t, :], hT_d[t * 128:(t + 1) * 128, :], [hT_d], [hts[t]])
            k = 0
            for g in range(9):
                w = wg[g % 2]
                tk.dma("pool", w[:, :, :], w_in[l * D:(l + 1) * D, g * 512:(g + 1) * 512].rearrange("(c p) n -> p c n", p=128),
                       [w_in], [w])
                for t in range(NT):
                    ps = pss[k % 4]
                    o = ot[k % 4]
                    k += 1
                    for c in range(16):
                        mm(ps[:, :], hTall[:, t, c * 128:(c + 1) * 128], w[:, c, :], c == 0, c == 15, [hts[t], w], [ps])
                    evac(o[:, :], ps[:, :], [ps], [o])
                    tk.dma("sp", p_d[t * 128:(t + 1) * 128, g * 512:(g + 1) * 512], o[:, :], [o], [p_d])
        tk.barrier()

    def stage_C(l):
        with ExitStack() as es:
            gqk = tk.sb(es, "gqk", [128, 10, 128], F32)
            pts = [tk.sb(es, f"ptC{i}", [128, INW], F32) for i in range(2)]
            ropes = [tk.sb(es, f"ropeC{i}", [128, 3, 256], F32) for i in range(2)]
            junk = tk.sb(es, "junkC", [128, 1280], F32)
            st = tk.sb(es, "stC", [128, 40], F32)
            qn = tk.sb(es, "qn", [128, 10, 128], F32)
            t1 = tk.sb(es, "t1C", [128, 10, 128], F32)
            t2 = tk.sb(es, "t2C", [128, 10, 128], F32)
            qr = tk.sb(es, "qr", [128, 1280], BF16)
            rr = tk.sb(es, "rr", [128, 8, 128], F32)
            rfb = [tk.sb(es, f"rfb{i}", [128, 1024], BF16) for i in range(2)]
            gs = tk.sb(es, "gsC", [128, 1024], F32)
            trs = [tk.sb(es, f"trC{i}", [128, 512], BF16) for i in range(2)]
            pst = [tk.ps(es, f"pstC{i}", [128, 512]) for i in range(2)]
            for h in range(10):
                src = q_norm if h < 8 else k_norm
                tk.dma("sp", gqk[:, h, :], src[l, :].partition_broadcast(128), [src], [gqk])
            kk = [0]

            def transpose_out(srcbuf, col0, nblk, dst, row0, t):
                ps = pst[kk[0] % 2]
                tr = trs[kk[0] % 2]
                kk[0] += 1
                for j in range(nblk):
                    mm(ps[:, j * 128:(j + 1) * 128], srcbuf[:, col0 + j * 128:col0 + (j + 1) * 128], identb[:, :], True, True,
                       [srcbuf, identb], [ps])
                evac(tr[:, 0:nblk * 128], ps[:, 0:nblk * 128], [ps], [tr])
                tk.dma("sp", dst[row0:row0 + nblk * 128, t * 128:(t + 1) * 128].rearrange("(j p) n -> p j n", p=128),
                       tr[:, 0:nblk * 128].rearrange("p (j n) -> p j n", j=nblk), [tr], [dst])

            def rope(src3, nh, half, cs, out3):
                nb = 128 // (2 * half)
                cosb = cs[:, 0:128].unsqueeze(1).to_broadcast([128, nh, 128])
                dve(lambda: V.tensor_tensor(t1[:, 0:nh, :], src3, cosb, ALU.mult), [qn, rr, ropes[0], ropes[1]], [t1])
                sv = src3.rearrange("p h (b s w) -> p h b s w", b=nb, s=2)
                tv = t2[:, 0:nh, :].rearrange("p h (b s w) -> p h b s w", b=nb, s=2)
                sn = cs[:, 128:256].rearrange("p (b s w) -> p b s w", b=nb, s=2)
                for b in range(nb):
                    for s_ in range(2):
                        sb_ = sn[:, b, s_, :].unsqueeze(1).to_broadcast([128, nh, half])
                        dve(lambda b=b, s_=s_, sb_=sb_: V.tensor_tensor(tv[:, :, b, s_, :], sv[:, :, b, 1 - s_, :], sb_, ALU.mult),
                            [qn, rr, ropes[0], ropes[1]], [t2])
                dve(lambda: V.tensor_tensor(out3, t1[:, 0:nh, :], t2[:, 0:nh, :], ALU.add), [t1, t2], [qr, rfb[0], rfb[1]])

            for t in range(NT):
                pt = pts[t % 2]
                rp = ropes[t % 2]
                tk.dma("sp", pt[:, :], p_d[t * 128:(t + 1) * 128, :], [p_d], [pt])
                tk.dma("sp", rp[:, 0, :], t_ropeA[t * 128:(t + 1) * 128, :], [], [rp])
                tk.dma("sp", rp[:, 1, :], t_ropeF[t * 128:(t + 1) * 128, :], [], [rp])
                tk.dma("sp", rp[:, 2, :], t_ropeB[t * 128:(t + 1) * 128, :], [], [rp])
                qk3 = pt[:, 0:1280].rearrange("p (h d) -> p h d", h=10)
                act(lambda: A.activation(junk[:, :], pt[:, 0:1280], AF.Square), [pt], [junk])
                dve(lambda: V.tensor_reduce(st[:, 0:10], junk[:, :].rearrange("p (h d) -> p h d", h=10), AX.X, ALU.add), [junk], [st])
                dve(lambda: V.tensor_scalar(st[:, 10:20], st[:, 0:10], 1.0 / 128, EPS, ALU.mult, ALU.add), [st], [st])
                act(lambda: A.activation(st[:, 20:30], st[:, 10:20], AF.Sqrt), [st], [st])
                dve(lambda: V.reciprocal(st[:, 30:40], st[:, 20:30]), [st], [st])
                dve(lambda: V.tensor_tensor(qn[:, :, :], qk3, st[:, 30:40].unsqueeze(2).to_broadcast([128, 10, 128]), ALU.mult),
                    [pt, st], [qn])
                dve(lambda: V.tensor_tensor(qn[:, :, :], qn[:, :, :], gqk[:, :, :], ALU.mult), [qn, gqk], [qn])
                rope(qn[:, :, :], 10, 32, rp[:, 0, :], qr[:, :].rearrange("p (h d) -> p h d", h=10))
                transpose_out(qr, 0, 4, qT_d, 0, t)
                transpose_out(qr, 512, 4, qT_d, 512, t)
                transpose_out(qr, 1024, 2, kT_d, 0, t)
                tk.dma("pool", v_d[t * 128:(t + 1) * 128, :], pt[:, 1280:1536], [pt], [v_d])
                tk.dma("pool", f_d[t * 128:(t + 1) * 128, :], pt[:, 1536:2048], [pt], [f_d])
                tk.dma("pool", rv_d[t * 128:(t + 1) * 128, :], pt[:, 3072:3584], [pt], [rv_d])
                dve(lambda: V.tensor_copy(rr[:, 0:4, :], pt[:, 2048:2560].rearrange("p (h d) -> p h d", h=4)), [pt], [rr])
                dve(lambda: V.tensor_scalar(rr[:, 4:8, :], pt[:, 2560:3072].rearrange("p (h d) -> p h d", h=4), 128.0 ** -0.5, None, ALU.mult),
                    [pt], [rr])
                for di in range(2):
                    rope(rr[:, :, :], 8, 64, rp[:, 1 + di, :], rfb[di][:, :].rearrange("p (h d) -> p h d", h=8))
                    transpose_out(rfb[di], 0, 4, rT_d, di * 1024, t)
                    transpose_out(rfb[di], 512, 4, rT_d, di * 1024 + 512, t)
                    tk.dma("sp", rk_d[t * 128:(t + 1) * 128, di * 512:(di + 1) * 512], rfb[di][:, 512:1024], [rfb[di]], [rk_d])
                act(lambda: A.activation(gs[:, :], pt[:, 3584:4608], AF.Silu), [pt], [gs])
                tk.dma("sp", g_d[t * 128:(t + 1) * 128, :], gs[:, :], [gs], [g_d])
        tk.barrier()

    def stage_D(l):
        with ExitStack() as es:
            gq = tk.sb(es, "gqD", [128, 256], F32)
            stb = tk.sb(es, "stD", [128, 4], F32)
            kT = tk.sb(es, "kTD", [128, T], BF16)
            vv = tk.sb(es, "vD", [128, NT, 128], BF16)
            qTb = [tk.sb(es, f"qTb{i}", [128, 512], BF16) for i in range(2)]
            pT = [tk.sb(es, f"pT{i}", [128, 512], BF16) for i in range(2)]
            psS = [tk.ps(es, f"psS{i}", [128, 512]) for i in range(2)]
            psO = [tk.ps(es, f"psO{i}", [128, 512]) for i in range(2)]
            psZ = [tk.ps(es, f"psZ{i}", [128, 512]) for i in range(2)]
            rz = tk.sb(es, "rzD", [128, 512], F32)
            ob = [tk.sb(es, f"obD{i}", [128, 512], BF16) for i in range(2)]
            tk.dma("sp", gq[:, 0:128], q_norm[l, :].partition_broadcast(128), [q_norm], [gq])
            tk.dma("sp", gq[:, 128:256], k_norm[l, :].partition_broadcast(128), [k_norm], [gq])
            dve(lambda: V.tensor_reduce(stb[:, 0:2], gq[:, :].rearrange("p (a d) -> p a d", a=2), AX.X, ALU.max, apply_absolute_value=True),
                [gq], [stb])
            dve(lambda: V.tensor_tensor(stb[:, 2:3], stb[:, 0:1], stb[:, 1:2], ALU.mult), [stb], [stb])
            dve(lambda: V.tensor_scalar(stb[:, 3:4], stb[:, 2:3], -(128.0 ** 0.5), None, ALU.mult), [stb], [stb])
            scale = 128.0 ** -0.5
            kq = 0
            for kvh in range(2):
                tk.dma("sp", kT[:, :], kT_d[kvh * 128:(kvh + 1) * 128, :], [kT_d], [kT])
                tk.dma("sp", vv[:, :, :], v_d[:, kvh * 128:(kvh + 1) * 128].rearrange("(c p) d -> p c d", p=128), [v_d], [vv])
                for hh in range(4):
                    hq = kvh * 4 + hh
                    blocks = [(0, 256, [0, 1])] + [(256 + qb * 512, 512, list(range(NT))) for qb in range(8)]
                    for (c0, nq, chunks) in blocks:
                        qb_ = qTb[kq % 2]
                        po = psO[kq % 2]
                        pz = psZ[kq % 2]
                        o = ob[kq % 2]
                        kq += 1
                        tk.dma("sp", qb_[:, 0:nq], qT_d[hq * 128:(hq + 1) * 128, c0:c0 + nq], [qT_d], [qb_])
                        nch = len(chunks)

                        def s_mm(i):
                            kc = chunks[i]
                            mm(psS[i % 2][:, 0:nq], kT[:, kc * 128:(kc + 1) * 128], qb_[:, 0:nq], True, True, [kT, qb_], [psS[i % 2]])
                        s_mm(0)
                        for i in range(nch):
                            kc = chunks[i]
                            if i + 1 < nch:
                                s_mm(i + 1)
                            p = pT[i % 2]
                            act(lambda i=i, p=p: A.activation(p[:, 0:nq], psS[i % 2][:, 0:nq], AF.Exp, bias=stb[:, 3:4], scale=scale),
                                [psS[i % 2], stb], [p])
                            mm(po[:, 0:nq], vv[:, kc, :], p[:, 0:nq], i == 0, i == nch - 1, [vv, p], [po])
                            mm(pz[:, 0:nq], onesb[:, :], p[:, 0:nq], i == 0, i == nch - 1, [onesb, p], [pz])
                        dve(lambda pz=pz: V.reciprocal(rz[:, 0:nq], pz[:, 0:nq]), [pz], [rz])
                        dve(lambda po=po, o=o: V.tensor_tensor(o[:, 0:nq], po[:, 0:nq], rz[:, 0:nq], ALU.mult), [po, rz], [o])
                        tk.dma("sp", mixT_d[hq * 128:(hq + 1) * 128, c0:c0 + nq], o[:, 0:nq], [o], [mixT_d])
        tk.barrier()

    def stage_E(l):
        with ExitStack() as es:
            cc = tk.sb(es, "ccE", [128, 256], BF16)
            tk.dma("sp", cc[:, :], t_CC[:, :], [], [cc])
            for (tok0, n, CLt, SLt, kbw) in ((LC, S, t_CL, t_SL, 512), (0, LC, t_CL2, t_SL2, 256)):
                nch = n // 128
                with ExitStack() as es2:
                    xf = tk.sb(es2, "xfE", [128, nch, 512], BF16)
                    CLp = tk.sb(es2, "CLp", [128, nch, kbw], BF16)
                    SLp = tk.sb(es2, "SLp", [128, nch, kbw], BF16)
                    psA = tk.ps(es2, "psAE", [128, 512])
                    psB = tk.ps(es2, "psBE", [128, 512])
                    psY = tk.ps(es2, "psYE", [128, 512])
                    aT = tk.sb(es2, "aTE", [128, 512], BF16)
                    bT = tk.sb(es2, "bTE", [128, 512], BF16)
                    yT = [tk.sb(es2, f"yTE{i}", [128, 512], BF16) for i in range(2)]
                    tk.dma("sp", xf[:, :, :], f_d[tok0:tok0 + n, :].rearrange("(c p) d -> p c d", p=128), [f_d], [xf])
                    k = 0
                    for kb in range(n // kbw):
                        tk.dma("sp", CLp[:, :, :], CLt[:, kb * kbw:(kb + 1) * kbw].rearrange("(c p) k -> p c k", p=128), [], [CLp])
                        tk.dma("sp", SLp[:, :, :], SLt[:, kb * kbw:(kb + 1) * kbw].rearrange("(c p) k -> p c k", p=128), [], [SLp])
                        for g in range(4):
                            for c in range(nch):
                                mm(psA[:, 0:kbw], xf[:, c, g * 128:(g + 1) * 128], CLp[:, c, :], c == 0, c == nch - 1, [xf, CLp], [psA])
                            for c in range(nch):
                                mm(psB[:, 0:kbw], xf[:, c, g * 128:(g + 1) * 128], SLp[:, c, :], c == 0, c == nch - 1, [xf, SLp], [psB])
                            dve(lambda: V.tensor_copy(aT[:, 0:kbw], psA[:, 0:kbw]), [psA], [aT])
                            act(lambda: A.copy(bT[:, 0:kbw], psB[:, 0:kbw]), [psB], [bT])
                            mm(psY[:, 0:kbw], cc[:, 0:128], aT[:, 0:kbw], True, False, [cc, aT], [psY])
                            mm(psY[:, 0:kbw], cc[:, 128:256], bT[:, 0:kbw], False, True, [cc, bT], [psY])
                            y = yT[k % 2]
                            k += 1
                            evac(y[:, 0:kbw], psY[:, 0:kbw], [psY], [y])
                            tk.dma("sp", mixT_d[1024 + g * 128:1024 + (g + 1) * 128, tok0 + kb * kbw:tok0 + (kb + 1) * kbw],
                                   y[:, 0:kbw], [y], [mixT_d])
                tk.barrier()

    def stage_F(l):
        with ExitStack() as es:
            rt = tk.sb(es, "rtF", [128, 6 * 128 + 2], F32)
            lgb = tk.sb(es, "lgb", [128, 8], F32)
            tk.dma("sp", rt[:, :], t_ret[:, :], [], [rt])
            tk.dma("sp", lgb[:, :], rld[l, :].partition_broadcast(128), [rld], [lgb])
            maskT = tk.sb(es, "maskT", [128, 128], F32)
            qdr = tk.sb(es, "qdr", [128, 128], F32)
            kdc = tk.sb(es, "kdc", [128, 2], F32)
            rqT = tk.sb(es, "rqTF", [128, NT, 128], BF16)
            rkT = tk.sb(es, "rkTF", [128, NT, 128], BF16)
            rkt = tk.sb(es, "rktF", [128, NT, 128], BF16)
            rvt = tk.sb(es, "rvtF", [128, NT, 128], BF16)
            gt = tk.sb(es, "gtF", [128, NT, 128], F32)
            smA = tk.sb(es, "smAF", [128, NT, 128], BF16)
            Uall = tk.sb(es, "UallF", [128, NT, 128], F32)
            SbA = tk.sb(es, "SbAF", [128, NT, 128], BF16)
            obuf = tk.sb(es, "obufF", [128, NT, 128], F32)
            Sf = tk.sb(es, "SfF", [128, 128], F32)
            st = tk.sb(es, "stF", [128, 6, NT], F32)
            psg = [tk.ps(es, f"psgF{i}", [128, 4, 128]) for i in range(4)]
            kp = [0]
            groups = [list(range(g, min(g + 4, NT))) for g in range(0, NT, 4)]
            for di in range(2):
                for h in range(4):
                    li = di * 4 + h
                    act(lambda: A.activation(maskT[:, :], rt[:, (2 * di) * 128:(2 * di + 1) * 128], AF.Exp, scale=lgb[:, li:li + 1]),
                        [rt, lgb], [maskT])
                    dve(lambda: V.tensor_tensor(maskT[:, :], maskT[:, :], rt[:, (2 * di + 1) * 128:(2 * di + 2) * 128], ALU.mult),
                        [maskT, rt], [maskT])
                    act(lambda: A.activation(qdr[:, :], rt[:, (4 + di) * 128:(5 + di) * 128], AF.Exp, scale=lgb[:, li:li + 1]),
                        [rt, lgb], [qdr])
                    act(lambda: A.activation(kdc[:, 0:1], rt[:, 768 + di:769 + di], AF.Exp, scale=lgb[:, li:li + 1]), [rt, lgb], [kdc])
                    act(lambda: A.activation(kdc[:, 1:2], lgb[:, li:li + 1], AF.Exp, scale=128.0), [lgb], [kdc])
                    tk.dma("sp", rqT[:, :, :], rT_d[di * 1024 + h * 128:di * 1024 + (h + 1) * 128, :].rearrange("p (c n) -> p c n", n=128),
                           [rT_d], [rqT])
                    tk.dma("sp", rkT[:, :, :], rT_d[di * 1024 + 512 + h * 128:di * 1024 + 512 + (h + 1) * 128, :].rearrange("p (c n) -> p c n", n=128),
                           [rT_d], [rkT])
                    tk.dma("sp", rkt[:, :, :], rk_d[:, di * 512 + h * 128:di * 512 + (h + 1) * 128].rearrange("(c p) d -> p c d", p=128),
                           [rk_d], [rkt])
                    if di == 0 or True:
                        tk.dma("sp", rvt[:, :, :], rv_d[:, h * 128:(h + 1) * 128].rearrange("(c p) d -> p c d", p=128), [rv_d], [rvt])
                    tk.dma("sp", gt[:, :, :], g_d[:, di * 512 + h * 128:di * 512 + (h + 1) * 128].rearrange("(c p) d -> p c d", p=128),
                           [g_d], [gt])
                    dve(lambda: V.tensor_scalar(rkt[:, :, :], rkt[:, :, :], kdc[:, 0:1], None, ALU.mult), [rkt, kdc], [rkt])
                    for grp in groups:
                        n = len(grp)
                        c0 = grp[0]
                        ps = psg[kp[0] % 4]
                        kp[0] += 1
                        for j, c in enumerate(grp):
                            mm(ps[:, j, :], rkT[:, c, :], rqT[:, c, :], True, True, [rkT, rqT], [ps])
                        dve(lambda ps=ps, n=n, c0=c0: V.tensor_tensor(smA[:, c0:c0 + n, :], ps[:, 0:n, :],
                                                                     maskT[:, :].unsqueeze(1).to_broadcast([128, n, 128]), ALU.mult),
                            [ps, maskT], [smA])
                        ps = psg[kp[0] % 4]
                        kp[0] += 1
                        for j, c in enumerate(grp):
                            mm(ps[:, j, :], rkt[:, c, :], rvt[:, c, :], True, True, [rkt, rvt], [ps])
                        act(lambda ps=ps, n=n, c0=c0: A.copy(Uall[:, c0:c0 + n, :], ps[:, 0:n, :]), [ps], [Uall])
                    dve(lambda: V.tensor_tensor(rqT[:, :, :], rqT[:, :, :], qdr[:, :].unsqueeze(1).to_broadcast([128, NT, 128]), ALU.mult),
                        [rqT, qdr], [rqT])
                    order = list(range(NT)) if di == 0 else [1, 0] + list(range(NT - 1, 1, -1))
                    dve(lambda: V.memset(Sf[:, :], 0.0), [], [Sf])
                    dve(lambda: V.memset(SbA[:, order[0], :], 0.0), [], [SbA])
                    for k in range(NT - 1):
                        c = order[k]
                        cn = order[k + 1]
                        dve(lambda c=c: V.scalar_tensor_tensor(Sf[:, :], Sf[:, :], kdc[:, 1:2], Uall[:, c, :], ALU.mult, ALU.add),
                            [Sf, kdc, Uall], [Sf])
                        dve(lambda cn=cn: V.tensor_copy(SbA[:, cn, :], Sf[:, :]), [Sf], [SbA])
                    for grp in groups:
                        n = len(grp)
                        c0 = grp[0]
                        ps = psg[kp[0] % 4]
                        kp[0] += 1
                        for j, c in enumerate(grp):
                            mm(ps[:, j, :], smA[:, c, :], rvt[:, c, :], True, False, [smA, rvt], [ps])
                            mm(ps[:, j, :], rqT[:, c, :], SbA[:, c, :], False, True, [rqT, SbA], [ps])
                        evac(obuf[:, c0:c0 + n, :], ps[:, 0:n, :], [ps], [obuf])
                    dve(lambda: V.tensor_reduce(st[:, 0, :], obuf[:, :, :], AX.X, ALU.add), [obuf], [st])
                    dve(lambda: V.tensor_scalar(st[:, 1, :], st[:, 0, :], -1.0 / 128, None, ALU.mult), [st], [st])
                    dve(lambda: V.tensor_tensor(obuf[:, :, :], obuf[:, :, :], st[:, 1, :].unsqueeze(2).to_broadcast([128, NT, 128]), ALU.add),
                        [obuf, st], [obuf])
                    act(lambda: A.activation(Uall[:, :, :], obuf[:, :, :], AF.Square), [obuf], [Uall])
                    dve(lambda: V.tensor_reduce(st[:, 2, :], Uall[:, :, :], AX.X, ALU.add), [Uall], [st])
                    dve(lambda: V.tensor_scalar(st[:, 3, :], st[:, 2, :], 1.0 / 128, EPS, ALU.mult, ALU.add), [st], [st])
                    act(lambda: A.activation(st[:, 4, :], st[:, 3, :], AF.Sqrt), [st], [st])
                    dve(lambda: V.reciprocal(st[:, 5, :], st[:, 4, :]), [st], [st])
                    dve(lambda: V.tensor_tensor(obuf[:, :, :], obuf[:, :, :], st[:, 5, :].unsqueeze(2).to_broadcast([128, NT, 128]), ALU.mult),
                        [obuf, st], [obuf])
                    dve(lambda: V.tensor_tensor(obuf[:, :, :], obuf[:, :, :], gt[:, :, :], ALU.mult), [obuf, gt], [obuf])
                    tk.dma("sp", ro_d[:, di * 512 + h * 128:di * 512 + (h + 1) * 128].rearrange("(c p) e -> p c e", p=128), obuf[:, :, :],
                           [obuf], [ro_d])
        tk.barrier()

    def stage_G(l):
        with ExitStack() as es:
            ro = [tk.sb(es, f"roG{i}", [128, 1024], F32) for i in range(2)]
            rb = [tk.sb(es, f"rbG{i}", [128, 512], BF16) for i in range(2)]
            tr = [tk.sb(es, f"trG{i}", [128, 512], BF16) for i in range(2)]
            pst = [tk.ps(es, f"pstG{i}", [128, 512]) for i in range(2)]
            for t in range(NT):
                r_, b_, t_, ps = ro[t % 2], rb[t % 2], tr[t % 2], pst[t % 2]
                tk.dma("sp", r_[:, :], ro_d[t * 128:(t + 1) * 128, :], [ro_d], [r_])
                dve(lambda: V.tensor_tensor(b_[:, :], r_[:, 0:512], r_[:, 512:1024], ALU.add), [r_], [b_])
                for j in range(4):
                    mm(ps[:, j * 128:(j + 1) * 128], b_[:, j * 128:(j + 1) * 128], identb[:, :], True, True, [b_, identb], [ps])
                evac(t_[:, :], ps[:, :], [ps], [t_])
                tk.dma("sp", mixT_d[1536:2048, t * 128:(t + 1) * 128].rearrange("(j p) n -> p j n", p=128),
                       t_[:, :].rearrange("p (j n) -> p j n", j=4), [t_], [mixT_d])
        tk.barrier()

    def stage_H(l, affT):
        with ExitStack() as es:
            wo = tk.sb(es, "woH", [128, 16, D], BF16)
            gm = tk.sb(es, "gmH", [128, D], F32)
            GT1 = tk.sb(es, "GT1", [128, D], F32)
            G2_ = tk.sb(es, "G2", [128, D], F32)
            SH2_ = tk.sb(es, "SH2", [128, D], F32)
            GT = [GT1, GT1]
            G2 = [G2_, G2_]
            SH2 = [SH2_, SH2_]
            wr = tk.sb(es, "wrH", [128, 16, 16], F32)
            xts = [tk.sb(es, f"xtH{i}", [128, D], F32) for i in range(2)]
            mT = [tk.sb(es, f"mTH{i}", [128, 16, 128], BF16) for i in range(2)]
            tmps = [tk.sb(es, f"tmpH{i}", [128, D], F32) for i in range(2)]
            h2bs = [tk.sb(es, f"h2bH{i}", [128, D], BF16) for i in range(2)]
            h2T = tk.sb(es, "h2TH", [128, 16, 128], F32)
            st = tk.sb(es, "stH", [128, 8], F32)
            lg = tk.sb(es, "lgH", [128, 16], F32)
            psY = [tk.ps(es, f"psYH{i}", [128, 512]) for i in range(2)]
            psT = [tk.ps(es, f"psTH{i}", [128, 512]) for i in range(2)]
            psL = tk.ps(es, "psLH", [128, 16])
            psAT = tk.ps(es, "psATH", [16, 128])
            for nb in range(4):
                tk.dma("pool", wo[:, :, nb * 512:(nb + 1) * 512],
                       w_out[l * D:(l + 1) * D, nb * 512:(nb + 1) * 512].rearrange("(c p) n -> p c n", p=128), [w_out], [wo])
            tk.dma("sp", gm[:, :], norm_ffn[l, :].partition_broadcast(128), [norm_ffn], [gm])
            tk.dma("sp", wr[:, :, :], w_router[l * D:(l + 1) * D, :].rearrange("(c p) e -> p c e", p=128), [w_router], [wr])
            for t in range(NT):
                w = 1 if t < 2 else 0
                if t in (0, 2):
                    tk.dma("sp", GT[w][:, :], modrow(l, w, 2), [mod_d], [GT[w]])
                    tk.dma("sp", G2[w][:, :], modrow(l, w, 4), [mod_d], [G2[w]])
                    tk.dma("sp", SH2[w][:, :], modrow(l, w, 3), [mod_d], [SH2[w]])
                    dve(lambda w=w: V.tensor_tensor(G2[w][:, :], G2[w][:, :], gm[:, :], ALU.mult), [G2[w], gm], [G2[w]])
                m = mT[t % 2]
                xt, tmp, h2b = xts[t % 2], tmps[t % 2], h2bs[t % 2]
                junk = h2b
                h2f = tmp
                rows = slice(t * 128, (t + 1) * 128)
                tk.dma("sp", xt[:, :], xcur[rows, :], [xcur], [xt])
                tk.dma("sp", m[:, :, :], mixT_d[:, rows].rearrange("(c p) n -> p c n", p=128), [mixT_d], [m])
                for nb in range(4):
                    ps = psY[nb % 2]
                    cs = slice(nb * 512, (nb + 1) * 512)
                    for c in range(16):
                        mm(ps[:, :], m[:, c, :], wo[:, c, cs], c == 0, c == 15, [m, wo], [ps])
                    dve(lambda ps=ps, cs=cs, w=w: V.tensor_tensor(tmp[:, cs], ps[:, :], GT[w][:, cs], ALU.mult), [ps, GT[w]], [tmp])
                    dve(lambda cs=cs: V.tensor_tensor(xt[:, cs], xt[:, cs], tmp[:, cs], ALU.add), [xt, tmp], [xt])
                tk.dma("sp", xcur[rows, :], xt[:, :], [xt], [xcur])
                rms_rstd((junk, st), xt, D)
                dve(lambda w=w: V.scalar_tensor_tensor(tmp[:, :], xt[:, :], st[:, 3:4], G2[w][:, :], ALU.mult, ALU.mult), [xt, st, G2[w]], [tmp])
                dve(lambda w=w: V.tensor_tensor(h2f[:, :], tmp[:, :], SH2[w][:, :], ALU.add), [tmp, SH2[w]], [h2f])
                act(lambda: A.copy(h2b[:, :], h2f[:, :]), [h2f], [h2b])
                tk.dma("sp", h2_d[rows, :], h2b[:, :], [h2b], [h2_d])
                for g4 in range(4):
                    ps = psT[g4 % 2]
                    for j in range(4):
                        c = g4 * 4 + j
                        mm(ps[:, j * 128:(j + 1) * 128], h2f[:, c * 128:(c + 1) * 128], identf[:, :], True, True, [h2f, identf], [ps])
                    evac(h2T[:, g4 * 4:(g4 + 1) * 4, :], ps[:, :].rearrange("p (j n) -> p j n", j=4), [ps], [h2T])
                for c in range(16):
                    mm(psL[:, :], h2T[:, c, :], wr[:, c, :], c == 0, c == 15, [h2T, wr], [psL])
                dve(lambda: V.reduce_max(st[:, 4:5], psL[:, :], axis=AX.X), [psL], [st])
                dve(lambda: V.tensor_scalar(st[:, 5:6], st[:, 4:5], -1.0, None, ALU.mult), [st], [st])
                act(lambda: A.activation(lg[:, :], psL[:, :], AF.Exp, bias=st[:, 5:6], scale=1.0, accum_out=st[:, 6:7]), [psL, st], [lg, st])
                dve(lambda: V.reciprocal(st[:, 7:8], st[:, 6:7]), [st], [st])
                dve(lambda t=t: V.tensor_scalar(aff_all[:, t, :], lg[:, :], st[:, 7:8], None, ALU.mult), [lg, st], [aff_all])
                mm(psAT[:, :], aff_all[:, t, :], identf[:, :], True, True, [aff_all, identf], [psAT])
                evac(affT[:, rows], psAT[:, :], [psAT], [affT])
        tk.barrier()

    def stage_I(l, affT):
        with ExitStack() as es:
            lo = tk.sb(es, "loI", [16, 2], F32)
            hi = tk.sb(es, "hiI", [16, 2], F32)
            mid = tk.sb(es, "midI", [16, 2], F32)
            cnt = tk.sb(es, "cntI", [16, 2], F32)
            ge = tk.sb(es, "geI", [16, 2], F32)
            d1 = tk.sb(es, "d1I", [16, 2], F32)
            junk = tk.sb(es, "junkI", [16, S], F32)
            dg = tk.sb(es, "dgI", [16, 2, 16], F32)
            thrb = tk.sb(es, "thrbI", [128, 2, 16], F32)
            sel = tk.sb(es, "selI", [128, NT, 16], F32)
            selb = tk.sb(es, "selbI", [128, NT, 16], BF16)
            tmp = tk.sb(es, "tmpI", [128, 16], F32)
            psb = tk.ps(es, "psbI", [128, 16])
            pss = [tk.ps(es, f"pssI{i}", [128, 16]) for i in range(2)]
            pst = [tk.ps(es, f"pstI{i}", [16, 128]) for i in range(2)]
            sidxT = tk.sb(es, "sidxTI", [16, T], F32)
            gateT = tk.sb(es, "gateTI", [16, T], F32)
            dve(lambda: V.memset(lo[:, :], 0.0), [], [lo])
            dve(lambda: V.memset(hi[:, :], 1.0), [], [hi])
            sets = [(0, LC, CAP_C), (LC, S, CAP_L)]
            for it in range(34):
                dve(lambda: V.tensor_tensor(mid[:, :], lo[:, :], hi[:, :], ALU.add), [lo, hi], [mid])
                dve(lambda: V.tensor_scalar(mid[:, :], mid[:, :], 0.5, None, ALU.mult), [mid], [mid])
                for si, (c0, n, cap) in enumerate(sets):
                    dve(lambda si=si, c0=c0, n=n: V.tensor_scalar(junk[:, 0:n], affT[:, c0:c0 + n], mid[:, si:si + 1], None, ALU.is_ge),
                        [affT, mid], [junk])
                    dve(lambda si=si, n=n: V.reduce_sum(cnt[:, si:si + 1], junk[:, 0:n], axis=AX.X), [junk], [cnt])
                    dve(lambda si=si, cap=cap: V.tensor_scalar(ge[:, si:si + 1], cnt[:, si:si + 1], float(cap) - 0.5, None, ALU.is_ge), [cnt], [ge])
                dve(lambda: V.tensor_tensor(d1[:, :], mid[:, :], lo[:, :], ALU.subtract), [mid, lo], [d1])
                dve(lambda: V.tensor_tensor(d1[:, :], d1[:, :], ge[:, :], ALU.mult), [d1, ge], [d1])
                dve(lambda: V.tensor_tensor(lo[:, :], lo[:, :], d1[:, :], ALU.add), [lo, d1], [lo])
                dve(lambda: V.tensor_tensor(d1[:, :], hi[:, :], mid[:, :], ALU.subtract), [hi, mid], [d1])
                dve(lambda: V.tensor_tensor(d1[:, :], d1[:, :], ge[:, :], ALU.mult), [d1, ge], [d1])
                dve(lambda: V.tensor_tensor(hi[:, :], mid[:, :], d1[:, :], ALU.add), [mid, d1], [hi])
            for si in range(2):
                dve(lambda si=si: V.tensor_scalar(dg[:, si, :], identf[0:16, 0:16], lo[:, si:si + 1], None, ALU.mult), [identf, lo], [dg])
                mm(psb[:, :], onesf[0:16, :], dg[:, si, :], True, True, [onesf, dg], [psb])
                dve(lambda si=si: V.tensor_copy(thrb[:, si, :], psb[:, :]), [psb], [thrb])
            dve(lambda: V.tensor_tensor(sel[:, 0:2, :], aff_all[:, 0:2, :], thrb[:, 0, :].unsqueeze(1).to_broadcast([128, 2, 16]), ALU.is_ge),
                [aff_all, thrb], [sel])
            dve(lambda: V.tensor_tensor(sel[:, 2:NT, :], aff_all[:, 2:NT, :], thrb[:, 1, :].unsqueeze(1).to_broadcast([128, NT - 2, 16]), ALU.is_ge),
                [aff_all, thrb], [sel])
            dve(lambda: V.tensor_tensor(gate_all[:, :, :], aff_all[:, :, :], sel[:, :, :], ALU.mult), [aff_all, sel], [gate_all])
            dve(lambda: V.tensor_copy(selb[:, :, :], sel[:, :, :]), [sel], [selb])
            k = 0
            for (t0, t1, off) in ((0, 2, float(CAP_L)), (2, NT, 0.0)):
                for t in range(t0, t1):
                    ps = pss[k % 2]
                    k += 1
                    for c in range(t0, t):
                        mm(ps[:, :], onesb[:, :], selb[:, c, :], c == t0, False, [onesb, selb], [ps])
                    mm(ps[:, :], ustrb[:, :], selb[:, t, :], t == t0, True, [ustrb, selb], [ps])
                    dve(lambda ps=ps, off=off: V.tensor_scalar(tmp[:, :], ps[:, :], off + 1.0, None, ALU.add), [ps], [tmp])
                    dve(lambda t=t: V.tensor_tensor(tmp[:, :], tmp[:, :], sel[:, t, :], ALU.mult), [tmp, sel], [tmp])
                    dve(lambda t=t: V.tensor_scalar(sidx_all[:, t, :], tmp[:, :], -1.0, None, ALU.add), [tmp], [sidx_all])
            for t in range(NT):
                rows = slice(t * 128, (t + 1) * 128)
                mm(pst[0][:, :], sidx_all[:, t, :], identf[:, :], True, True, [sidx_all, identf], [pst[0]])
                dve(lambda rows=rows: V.tensor_copy(sidxT[:, rows], pst[0][:, :]), [pst[0]], [sidxT])
                mm(pst[1][:, :], gate_all[:, t, :], identf[:, :], True, True, [gate_all, identf], [pst[1]])
                act(lambda rows=rows: A.copy(gateT[:, rows], pst[1][:, :]), [pst[1]], [gateT])
            lo_ = tk.sb(es, "loTI", [16, T], F32)
            hib = tk.sb(es, "hibI", [16, T], BF16)
            lob = tk.sb(es, "lobI", [16, T], BF16)
            gtb = tk.sb(es, "gtbI", [16, T], BF16)
            dve(lambda: V.tensor_scalar(sidxT[:, :], sidxT[:, :], 1.0, None, ALU.add), [sidxT], [sidxT])
            cb = tk.sb(es, "cbI", [16, T], BF16)
            dve(lambda: V.tensor_scalar(hib[:, :], sidxT[:, :], 256.0, None, ALU.min), [sidxT], [hib])
            dve(lambda: V.tensor_scalar(lo_[:, :], sidxT[:, :], 256.0, None, ALU.min), [sidxT], [lo_])
            dve(lambda: V.tensor_tensor(sidxT[:, :], sidxT[:, :], lo_[:, :], ALU.subtract), [sidxT, lo_], [sidxT])
            dve(lambda: V.tensor_scalar(lob[:, :], sidxT[:, :], 256.0, None, ALU.min), [sidxT], [lob])
            dve(lambda: V.tensor_scalar(lo_[:, :], sidxT[:, :], 256.0, None, ALU.min), [sidxT], [lo_])
            dve(lambda: V.tensor_tensor(cb[:, :], sidxT[:, :], lo_[:, :], ALU.subtract), [sidxT, lo_], [cb])
            act(lambda: A.copy(gtb[:, :], gateT[:, :]), [gateT], [gtb])
            tk.dma("sp", shl_d[0:16, :], hib[:, :], [hib], [shl_d])
            tk.dma("sp", shl_d[16:32, :], lob[:, :], [lob], [shl_d])
            tk.dma("sp", shl_d[32:48, :], cb[:, :], [cb], [shl_d])
            tk.dma("sp", gtb_d[:, :], gtb[:, :], [gtb], [gtb_d])
        tk.barrier()

    def stage_J(l):
        with ExitStack() as es:
            h2b = tk.sb(es, "h2bJ", [128, NT, 512], BF16)
            P = [tk.sb(es, f"PJ{i}", [128, 512], BF16) for i in range(3)]
            psG = [tk.ps(es, f"psGJ{i}", [128, 512]) for i in range(4)]
            psC = tk.ps(es, "psCJ", [128, 4, 32])
            xg = [tk.sb(es, f"xgJ{i}", [128, 4, NSL], BF16) for i in range(2)]
            k = 0
            for db in range(4):
                tk.dma("sp", h2b[:, :, :], h2_d[:, db * 512:(db + 1) * 512].rearrange("(t p) d -> p t d", p=128), [h2_d], [h2b])
                for e in range(NE):
                    xo = xg[e % 2]
                    for t in range(NT):
                        p = P[k % 3]
                        k += 1
                        if t < 2:
                            dve(lambda p=p, t=t, e=e: V.tensor_scalar(p[:, 0:32], iota5[:, 0:32], float(CAP_L), sidx_all[:, t, e:e + 1],
                                                                      ALU.add, ALU.is_equal), [iota5, sidx_all], [p])
                            for j in range(4):
                                mm(psC[:, j, :], h2b[:, t, j * 128:(j + 1) * 128], p[:, 0:32], t == 0, t == 1, [h2b, p], [psC])
                        else:
                            dve(lambda p=p, t=t, e=e: V.tensor_scalar(p[:, :], iota5[:, :], sidx_all[:, t, e:e + 1], None, ALU.is_equal),
                                [iota5, sidx_all], [p])
                            for j in range(4):
                                mm(psG[j][:, :], h2b[:, t, j * 128:(j + 1) * 128], p[:, :], t == 2, t == NT - 1, [h2b, p], [psG[j]])
                    for j in range(4):
                        evac(xo[:, j, 0:CAP_L], psG[j][:, :], [psG[j]], [xo])
                    evac(xo[:, :, CAP_L:NSL], psC[:, :, :], [psC], [xo])
                    tk.dma("sp", xg_d[e * D + db * 512:e * D + (db + 1) * 512, :].rearrange("(j p) s -> p j s", p=128), xo[:, :, :], [xo], [xg_d])
        tk.barrier()

    def stage_K(l):
        with ExitStack() as es:
            wgs = [tk.sb(es, f"wgK{i}", [128, 16, FF], BF16) for i in range(2)]
            wus = [tk.sb(es, f"wuK{i}", [128, 16, FF], BF16) for i in range(2)]
            wd = tk.sb(es, "wdK", [128, 8, D], BF16)
            xg0 = tk.sb(es, "xgK", [128, 16, NSL], BF16)
            gT = tk.sb(es, "gTK", [128, 8, NSL], BF16)
            sa = [tk.sb(es, f"saK{i}", [128, NSL], F32) for i in range(2)]
            psA = [tk.ps(es, f"psAK{i}", [128, 512]) for i in range(2)]
            psU = [tk.ps(es, f"psUK{i}", [128, 512]) for i in range(2)]
            psA2 = tk.ps(es, "psA2K", [128, 2, 32])
            psY = [tk.ps(es, f"psYK{i}", [128, 512]) for i in range(2)]
            yo = [tk.sb(es, f"yoK{i}", [128, 512], BF16) for i in range(2)]
            ky = 0

            def load_gu(e):
                r0 = (l * NE + e) * D
                for hf in range(2):
                    fs = slice(hf * 512, (hf + 1) * 512)
                    tk.dma("pool", wgs[e % 2][:, :, fs], w_gate[r0:r0 + D, fs].rearrange("(c p) f -> p c f", p=128), [w_gate], [wgs[e % 2]])
                    tk.dma("pool", wus[e % 2][:, :, fs], w_up[r0:r0 + D, fs].rearrange("(c p) f -> p c f", p=128), [w_up], [wus[e % 2]])

            def load_d(e):
                r1 = (l * NE + e) * FF
                for nb in range(4):
                    cs = slice(nb * 512, (nb + 1) * 512)
                    tk.dma("pool", wd[:, :, cs], w_down[r1:r1 + FF, cs].rearrange("(c p) n -> p c n", p=128), [w_down], [wd])

            load_gu(0)
            load_d(0)
            for e in range(NE):
                wg, wu = wgs[e % 2], wus[e % 2]
                x_ = xg0
                tk.dma("sp", x_[:, :, :], xg_d[e * D:(e + 1) * D, :].rearrange("(c p) s -> p c s", p=128), [xg_d], [x_])
                if e + 1 < NE:
                    load_gu(e + 1)
                for fc in range(8):
                    pa, pu, s_ = psA[fc % 2], psU[fc % 2], sa[fc % 2]
                    fsl = slice(fc * 128, (fc + 1) * 128)
                    for c in range(16):
                        mm(pa[:, :], wg[:, c, fsl], x_[:, c, 0:CAP_L], c == 0, c == 15, [wg, x_], [pa])
                    for c in range(16):
                        mm(pu[:, :], wu[:, c, fsl], x_[:, c, 0:CAP_L], c == 0, c == 15, [wu, x_], [pu])
                    for c in range(16):
                        mm(psA2[:, 0, :], wg[:, c, fsl], x_[:, c, CAP_L:NSL], c == 0, c == 15, [wg, x_], [psA2])
                    for c in range(16):
                        mm(psA2[:, 1, :], wu[:, c, fsl], x_[:, c, CAP_L:NSL], c == 0, c == 15, [wu, x_], [psA2])
                    act(lambda: A.activation(s_[:, 0:CAP_L], pa[:, :], AF.Silu), [pa], [s_])
                    act(lambda: A.activation(s_[:, CAP_L:NSL], psA2[:, 0, :], AF.Silu), [psA2], [s_])
                    dve(lambda fc=fc: V.tensor_tensor(gT[:, fc, 0:CAP_L], s_[:, 0:CAP_L], pu[:, :], ALU.mult), [s_, pu], [gT])
                    dve(lambda fc=fc: V.tensor_tensor(gT[:, fc, CAP_L:NSL], s_[:, CAP_L:NSL], psA2[:, 1, :], ALU.mult), [s_, psA2], [gT])
                for stl in range(5):
                    s0 = stl * 128
                    m = 128 if stl < 4 else CAP_C
                    for nb in range(4):
                        ps, y_ = psY[ky % 2], yo[ky % 2]
                        ky += 1
                        for fc in range(8):
                            mm(ps[0:m, :], gT[:, fc, s0:s0 + m], wd[:, fc, nb * 512:(nb + 1) * 512], fc == 0, fc == 7, [gT, wd], [ps])
                        evac(y_[0:m, :], ps[0:m, :], [ps], [y_])
                        tk.dma("sp", y_d[e * NSL + s0:e * NSL + s0 + m, nb * 512:(nb + 1) * 512], y_[0:m, :], [y_], [y_d])
                if e + 1 < NE:
                    load_d(e + 1)
        tk.barrier()

    def stage_L(l, last):
        with ExitStack() as es:
            yb = tk.sb(es, "ybL", [128, NE, 5, 512], BF16)
            GT2 = [tk.sb(es, f"GT2{w}", [128, D], F32) for w in range(2)]
            psS = [tk.ps(es, f"psSL{i}", [128, 512]) for i in range(2)]
            psG = [tk.ps(es, f"psGL{i}", [128, 512]) for i in range(2)]
            psX = [tk.ps(es, f"psXL{i}", [128, 512]) for i in range(4)]
            eq = [tk.sb(es, f"eqL{i}", [128, 512], F32) for i in range(3)]
            ST = [tk.sb(es, f"STL{i}", [128, 512], BF16) for i in range(8)]
            xt = [tk.sb(es, f"xtL{i}", [128, 512], F32) for i in range(2)]
            tmp = [tk.sb(es, f"tmpL{i}", [128, 512], F32) for i in range(2)]
            eselHL = tk.sb(es, "eselHL", [48, NE, 128], BF16)
            eselG = tk.sb(es, "eselG", [16, NE, 128], BF16)
            shl = tk.sb(es, "shlL", [48, T], BF16)
            gtb = tk.sb(es, "gtbL", [16, T], BF16)
            iop1 = tk.sb(es, "iop1L", [128, 8], F32)
            tk.dma("pool", eselHL[:, :, :], t_esel[:, :].rearrange("k (e m) -> k e m", e=NE), [], [eselHL])
            dve(lambda: V.tensor_copy(eselG[:, :, :], identf[0:16, 0:16].unsqueeze(2).to_broadcast([16, 16, 128])), [identf], [eselG])
            dve(lambda: V.tensor_scalar(iop1[:, :], iotap[:, :], 1.0, None, ALU.add), [iotap], [iop1])
            tk.dma("sp", shl[:, :], shl_d[:, :], [shl_d], [shl])
            tk.dma("sp", gtb[:, :], gtb_d[:, :], [gtb_d], [gtb])
            for w in range(2):
                tk.dma("sp", GT2[w][:, :], modrow(l, w, 5), [mod_d], [GT2[w]])
            dve(lambda: V.memset(yb[:, :, :, :], 0.0), [], [yb])
            ks = 0
            kx = 0
            for db in range(4):
                dcs = slice(db * 512, (db + 1) * 512)
                for e in range(NE):
                    tk.dma("sp", yb[:, e, 0:4, :], y_d[e * NSL:e * NSL + CAP_L, dcs].rearrange("(c p) d -> p c d", p=128), [y_d], [yb])
                    tk.dma("sp", yb[0:CAP_C, e, 4, :], y_d[e * NSL + CAP_L:(e + 1) * NSL, dcs], [y_d], [yb])
                blocks = [(0, 256, True)] + [(256 + qb * 512, 512, False) for qb in range(8)]
                for (c0, ntok, isctx) in blocks:
                    if last and isctx:
                        continue
                    nsub = ntok // 128
                    chunks = [4] if isctx else [0, 1, 2, 3]
                    np_ = CAP_C if isctx else 128

                    def bc(e):
                        mm(psS[e % 2][:, 0:ntok], eselHL[:, e, :], shl[:, c0:c0 + ntok], True, True, [eselHL, shl], [psS[e % 2]])
                        mm(psG[e % 2][:, 0:ntok], eselG[:, e, :], gtb[:, c0:c0 + ntok], True, True, [eselG, gtb], [psG[e % 2]])
                    bc(0)
                    for e in range(NE):
                        if e + 1 < NE:
                            bc(e + 1)
                        pS, pG = psS[e % 2], psG[e % 2]
                        gq_ = eq[e % 3]
                        act(lambda: A.copy(gq_[0:np_, 0:ntok], pG[0:np_, 0:ntok]), [pG], [gq_])
                        sts = []
                        for ci, c in enumerate(chunks):
                            s_ = ST[ks % 8]
                            ks += 1
                            dve(lambda s_=s_, c=c: V.scalar_tensor_tensor(
                                s_[0:np_, 0:ntok], pS[0:np_, 0:ntok], iop1[0:np_, c:c + 1], gq_[0:np_, 0:ntok], ALU.is_equal, ALU.mult),
                                [pS, iop1, gq_], [s_])
                            sts.append(s_)
                        for ci, c in enumerate(chunks):
                            s_ = sts[ci]
                            first = (e == 0 and ci == 0)
                            lastm = (e == NE - 1 and ci == len(chunks) - 1)
                            for m in range(nsub):
                                mm(psX[m][:, :], s_[0:np_, m * 128:(m + 1) * 128], yb[0:np_, e, c, :], first, lastm, [s_, yb], [psX[m]])
                    w = 1 if isctx else 0
                    for m in range(nsub):
                        x_, t_ = xt[kx % 2], tmp[kx % 2]
                        kx += 1
                        rows = slice(c0 + m * 128, c0 + (m + 1) * 128)
                        tk.dma("sp", x_[:, :], xcur[rows, dcs], [xcur], [x_])
                        dve(lambda m=m, t_=t_, w=w: V.tensor_tensor(t_[:, :], psX[m][:, :], GT2[w][:, dcs], ALU.mult), [psX[m], GT2[w]], [t_])
                        dve(lambda x_=x_, t_=t_: V.tensor_tensor(x_[:, :], x_[:, :], t_[:, :], ALU.add), [x_, t_], [x_])
                        if last:
                            tk.dma("sp", out_d[c0 - LC + m * 128:c0 - LC + (m + 1) * 128, dcs], x_[:, :], [x_], [out_d])
                        else:
                            tk.dma("sp", xcur[rows, dcs], x_[:, :], [x_], [xcur])
        tk.barrier()

    tk.marks = []

    def mark(name):
        tk.marks.append((name, dict(tk.ccnt)))
    stage_adaln()
    mark("ada")
    for l in range(depth):
        stage_A(l)
        mark(f"A{l}")
        stage_B(l)
        mark(f"B{l}")
        stage_C(l)
        mark(f"C{l}")
        stage_D(l)
        mark(f"D{l}")
        stage_E(l)
        mark(f"E{l}")
        stage_F(l)
        mark(f"F{l}")
        stage_G(l)
        mark(f"G{l}")
        with ExitStack() as esHI:
            affT = tk.sb(esHI, "affT", [16, T], F32)
            stage_H(l, affT)
            mark(f"H{l}")
            stage_I(l, affT)
            mark(f"I{l}")
        stage_J(l)
        mark(f"J{l}")
        stage_K(l)
        mark(f"K{l}")
        stage_L(l, l == depth - 1)
        mark(f"L{l}")
    tk.barrier()
    top.close()
    return nc, tk


def _tables():
    f32 = np.float32
    bf = ml_dtypes.bfloat16
    tb = {}
    tb["t_ident"] = np.eye(128, dtype=f32)
    tb["t_ustr"] = np.triu(np.ones((128, 128), f32), 1)
    es_ = np.zeros((48, 16, 128), f32)
    for e in range(16):
        es_[e, e, :] = 1.0
        es_[16 + e, e, :] = 1.0
        es_[32 + e, e, :] = 1.0
    tb["t_esel"] = es_.reshape(48, 16 * 128)

    def rope_tab(pos, half):
        freqs = (f32(10000.0) ** (-np.arange(half, dtype=f32) / f32(half))).astype(f32)
        ang = (pos.astype(f32)[:, None] * freqs[None, :]).astype(f32)
        return np.cos(ang).astype(f32), np.sin(ang).astype(f32)

    rows = np.repeat(np.arange(S // 64), 64)
    cols = np.tile(np.arange(64), S // 64)
    cr, sr = rope_tab(rows, 32)
    cc_, sc_ = rope_tab(cols, 32)
    cosA = np.concatenate([cr, cr, cc_, cc_], axis=1)
    sinA = np.concatenate([-sr, sr, -sc_, sc_], axis=1)
    A_ = np.zeros((T, 256), f32)
    A_[:LC, :128] = 1.0
    A_[LC:, :128] = cosA
    A_[LC:, 128:] = sinA
    tb["t_ropeA"] = A_
    posF = np.arange(T)
    posB = np.concatenate([LC - 1 - np.arange(LC), LC + (S - 1 - np.arange(S))])
    for nm, pos in (("t_ropeF", posF), ("t_ropeB", posB)):
        c_, s_ = rope_tab(pos, 64)
        tb[nm] = np.concatenate([c_, c_, -s_, s_], axis=1).astype(f32)

    def dft(n):
        idx = np.arange(n, dtype=np.int64)
        m = (idx[:, None] * idx[None, :]) % n
        ang = 2.0 * np.pi * m.astype(np.float64) / n
        return np.cos(ang), np.sin(ang)
    c, s = dft(S)
    tb["t_CL"] = (c / np.sqrt(S)).astype(bf)
    tb["t_SL"] = (s / np.sqrt(S)).astype(bf)
    c, s = dft(LC)
    tb["t_CL2"] = (c / np.sqrt(LC)).astype(bf)
    tb["t_SL2"] = (s / np.sqrt(LC)).astype(bf)
    c, s = dft(128)
    tb["t_CC"] = np.concatenate([c / np.sqrt(128.0), -s / np.sqrt(128.0)], axis=1).astype(bf)
    j = np.arange(128, dtype=f32)[:, None]
    i = np.arange(128, dtype=f32)[None, :]
    Df = np.maximum(i - j, 0.0)
    Mf = (i >= j).astype(f32)
    Db = np.maximum(j - i, 0.0)
    Mb = (j >= i).astype(f32)
    rowf = np.broadcast_to(i + 1.0, (128, 128))
    rowb = np.broadcast_to(128.0 - i, (128, 128))
    colf = 127.0 - j
    colb = j + 0.0
    tb["t_ret"] = np.ascontiguousarray(np.concatenate([Df, Mf, Db, Mb, rowf, rowb, colf, colb], axis=1).astype(f32))
    return tb


_CACHE = {}


def kernel(x, c, ctx, c_ctx, w_ada, b_ada, norm_mix, norm_ffn, w_in, q_norm, k_norm,
           ret_log_decay, w_out, w_router, w_gate, w_up, w_down):
    depth = int(os.environ.get("MK_DEPTH", w_ada.shape[0]))
    ncores = int(os.environ.get("MK_CORES", x.shape[0]))
    f = np.ascontiguousarray
    tb = _tables()
    shared = {
        "w_ada": f(w_ada[:depth].reshape(depth * D, 6 * D)),
        "b_ada": f(b_ada[:depth]),
        "norm_mix": f(norm_mix[:depth]),
        "norm_ffn": f(norm_ffn[:depth]),
        "w_in": f(w_in[:depth].reshape(depth * D, INW)),
        "q_norm": f(q_norm[:depth]),
        "k_norm": f(k_norm[:depth]),
        "rld": f(ret_log_decay[:depth].reshape(depth, 8)),
        "w_out": f(w_out[:depth].reshape(depth * D, D)),
        "w_router": f(w_router[:depth].reshape(depth * D, NE)),
        "w_gate": f(w_gate[:depth].reshape(depth * NE * D, FF)),
        "w_up": f(w_up[:depth].reshape(depth * NE * D, FF)),
        "w_down": f(w_down[:depth].reshape(depth * NE * FF, D)),
    }
    shared.update(tb)
    in_maps = []
    for b in range(ncores):
        m = dict(shared)
        m["x"] = f(x[b])
        m["ctx"] = f(ctx[b])
        m["cvec"] = f(np.stack([c[b], c_ctx], axis=0))
        in_maps.append(m)
    nc, tk = build(depth)
    res = run_bass_kernel_spmd(nc, in_maps, core_ids=list(range(ncores)))
    out = np.stack([np.asarray(res.results[b]["out"]) for b in range(ncores)], axis=0)
    return out.astype(np.float32)
```

```python
import os
import numpy as np
import ml_dtypes
from contextlib import ExitStack
import concourse.bass as bass
import concourse.mybir as mybir
from concourse.bass_utils import run_bass_kernel_spmd

F32 = mybir.dt.float32
BF16 = mybir.dt.bfloat16
ALU = mybir.AluOpType
AF = mybir.ActivationFunctionType
AX = mybir.AxisListType

D = 2048
S = 4096
LC = 256
T = S + LC
NT = T // 128
NE = 16
FF = 1024
INW = 4608
CAP_L = 512
CAP_C = 32
NSL = CAP_L + CAP_C
EPS = 1e-6


class Buf:
    __slots__ = ("t", "w", "r", "name")

    def __init__(self, t, name=""):
        self.t = t
        self.w = None
        self.r = {}
        self.name = name

    def __getitem__(self, idx):
        return self.t[idx]


class TK:
    NDMA = 8

    def __init__(self, nc, es):
        self.nc = nc
        self.es = es
        self.eng = {"pe": nc.tensor, "act": nc.scalar, "dve": nc.vector, "pool": nc.gpsimd, "sp": nc.sync}
        self.csem = {k: es.enter_context(nc.semaphore("s_" + k)) for k in ("pe", "act", "dve", "pool")}
        self.ccnt = {k: 0 for k in self.csem}
        self.dsem = {q: [es.enter_context(nc.semaphore(f"d_{q}{i}")) for i in range(self.NDMA)] for q in ("sp", "pool")}
        self.dcnt = {q: [0] * self.NDMA for q in self.dsem}
        self.dnext = {q: 0 for q in self.dsem}
        self.waited = {}
        self.n_inst = 0
        self.uid = 0

    def sb(self, es, name, shape, dt):
        self.uid += 1
        return Buf(es.enter_context(self.nc.sbuf_tensor(f"{name}_{self.uid}", shape, dt)), name)

    def ps(self, es, name, shape, dt=F32):
        self.uid += 1
        return Buf(es.enter_context(self.nc.psum_tensor(f"{name}_{self.uid}", shape, dt)), name)

    def dram(self, name, shape, dt, kind="Internal"):
        return Buf(self.nc.dram_tensor(name, shape, dt, kind=kind), name)

    def _wait(self, e, tok):
        sem, val = tok[0], tok[1]
        key = (e, id(sem))
        if self.waited.get(key, 0) >= val:
            return
        self.waited[key] = val
        self.eng[e].wait_ge(sem, val)

    def _deps(self, e, reads, writes, skip_same):
        for b in list(reads) + list(writes):
            if b.w is not None and not (skip_same and b.w[2] == e):
                self._wait(e, b.w)
        for b in writes:
            for tok in b.r.values():
                if not (skip_same and tok[2] == e):
                    self._wait(e, tok)

    def _commit(self, tok, reads, writes):
        for b in writes:
            b.w = tok
            b.r = {}
        for b in reads:
            if b in writes:
                continue
            b.r[id(tok[0])] = tok

    def op(self, e, fn, reads=(), writes=(), skip_same=False):
        self._deps(e, reads, writes, skip_same)
        ins = fn()
        self.ccnt[e] += 1
        ins.then_inc(self.csem[e], 1)
        tok = (self.csem[e], self.ccnt[e], e)
        self._commit(tok, reads, writes)
        self.n_inst += 1
        return tok

    def dma(self, q, out, in_, reads=(), writes=(), **kw):
        i = self.dnext[q]
        self.dnext[q] = (i + 1) % self.NDMA
        sem = self.dsem[q][i]
        if self.dcnt[q][i] > 0:
            self._wait(q, (sem, 16 * self.dcnt[q][i]))
        self._deps(q, reads, writes, False)
        ins = self.eng[q].dma_start(out=out, in_=in_, **kw)
        self.dcnt[q][i] += 1
        ins.then_inc(sem, 16)
        tok = (sem, 16 * self.dcnt[q][i], "dma_" + q)
        self._commit(tok, reads, writes)
        self.n_inst += 1
        return tok

    def barrier(self):
        for e in ("pe", "act", "dve", "pool", "sp"):
            for k in self.csem:
                if self.ccnt[k] > 0:
                    self._wait(e, (self.csem[k], self.ccnt[k]))
            for q in self.dsem:
                for i in range(self.NDMA):
                    if self.dcnt[q][i] > 0:
                        self._wait(e, (self.dsem[q][i], 16 * self.dcnt[q][i]))


def build(depth):
    nc = bass.Bass("TRN2", target_bir_lowering=False)
    top = ExitStack()
    tk = TK(nc, top)
    V, A, PE = nc.vector, nc.scalar, nc.tensor

    def dve(fn, r, w):
        return tk.op("dve", fn, r, w)

    def act(fn, r, w):
        return tk.op("act", fn, r, w)

    def mm(ps_ap, lhsT, rhs, start, stop, r, w):
        return tk.op("pe", lambda: PE.matmul(ps_ap, lhsT, rhs, start=start, stop=stop), r, w, skip_same=True)

    flip = [0]

    def evac(out_ap, in_ap, r, w):
        flip[0] ^= 1
        if flip[0]:
            return dve(lambda: V.tensor_copy(out_ap, in_ap), r, w)
        return act(lambda: A.copy(out_ap, in_ap), r, w)

    EI = "ExternalInput"
    x_in = tk.dram("x", [S, D], F32, EI)
    ctx_in = tk.dram("ctx", [LC, D], F32, EI)
    cvec = tk.dram("cvec", [2, D], F32, EI)
    w_ada = tk.dram("w_ada", [depth * D, 6 * D], F32, EI)
    b_ada = tk.dram("b_ada", [depth, 6 * D], F32, EI)
    norm_mix = tk.dram("norm_mix", [depth, D], F32, EI)
    norm_ffn = tk.dram("norm_ffn", [depth, D], F32, EI)
    w_in = tk.dram("w_in", [depth * D, INW], F32, EI)
    q_norm = tk.dram("q_norm", [depth, 128], F32, EI)
    k_norm = tk.dram("k_norm", [depth, 128], F32, EI)
    rld = tk.dram("rld", [depth, 8], F32, EI)
    w_out = tk.dram("w_out", [depth * D, D], F32, EI)
    w_router = tk.dram("w_router", [depth * D, NE], F32, EI)
    w_gate = tk.dram("w_gate", [depth * NE * D, FF], F32, EI)
    w_up = tk.dram("w_up", [depth * NE * D, FF], F32, EI)
    w_down = tk.dram("w_down", [depth * NE * FF, D], F32, EI)
    t_ident = tk.dram("t_ident", [128, 128], F32, EI)
    t_ropeA = tk.dram("t_ropeA", [T, 256], F32, EI)
    t_ropeF = tk.dram("t_ropeF", [T, 256], F32, EI)
    t_ropeB = tk.dram("t_ropeB", [T, 256], F32, EI)
    t_CL = tk.dram("t_CL", [S, S], BF16, EI)
    t_SL = tk.dram("t_SL", [S, S], BF16, EI)
    t_CL2 = tk.dram("t_CL2", [LC, LC], BF16, EI)
    t_SL2 = tk.dram("t_SL2", [LC, LC], BF16, EI)
    t_CC = tk.dram("t_CC", [128, 256], BF16, EI)
    t_ret = tk.dram("t_ret", [128, 6 * 128 + 2], F32, EI)
    t_ustr = tk.dram("t_ustr", [128, 128], F32, EI)
    out_d = tk.dram("out", [S, D], F32, "ExternalOutput")

    xcur = tk.dram("xcur", [T, D], F32)
    mod_d = tk.dram("mod_d", [depth * 12 * 128, D], F32)
    hT_d = tk.dram("hT_d", [T, D], BF16)
    p_d = tk.dram("p_d", [T, INW], F32)
    qT_d = tk.dram("qT_d", [1024, T], BF16)
    kT_d = tk.dram("kT_d", [256, T], BF16)
    v_d = tk.dram("v_d", [T, 256], BF16)
    f_d = tk.dram("f_d", [T, 512], BF16)
    rT_d = tk.dram("rT_d", [4 * 512, T], BF16)
    rk_d = tk.dram("rk_d", [T, 1024], BF16)
    rv_d = tk.dram("rv_d", [T, 512], BF16)
    g_d = tk.dram("g_d", [T, 1024], F32)
    ro_d = tk.dram("ro_d", [T, 1024], F32)
    mixT_d = tk.dram("mixT_d", [D, T], BF16)
    h2_d = tk.dram("h2_d", [T, D], BF16)
    xg_d = tk.dram("xg_d", [NE * D, NSL], BF16)
    y_d = tk.dram("y_d", [NE * NSL, D], BF16)

    identf = tk.sb(top, "identf", [128, 128], F32)
    identb = tk.sb(top, "identb", [128, 128], BF16)
    onesf = tk.sb(top, "onesf", [128, 128], F32)
    onesb = tk.sb(top, "onesb", [128, 128], BF16)
    ustrb = tk.sb(top, "ustrb", [128, 128], BF16)
    iota5 = tk.sb(top, "iota5", [128, 512], F32)
    iotap = tk.sb(top, "iotap", [128, 8], F32)
    tk.dma("sp", identf[:, :], t_ident[:, :], [t_ident], [identf])
    tk.dma("pool", identb[:, :], t_ident[:, :], [t_ident], [identb])
    tk.dma("pool", ustrb[:, :], t_ustr[:, :], [t_ustr], [ustrb])
    tk.op("pool", lambda: nc.gpsimd.memset(onesf[:, :], 1.0), [], [onesf])
    tk.op("pool", lambda: nc.gpsimd.memset(onesb[:, :], 1.0), [], [onesb])
    tk.op("pool", lambda: nc.gpsimd.iota(iota5[:, :], [[1, 512]], base=0, channel_multiplier=0,
                                          allow_small_or_imprecise_dtypes=True), [], [iota5])
    tk.op("pool", lambda: nc.gpsimd.iota(iotap[:, :], [[128, 8]], base=0, channel_multiplier=1,
                                          allow_small_or_imprecise_dtypes=True), [], [iotap])
    aff_all = tk.sb(top, "aff_all", [128, NT, 16], F32)
    sidx_all = tk.sb(top, "sidx_all", [128, NT, 16], F32)
    gate_all = tk.sb(top, "gate_all", [128, NT, 16], F32)
    shl_d = tk.dram("shl_d", [48, T], BF16)
    gtb_d = tk.dram("gtb_d", [16, T], BF16)
    t_esel = tk.dram("t_esel", [48, 16 * 128], F32, "ExternalInput")

    def modrow(l, w, i):
        r0 = ((l * 12) + w * 6 + i) * 128
        return mod_d[r0:r0 + 128, :]

    def stage_adaln():
        with ExitStack() as es:
            condT = tk.sb(es, "condT", [128, 2, 16], F32)
            scb = [tk.sb(es, f"scb{w}", [128, 16, 128], F32) for w in range(2)]
            wts = [tk.sb(es, f"wada{i}", [128, 16, 512], F32) for i in range(2)]
            brow = [tk.sb(es, f"brow{i}", [1, 512], F32) for i in range(2)]
            pss = [tk.ps(es, f"psada{i}", [128, 512]) for i in range(2)]
            ot = [tk.sb(es, f"oada{i}", [128, 512], F32) for i in range(2)]
            tk.dma("sp", condT[:, :, :], cvec[:, :].rearrange("w (c p) -> p w c", p=128), [cvec], [condT],
                   allow_slow_non_contiguous=True)
            act(lambda: A.activation(condT[:, :, :], condT[:, :, :], AF.Silu), [condT], [condT])
            for w in range(2):
                for c in range(16):
                    dve(lambda w=w, c=c: V.tensor_scalar(scb[w][:, c, :], onesf[:, :], condT[:, w, c:c + 1], None, ALU.mult),
                        [onesf, condT], [scb[w]])
            k = 0
            for l in range(depth):
                for nb in range(24):
                    wt = wts[nb % 2]
                    br = brow[nb % 2]
                    tk.dma("sp", wt[:, :, :], w_ada[l * D:(l + 1) * D, nb * 512:(nb + 1) * 512].rearrange("(c p) n -> p c n", p=128),
                           [w_ada], [wt])
                    tk.dma("sp", br[:, :], b_ada[l:l + 1, nb * 512:(nb + 1) * 512], [b_ada], [br])
                    piece = nb // 4
                    for w in range(2):
                        ps = pss[k % 2]
                        o = ot[k % 2]
                        k += 1
                        for c in range(16):
                            mm(ps[:, :], scb[w][:, c, :], wt[:, c, :], c == 0, False, [scb[w], wt], [ps])
                        mm(ps[:, :], onesf[0:1, :], br[0:1, :], False, True, [onesf, br], [ps])
                        if piece in (1, 4):
                            dve(lambda ps=ps, o=o: V.tensor_scalar(o[:, :], ps[:, :], 1.0, None, ALU.add), [ps], [o])
                        else:
                            evac(o[:, :], ps[:, :], [ps], [o])
                        tk.dma("sp", modrow(l, w, piece)[:, (nb % 4) * 512:(nb % 4 + 1) * 512], o[:, :], [o], [mod_d])
        tk.barrier()

    def rms_rstd(es_bufs, xt, width):
        junk, st = es_bufs
        act(lambda: A.activation(junk[:, 0:width], xt[:, 0:width], AF.Square, accum_out=st[:, 0:1]), [xt], [junk, st])
        dve(lambda: V.tensor_scalar(st[:, 1:2], st[:, 0:1], 1.0 / width, EPS, ALU.mult, ALU.add), [st], [st])
        act(lambda: A.activation(st[:, 2:3], st[:, 1:2], AF.Sqrt), [st], [st])
        dve(lambda: V.reciprocal(st[:, 3:4], st[:, 2:3]), [st], [st])
        return st

    def stage_A(l):
        with ExitStack() as es:
            gm = tk.sb(es, "gm", [128, D], F32)
            G = [tk.sb(es, f"G{w}", [128, D], F32) for w in range(2)]
            SH = [tk.sb(es, f"SH{w}", [128, D], F32) for w in range(2)]
            xts = [tk.sb(es, f"xt{i}", [128, D], F32) for i in range(2)]
            junk = tk.sb(es, "junk", [128, D], BF16)
            tmp = tk.sb(es, "tmpA", [128, D], F32)
            hb = [tk.sb(es, f"hb{i}", [128, D], BF16) for i in range(2)]
            hTt = [tk.sb(es, f"hTt{i}", [128, 16, 128], BF16) for i in range(2)]
            sts = [tk.sb(es, f"stA{i}", [128, 4], F32) for i in range(2)]
            pst = [tk.ps(es, f"pstA{i}", [128, 512]) for i in range(4)]
            tk.dma("sp", gm[:, :], norm_mix[l, :].partition_broadcast(128), [norm_mix], [gm])
            for w in range(2):
                tk.dma("sp", G[w][:, :], modrow(l, w, 1), [mod_d], [G[w]])
                tk.dma("sp", SH[w][:, :], modrow(l, w, 0), [mod_d], [SH[w]])
                dve(lambda w=w: V.tensor_tensor(G[w][:, :], G[w][:, :], gm[:, :], ALU.mult), [G[w], gm], [G[w]])
            for t in range(NT):
                w = 1 if t < 2 else 0
                xt = xts[t % 2]
                if l == 0:
                    src = ctx_in[t * 128:(t + 1) * 128, :] if t < 2 else x_in[(t - 2) * 128:(t - 1) * 128, :]
                    tk.dma("sp", xt[:, :], src, [], [xt])
                    tk.dma("sp", xcur[t * 128:(t + 1) * 128, :], xt[:, :], [xt], [xcur])
                else:
                    tk.dma("sp", xt[:, :], xcur[t * 128:(t + 1) * 128, :], [xcur], [xt])
                st = rms_rstd((junk, sts[t % 2]), xt, D)
                h = hb[t % 2]
                dve(lambda xt=xt, st=st, w=w: V.scalar_tensor_tensor(tmp[:, :], xt[:, :], st[:, 3:4], G[w][:, :], ALU.mult, ALU.mult),
                    [xt, st, G[w]], [tmp])
                dve(lambda h=h, w=w: V.tensor_tensor(h[:, :], tmp[:, :], SH[w][:, :], ALU.add), [tmp, SH[w]], [h])
                ht = hTt[t % 2]
                for g4 in range(4):
                    ps = pst[g4]
                    for j in range(4):
                        c = g4 * 4 + j
                        mm(ps[:, j * 128:(j + 1) * 128], h[:, c * 128:(c + 1) * 128], identb[:, :], True, True, [h, identb], [ps])
                    evac(ht[:, g4 * 4:(g4 + 1) * 4, :], ps[:, :].rearrange("p (j n) -> p j n", j=4), [ps], [ht])
                tk.dma("sp", hT_d[t * 128:(t + 1) * 128, :], ht[:, :, :].rearrange("p c n -> p (c n)"), [ht], [hT_d])
        tk.barrier()

    def stage_B(l):
        with ExitStack() as es:
            hTall = tk.sb(es, "hTall", [128, NT, D], BF16)
            hts = [Buf(hTall.t, f"hTall{t}") for t in range(NT)]
            wg = [tk.sb(es, f"wgB{i}", [128, 16, 512], BF16) for i in range(2)]
            pss = [tk.ps(es, f"psB{i}", [128, 512]) for i in range(4)]
            ot = [tk.sb(es, f"oB{i}", [128, 512], F32) for i in range(4)]
            for t in range(NT):
                tk.dma("sp", hTall[:, t, :], hT_d[t * 128:(t + 1) * 128, :], [hT_d], [hts[t]])
            k = 0
            for g in range(9):
                w = wg[g % 2]
                tk.dma("pool", w[:, :, :], w_in[l * D:(l + 1) * D, g * 512:(g + 1) * 512].rearrange("(c p) n -> p c n", p=128),
                       [w_in], [w])
                for t in range(NT):
                    ps = pss[k % 4]
                    o = ot[k % 4]
                    k += 1
                    for c in range(16):
                        mm(ps[:, :], hTall[:, t, c * 128:(c + 1) * 128], w[:, c, :], c == 0, c == 15, [hts[t], w], [ps])
                    evac(o[:, :], ps[:, :], [ps], [o])
                    tk.dma("sp", p_d[t * 128:(t + 1) * 128, g * 512:(g + 1) * 512], o[:, :], [o], [p_d])
        tk.barrier()

    def stage_C(l):
        with ExitStack() as es:
            gqk = tk.sb(es, "gqk", [128, 10, 128], F32)
            pts = [tk.sb(es, f"ptC{i}", [128, INW], F32) for i in range(2)]
            ropes = [tk.sb(es, f"ropeC{i}", [128, 3, 256], F32) for i in range(2)]
            junks = [tk.sb(es, f"junkC{i}", [128, 1280], F32) for i in range(2)]
            sts = [tk.sb(es, f"stC{i}", [128, 40], F32) for i in range(2)]
            qn = tk.sb(es, "qn", [128, 10, 128], F32)
            t1 = tk.sb(es, "t1C", [128, 10, 128], F32)
            t2 = tk.sb(es, "t2C", [128, 10, 128], F32)
            qrs = [tk.sb(es, f"qr{i}", [128, 1280], BF16) for i in range(2)]
            rrs = [tk.sb(es, f"rr{i}", [128, 8, 128], F32) for i in range(2)]
            rfbs = [[tk.sb(es, f"rfb{i}_{j}", [128, 1024], BF16) for i in range(2)] for j in range(2)]
            gss = [tk.sb(es, f"gsC{i}", [128, 1024], F32) for i in range(2)]
            trs = [tk.sb(es, f"trC{i}", [128, 512], BF16) for i in range(2)]
            pst = [tk.ps(es, f"pstC{i}", [128, 512]) for i in range(2)]
            for h in range(10):
                src = q_norm if h < 8 else k_norm
                tk.dma("sp", gqk[:, h, :], src[l, :].partition_broadcast(128), [src], [gqk])
            kk = [0]

            def transpose_out(srcbuf, col0, nblk, dst, row0, t):
                ps = pst[kk[0] % 2]
                tr = trs[kk[0] % 2]
                kk[0] += 1
                for j in range(nblk):
                    mm(ps[:, j * 128:(j + 1) * 128], srcbuf[:, col0 + j * 128:col0 + (j + 1) * 128], identb[:, :], True, True,
                       [srcbuf, identb], [ps])
                evac(tr[:, 0:nblk * 128], ps[:, 0:nblk * 128], [ps], [tr])
                tk.dma("sp", dst[row0:row0 + nblk * 128, t * 128:(t + 1) * 128].rearrange("(j p) n -> p j n", p=128),
                       tr[:, 0:nblk * 128].rearrange("p (j n) -> p j n", j=nblk), [tr], [dst])

            def rope(src3, nh, half, cs, out3, rb, wb):
                nb = 128 // (2 * half)
                cosb = cs[:, 0:128].unsqueeze(1).to_broadcast([128, nh, 128])
                dve(lambda: V.tensor_tensor(t1[:, 0:nh, :], src3, cosb, ALU.mult), rb, [t1])
                sv = src3.rearrange("p h (b s w) -> p h b s w", b=nb, s=2)
                tv = t2[:, 0:nh, :].rearrange("p h (b s w) -> p h b s w", b=nb, s=2)
                sn = cs[:, 128:256].rearrange("p (b s w) -> p b s w", b=nb, s=2)
                for b in range(nb):
                    for s_ in range(2):
                        sb_ = sn[:, b, s_, :].unsqueeze(1).to_broadcast([128, nh, half])
                        dve(lambda b=b, s_=s_, sb_=sb_: V.tensor_tensor(tv[:, :, b, s_, :], sv[:, :, b, 1 - s_, :], sb_, ALU.mult),
                            rb, [t2])
                dve(lambda: V.tensor_tensor(out3, t1[:, 0:nh, :], t2[:, 0:nh, :], ALU.add), [t1, t2], [wb])

            for t in range(NT):
                pt = pts[t % 2]
                rp = ropes[t % 2]
                junk, st, qr, rr, rfb, gs = junks[t % 2], sts[t % 2], qrs[t % 2], rrs[t % 2], rfbs[t % 2], gss[t % 2]
                tk.dma("sp", pt[:, :], p_d[t * 128:(t + 1) * 128, :], [p_d], [pt])
                tk.dma("sp", rp[:, 0, :], t_ropeA[t * 128:(t + 1) * 128, :], [], [rp])
                tk.dma("sp", rp[:, 1, :], t_ropeF[t * 128:(t + 1) * 128, :], [], [rp])
                tk.dma("sp", rp[:, 2, :], t_ropeB[t * 128:(t + 1) * 128, :], [], [rp])
                qk3 = pt[:, 0:1280].rearrange("p (h d) -> p h d", h=10)
                act(lambda: A.activation(junk[:, :], pt[:, 0:1280], AF.Square), [pt], [junk])
                dve(lambda: V.tensor_reduce(st[:, 0:10], junk[:, :].rearrange("p (h d) -> p h d", h=10), AX.X, ALU.add), [junk], [st])
                dve(lambda: V.tensor_scalar(st[:, 10:20], st[:, 0:10], 1.0 / 128, EPS, ALU.mult, ALU.add), [st], [st])
                act(lambda: A.activation(st[:, 20:30], st[:, 10:20], AF.Sqrt), [st], [st])
                dve(lambda: V.reciprocal(st[:, 30:40], st[:, 20:30]), [st], [st])
                dve(lambda: V.tensor_tensor(qn[:, :, :], qk3, st[:, 30:40].unsqueeze(2).to_broadcast([128, 10, 128]), ALU.mult),
                    [pt, st], [qn])
                dve(lambda: V.tensor_tensor(qn[:, :, :], qn[:, :, :], gqk[:, :, :], ALU.mult), [qn, gqk], [qn])
                rope(qn[:, :, :], 10, 32, rp[:, 0, :], qr[:, :].rearrange("p (h d) -> p h d", h=10), [qn, rp], qr)
                transpose_out(qr, 0, 4, qT_d, 0, t)
                transpose_out(qr, 512, 4, qT_d, 512, t)
                transpose_out(qr, 1024, 2, kT_d, 0, t)
                tk.dma("pool", v_d[t * 128:(t + 1) * 128, :], pt[:, 1280:1536], [pt], [v_d])
                tk.dma("pool", f_d[t * 128:(t + 1) * 128, :], pt[:, 1536:2048], [pt], [f_d])
                tk.dma("pool", rv_d[t * 128:(t + 1) * 128, :], pt[:, 3072:3584], [pt], [rv_d])
                dve(lambda: V.tensor_copy(rr[:, 0:4, :], pt[:, 2048:2560].rearrange("p (h d) -> p h d", h=4)), [pt], [rr])
                dve(lambda: V.tensor_scalar(rr[:, 4:8, :], pt[:, 2560:3072].rearrange("p (h d) -> p h d", h=4), 128.0 ** -0.5, None, ALU.mult),
                    [pt], [rr])
                for di in range(2):
                    rope(rr[:, :, :], 8, 64, rp[:, 1 + di, :], rfb[di][:, :].rearrange("p (h d) -> p h d", h=8), [rr, rp], rfb[di])
                    transpose_out(rfb[di], 0, 4, rT_d, di * 1024, t)
                    transpose_out(rfb[di], 512, 4, rT_d, di * 1024 + 512, t)
                    tk.dma("sp", rk_d[t * 128:(t + 1) * 128, di * 512:(di + 1) * 512], rfb[di][:, 512:1024], [rfb[di]], [rk_d])
                act(lambda: A.activation(gs[:, :], pt[:, 3584:4608], AF.Silu), [pt], [gs])
                tk.dma("sp", g_d[t * 128:(t + 1) * 128, :], gs[:, :], [gs], [g_d])
        tk.barrier()

    def stage_D(l):
        with ExitStack() as es:
            gq = tk.sb(es, "gqD", [128, 256], F32)
            stb = tk.sb(es, "stD", [128, 4], F32)
            kT = tk.sb(es, "kTD", [128, T], BF16)
            vv = tk.sb(es, "vD", [128, NT, 128], BF16)
            qTb = [[tk.sb(es, f"qTb{s_}_{i}", [128, 512], BF16) for i in range(2)] for s_ in range(2)]
            pT = [[tk.sb(es, f"pT{s_}_{i}", [128, 512], BF16) for i in range(2)] for s_ in range(2)]
            psS = [[tk.ps(es, f"psS{s_}_{i}", [128, 512]) for i in range(2)] for s_ in range(2)]
            psO = [tk.ps(es, f"psO{s_}", [128, 512]) for s_ in range(2)]
            psZ = [tk.ps(es, f"psZ{s_}", [128, 512]) for s_ in range(2)]
            accZ = [tk.sb(es, f"accZ{s_}", [128, 512], F32) for s_ in range(2)]
            rz = [tk.sb(es, f"rzD{s_}", [128, 512], F32) for s_ in range(2)]
            ob = [tk.sb(es, f"obD{s_}", [128, 512], BF16) for s_ in range(2)]
            tk.dma("sp", gq[:, 0:128], q_norm[l, :].partition_broadcast(128), [q_norm], [gq])
            tk.dma("sp", gq[:, 128:256], k_norm[l, :].partition_broadcast(128), [k_norm], [gq])
            dve(lambda: V.tensor_reduce(stb[:, 0:2], gq[:, :].rearrange("p (a d) -> p a d", a=2), AX.X, ALU.max, apply_absolute_value=True),
                [gq], [stb])
            dve(lambda: V.tensor_tensor(stb[:, 2:3], stb[:, 0:1], stb[:, 1:2], ALU.mult), [stb], [stb])
            dve(lambda: V.tensor_scalar(stb[:, 3:4], stb[:, 2:3], -(128.0 ** 0.5), None, ALU.mult), [stb], [stb])
            scale = 128.0 ** -0.5
            kq = 0
            for kvh in range(2):
                tk.dma("sp", kT[:, :], kT_d[kvh * 128:(kvh + 1) * 128, :], [kT_d], [kT])
                tk.dma("sp", vv[:, :, :], v_d[:, kvh * 128:(kvh + 1) * 128].rearrange("(c p) d -> p c d", p=128), [v_d], [vv])
                for hp in range(2):
                    heads = [kvh * 4 + 2 * hp, kvh * 4 + 2 * hp + 1]
                    blocks = [(0, 256, [0, 1])] + [(256 + qb * 512, 512, list(range(NT))) for qb in range(8)]
                    for (c0, nq, chunks) in blocks:
                        qbs = [qTb[s_][kq % 2] for s_ in range(2)]
                        kq += 1
                        for s_ in range(2):
                            hq = heads[s_]
                            tk.dma("sp", qbs[s_][:, 0:nq], qT_d[hq * 128:(hq + 1) * 128, c0:c0 + nq], [qT_d], [qbs[s_]])
                        nch = len(chunks)

                        def s_mm(s_, i):
                            kc = chunks[i]
                            mm(psS[s_][i % 2][:, 0:nq], kT[:, kc * 128:(kc + 1) * 128], qbs[s_][:, 0:nq], True, True,
                               [kT, qbs[s_]], [psS[s_][i % 2]])
                        s_mm(0, 0)
                        s_mm(1, 0)
                        for i in range(nch):
                            kc = chunks[i]
                            if i + 1 < nch:
                                s_mm(0, i + 1)
                                s_mm(1, i + 1)
                            for s_ in range(2):
                                p = pT[s_][i % 2]
                                act(lambda s_=s_, p=p: A.activation(p[:, 0:nq], psS[s_][i % 2][:, 0:nq], AF.Exp, bias=stb[:, 3:4], scale=scale),
                                    [psS[s_][i % 2], stb], [p])
                            for s_ in range(2):
                                p = pT[s_][i % 2]
                                mm(psO[s_][:, 0:nq], vv[:, kc, :], p[:, 0:nq], i == 0, i == nch - 1, [vv, p], [psO[s_]])
                                az = accZ[s_]
                                if i == 0:
                                    dve(lambda p=p, az=az: V.tensor_copy(az[:, 0:nq], p[:, 0:nq]), [p], [az])
                                else:
                                    dve(lambda p=p, az=az: V.tensor_tensor(az[:, 0:nq], az[:, 0:nq], p[:, 0:nq], ALU.add), [az, p], [az])
                        for s_ in range(2):
                            hq = heads[s_]
                            mm(psZ[s_][:, 0:nq], onesf[:, :], accZ[s_][:, 0:nq], True, True, [onesf, accZ[s_]], [psZ[s_]])
                            dve(lambda s_=s_: V.reciprocal(rz[s_][:, 0:nq], psZ[s_][:, 0:nq]), [psZ[s_]], [rz[s_]])
                            dve(lambda s_=s_: V.tensor_tensor(ob[s_][:, 0:nq], psO[s_][:, 0:nq], rz[s_][:, 0:nq], ALU.mult), [psO[s_], rz[s_]], [ob[s_]])
                            tk.dma("sp", mixT_d[hq * 128:(hq + 1) * 128, c0:c0 + nq], ob[s_][:, 0:nq], [ob[s_]], [mixT_d])
        tk.barrier()

    def stage_E(l):
        with ExitStack() as es:
            cc = tk.sb(es, "ccE", [128, 256], BF16)
            tk.dma("sp", cc[:, :], t_CC[:, :], [], [cc])
            for (tok0, n, CLt, SLt, kbw) in ((LC, S, t_CL, t_SL, 512), (0, LC, t_CL2, t_SL2, 256)):
                nch = n // 128
                with ExitStack() as es2:
                    xf = tk.sb(es2, "xfE", [128, nch, 512], BF16)
                    CLp = tk.sb(es2, "CLp", [128, nch, kbw], BF16)
                    SLp = tk.sb(es2, "SLp", [128, nch, kbw], BF16)
                    psA = tk.ps(es2, "psAE", [128, 512])
                    psB = tk.ps(es2, "psBE", [128, 512])
                    psY = tk.ps(es2, "psYE", [128, 512])
                    aT = tk.sb(es2, "aTE", [128, 512], BF16)
                    bT = tk.sb(es2, "bTE", [128, 512], BF16)
                    yT = [tk.sb(es2, f"yTE{i}", [128, 512], BF16) for i in range(2)]
                    tk.dma("sp", xf[:, :, :], f_d[tok0:tok0 + n, :].rearrange("(c p) d -> p c d", p=128), [f_d], [xf])
                    k = 0
                    for kb in range(n // kbw):
                        tk.dma("sp", CLp[:, :, :], CLt[:, kb * kbw:(kb + 1) * kbw].rearrange("(c p) k -> p c k", p=128), [], [CLp])
                        tk.dma("sp", SLp[:, :, :], SLt[:, kb * kbw:(kb + 1) * kbw].rearrange("(c p) k -> p c k", p=128), [], [SLp])
                        for g in range(4):
                            for c in range(nch):
                                mm(psA[:, 0:kbw], xf[:, c, g * 128:(g + 1) * 128], CLp[:, c, :], c == 0, c == nch - 1, [xf, CLp], [psA])
                            for c in range(nch):
                                mm(psB[:, 0:kbw], xf[:, c, g * 128:(g + 1) * 128], SLp[:, c, :], c == 0, c == nch - 1, [xf, SLp], [psB])
                            dve(lambda: V.tensor_copy(aT[:, 0:kbw], psA[:, 0:kbw]), [psA], [aT])
                            act(lambda: A.copy(bT[:, 0:kbw], psB[:, 0:kbw]), [psB], [bT])
                            mm(psY[:, 0:kbw], cc[:, 0:128], aT[:, 0:kbw], True, False, [cc, aT], [psY])
                            mm(psY[:, 0:kbw], cc[:, 128:256], bT[:, 0:kbw], False, True, [cc, bT], [psY])
                            y = yT[k % 2]
                            k += 1
                            evac(y[:, 0:kbw], psY[:, 0:kbw], [psY], [y])
                            tk.dma("sp", mixT_d[1024 + g * 128:1024 + (g + 1) * 128, tok0 + kb * kbw:tok0 + (kb + 1) * kbw],
                                   y[:, 0:kbw], [y], [mixT_d])
                tk.barrier()

    def stage_F(l):
        with ExitStack() as es:
            rt = tk.sb(es, "rtF", [128, 6 * 128 + 2], F32)
            lgb = tk.sb(es, "lgb", [128, 8], F32)
            tk.dma("sp", rt[:, :], t_ret[:, :], [], [rt])
            tk.dma("sp", lgb[:, :], rld[l, :].partition_broadcast(128), [rld], [lgb])
            maskT = tk.sb(es, "maskT", [128, 128], F32)
            qdr = tk.sb(es, "qdr", [128, 128], F32)
            kdc = tk.sb(es, "kdc", [128, 2], F32)
            rqT = tk.sb(es, "rqTF", [128, NT, 128], BF16)
            rkT = tk.sb(es, "rkTF", [128, NT, 128], BF16)
            rkt = tk.sb(es, "rktF", [128, NT, 128], BF16)
            rvt = tk.sb(es, "rvtF", [128, NT, 128], BF16)
            gt = tk.sb(es, "gtF", [128, NT, 128], F32)
            smA = tk.sb(es, "smAF", [128, NT, 128], BF16)
            Uall = tk.sb(es, "UallF", [128, NT, 128], F32)
            SbA = tk.sb(es, "SbAF", [128, NT, 128], BF16)
            obuf = tk.sb(es, "obufF", [128, NT, 128], F32)
            Sf = tk.sb(es, "SfF", [128, 128], F32)
            st = tk.sb(es, "stF", [128, 6, NT], F32)
            psg = [tk.ps(es, f"psgF{i}", [128, 4, 128]) for i in range(4)]
            kp = [0]
            groups = [list(range(g, min(g + 4, NT))) for g in range(0, NT, 4)]
            for di in range(2):
                for h in range(4):
                    li = di * 4 + h
                    act(lambda: A.activation(maskT[:, :], rt[:, (2 * di) * 128:(2 * di + 1) * 128], AF.Exp, scale=lgb[:, li:li + 1]),
                        [rt, lgb], [maskT])
                    dve(lambda: V.tensor_tensor(maskT[:, :], maskT[:, :], rt[:, (2 * di + 1) * 128:(2 * di + 2) * 128], ALU.mult),
                        [maskT, rt], [maskT])
                    act(lambda: A.activation(qdr[:, :], rt[:, (4 + di) * 128:(5 + di) * 128], AF.Exp, scale=lgb[:, li:li + 1]),
                        [rt, lgb], [qdr])
                    act(lambda: A.activation(kdc[:, 0:1], rt[:, 768 + di:769 + di], AF.Exp, scale=lgb[:, li:li + 1]), [rt, lgb], [kdc])
                    act(lambda: A.activation(kdc[:, 1:2], lgb[:, li:li + 1], AF.Exp, scale=128.0), [lgb], [kdc])
                    tk.dma("sp", rqT[:, :, :], rT_d[di * 1024 + h * 128:di * 1024 + (h + 1) * 128, :].rearrange("p (c n) -> p c n", n=128),
                           [rT_d], [rqT])
                    tk.dma("sp", rkT[:, :, :], rT_d[di * 1024 + 512 + h * 128:di * 1024 + 512 + (h + 1) * 128, :].rearrange("p (c n) -> p c n", n=128),
                           [rT_d], [rkT])
                    tk.dma("sp", rkt[:, :, :], rk_d[:, di * 512 + h * 128:di * 512 + (h + 1) * 128].rearrange("(c p) d -> p c d", p=128),
                           [rk_d], [rkt])
                    if di == 0 or True:
                        tk.dma("sp", rvt[:, :, :], rv_d[:, h * 128:(h + 1) * 128].rearrange("(c p) d -> p c d", p=128), [rv_d], [rvt])
                    tk.dma("sp", gt[:, :, :], g_d[:, di * 512 + h * 128:di * 512 + (h + 1) * 128].rearrange("(c p) d -> p c d", p=128),
                           [g_d], [gt])
                    dve(lambda: V.tensor_scalar(rkt[:, :, :], rkt[:, :, :], kdc[:, 0:1], None, ALU.mult), [rkt, kdc], [rkt])
                    for grp in groups:
                        n = len(grp)
                        c0 = grp[0]
                        ps = psg[kp[0] % 4]
                        kp[0] += 1
                        for j, c in enumerate(grp):
                            mm(ps[:, j, :], rkT[:, c, :], rqT[:, c, :], True, True, [rkT, rqT], [ps])
                        dve(lambda ps=ps, n=n, c0=c0: V.tensor_tensor(smA[:, c0:c0 + n, :], ps[:, 0:n, :],
                                                                     maskT[:, :].unsqueeze(1).to_broadcast([128, n, 128]), ALU.mult),
                            [ps, maskT], [smA])
                        ps = psg[kp[0] % 4]
                        kp[0] += 1
                        for j, c in enumerate(grp):
                            mm(ps[:, j, :], rkt[:, c, :], rvt[:, c, :], True, True, [rkt, rvt], [ps])
                        act(lambda ps=ps, n=n, c0=c0: A.copy(Uall[:, c0:c0 + n, :], ps[:, 0:n, :]), [ps], [Uall])
                    dve(lambda: V.tensor_tensor(rqT[:, :, :], rqT[:, :, :], qdr[:, :].unsqueeze(1).to_broadcast([128, NT, 128]), ALU.mult),
                        [rqT, qdr], [rqT])
                    order = list(range(NT)) if di == 0 else [1, 0] + list(range(NT - 1, 1, -1))
                    dve(lambda: V.memset(Sf[:, :], 0.0), [], [Sf])
                    dve(lambda: V.memset(SbA[:, order[0], :], 0.0), [], [SbA])
                    for k in range(NT - 1):
                        c = order[k]
                        cn = order[k + 1]
                        dve(lambda c=c: V.scalar_tensor_tensor(Sf[:, :], Sf[:, :], kdc[:, 1:2], Uall[:, c, :], ALU.mult, ALU.add),
                            [Sf, kdc, Uall], [Sf])
                        dve(lambda cn=cn: V.tensor_copy(SbA[:, cn, :], Sf[:, :]), [Sf], [SbA])
                    for grp in groups:
                        n = len(grp)
                        c0 = grp[0]
                        ps = psg[kp[0] % 4]
                        kp[0] += 1
                        for j, c in enumerate(grp):
                            mm(ps[:, j, :], smA[:, c, :], rvt[:, c, :], True, False, [smA, rvt], [ps])
                            mm(ps[:, j, :], rqT[:, c, :], SbA[:, c, :], False, True, [rqT, SbA], [ps])
                        evac(obuf[:, c0:c0 + n, :], ps[:, 0:n, :], [ps], [obuf])
                    dve(lambda: V.tensor_reduce(st[:, 0, :], obuf[:, :, :], AX.X, ALU.add), [obuf], [st])
                    dve(lambda: V.tensor_scalar(st[:, 1, :], st[:, 0, :], -1.0 / 128, None, ALU.mult), [st], [st])
                    dve(lambda: V.tensor_tensor(obuf[:, :, :], obuf[:, :, :], st[:, 1, :].unsqueeze(2).to_broadcast([128, NT, 128]), ALU.add),
                        [obuf, st], [obuf])
                    act(lambda: A.activation(Uall[:, :, :], obuf[:, :, :], AF.Square), [obuf], [Uall])
                    dve(lambda: V.tensor_reduce(st[:, 2, :], Uall[:, :, :], AX.X, ALU.add), [Uall], [st])
                    dve(lambda: V.tensor_scalar(st[:, 3, :], st[:, 2, :], 1.0 / 128, EPS, ALU.mult, ALU.add), [st], [st])
                    act(lambda: A.activation(st[:, 4, :], st[:, 3, :], AF.Sqrt), [st], [st])
                    dve(lambda: V.reciprocal(st[:, 5, :], st[:, 4, :]), [st], [st])
                    dve(lambda: V.tensor_tensor(obuf[:, :, :], obuf[:, :, :], st[:, 5, :].unsqueeze(2).to_broadcast([128, NT, 128]), ALU.mult),
                        [obuf, st], [obuf])
                    dve(lambda: V.tensor_tensor(obuf[:, :, :], obuf[:, :, :], gt[:, :, :], ALU.mult), [obuf, gt], [obuf])
                    tk.dma("sp", ro_d[:, di * 512 + h * 128:di * 512 + (h + 1) * 128].rearrange("(c p) e -> p c e", p=128), obuf[:, :, :],
                           [obuf], [ro_d])
        tk.barrier()

    def stage_G(l):
        with ExitStack() as es:
            ro = [tk.sb(es, f"roG{i}", [128, 1024], F32) for i in range(2)]
            rb = [tk.sb(es, f"rbG{i}", [128, 512], BF16) for i in range(2)]
            tr = [tk.sb(es, f"trG{i}", [128, 512], BF16) for i in range(2)]
            pst = [tk.ps(es, f"pstG{i}", [128, 512]) for i in range(2)]
            for t in range(NT):
                r_, b_, t_, ps = ro[t % 2], rb[t % 2], tr[t % 2], pst[t % 2]
                tk.dma("sp", r_[:, :], ro_d[t * 128:(t + 1) * 128, :], [ro_d], [r_])
                dve(lambda: V.tensor_tensor(b_[:, :], r_[:, 0:512], r_[:, 512:1024], ALU.add), [r_], [b_])
                for j in range(4):
                    mm(ps[:, j * 128:(j + 1) * 128], b_[:, j * 128:(j + 1) * 128], identb[:, :], True, True, [b_, identb], [ps])
                evac(t_[:, :], ps[:, :], [ps], [t_])
                tk.dma("sp", mixT_d[1536:2048, t * 128:(t + 1) * 128].rearrange("(j p) n -> p j n", p=128),
                       t_[:, :].rearrange("p (j n) -> p j n", j=4), [t_], [mixT_d])
        tk.barrier()

    def stage_H(l, affT):
        with ExitStack() as es:
            wo = tk.sb(es, "woH", [128, 16, D], BF16)
            gm = tk.sb(es, "gmH", [128, D], F32)
            GT1 = tk.sb(es, "GT1", [128, D], F32)
            G2_ = tk.sb(es, "G2", [128, D], F32)
            SH2_ = tk.sb(es, "SH2", [128, D], F32)
            GT = [GT1, GT1]
            G2 = [G2_, G2_]
            SH2 = [SH2_, SH2_]
            wr = tk.sb(es, "wrH", [128, 16, 16], F32)
            xts = [tk.sb(es, f"xtH{i}", [128, D], F32) for i in range(2)]
            mT = [tk.sb(es, f"mTH{i}", [128, 16, 128], BF16) for i in range(2)]
            tmps = [tk.sb(es, f"tmpH{i}", [128, D], F32) for i in range(2)]
            h2bs = [tk.sb(es, f"h2bH{i}", [128, D], BF16) for i in range(2)]
            h2T = tk.sb(es, "h2TH", [128, 16, 128], F32)
            st = tk.sb(es, "stH", [128, 8], F32)
            lg = tk.sb(es, "lgH", [128, 16], F32)
            psY = [tk.ps(es, f"psYH{i}", [128, 512]) for i in range(2)]
            psT = [tk.ps(es, f"psTH{i}", [128, 512]) for i in range(2)]
            psL = tk.ps(es, "psLH", [128, 16])
            psAT = tk.ps(es, "psATH", [16, 128])
            for nb in range(4):
                tk.dma("pool", wo[:, :, nb * 512:(nb + 1) * 512],
                       w_out[l * D:(l + 1) * D, nb * 512:(nb + 1) * 512].rearrange("(c p) n -> p c n", p=128), [w_out], [wo])
            tk.dma("sp", gm[:, :], norm_ffn[l, :].partition_broadcast(128), [norm_ffn], [gm])
            tk.dma("sp", wr[:, :, :], w_router[l * D:(l + 1) * D, :].rearrange("(c p) e -> p c e", p=128), [w_router], [wr])
            for t in range(NT):
                w = 1 if t < 2 else 0
                if t in (0, 2):
                    tk.dma("sp", GT[w][:, :], modrow(l, w, 2), [mod_d], [GT[w]])
                    tk.dma("sp", G2[w][:, :], modrow(l, w, 4), [mod_d], [G2[w]])
                    tk.dma("sp", SH2[w][:, :], modrow(l, w, 3), [mod_d], [SH2[w]])
                    dve(lambda w=w: V.tensor_tensor(G2[w][:, :], G2[w][:, :], gm[:, :], ALU.mult), [G2[w], gm], [G2[w]])
                m = mT[t % 2]
                xt, tmp, h2b = xts[t % 2], tmps[t % 2], h2bs[t % 2]
                junk = h2b
                h2f = tmp
                rows = slice(t * 128, (t + 1) * 128)
                tk.dma("sp", xt[:, :], xcur[rows, :], [xcur], [xt])
                tk.dma("sp", m[:, :, :], mixT_d[:, rows].rearrange("(c p) n -> p c n", p=128), [mixT_d], [m])
                for nb in range(4):
                    ps = psY[nb % 2]
                    cs = slice(nb * 512, (nb + 1) * 512)
                    for c in range(16):
                        mm(ps[:, :], m[:, c, :], wo[:, c, cs], c == 0, c == 15, [m, wo], [ps])
                    dve(lambda ps=ps, cs=cs, w=w: V.tensor_tensor(tmp[:, cs], ps[:, :], GT[w][:, cs], ALU.mult), [ps, GT[w]], [tmp])
                    dve(lambda cs=cs: V.tensor_tensor(xt[:, cs], xt[:, cs], tmp[:, cs], ALU.add), [xt, tmp], [xt])
                tk.dma("sp", xcur[rows, :], xt[:, :], [xt], [xcur])
                rms_rstd((junk, st), xt, D)
                dve(lambda w=w: V.scalar_tensor_tensor(tmp[:, :], xt[:, :], st[:, 3:4], G2[w][:, :], ALU.mult, ALU.mult), [xt, st, G2[w]], [tmp])
                dve(lambda w=w: V.tensor_tensor(h2f[:, :], tmp[:, :], SH2[w][:, :], ALU.add), [tmp, SH2[w]], [h2f])
                act(lambda: A.copy(h2b[:, :], h2f[:, :]), [h2f], [h2b])
                tk.dma("sp", h2_d[rows, :], h2b[:, :], [h2b], [h2_d])
                for g4 in range(4):
                    ps = psT[g4 % 2]
                    for j in range(4):
                        c = g4 * 4 + j
                        mm(ps[:, j * 128:(j + 1) * 128], h2f[:, c * 128:(c + 1) * 128], identf[:, :], True, True, [h2f, identf], [ps])
                    evac(h2T[:, g4 * 4:(g4 + 1) * 4, :], ps[:, :].rearrange("p (j n) -> p j n", j=4), [ps], [h2T])
                for c in range(16):
                    mm(psL[:, :], h2T[:, c, :], wr[:, c, :], c == 0, c == 15, [h2T, wr], [psL])
                dve(lambda: V.reduce_max(st[:, 4:5], psL[:, :], axis=AX.X), [psL], [st])
                dve(lambda: V.tensor_scalar(st[:, 5:6], st[:, 4:5], -1.0, None, ALU.mult), [st], [st])
                act(lambda: A.activation(lg[:, :], psL[:, :], AF.Exp, bias=st[:, 5:6], scale=1.0, accum_out=st[:, 6:7]), [psL, st], [lg, st])
                dve(lambda: V.reciprocal(st[:, 7:8], st[:, 6:7]), [st], [st])
                dve(lambda t=t: V.tensor_scalar(aff_all[:, t, :], lg[:, :], st[:, 7:8], None, ALU.mult), [lg, st], [aff_all])
                mm(psAT[:, :], aff_all[:, t, :], identf[:, :], True, True, [aff_all, identf], [psAT])
                evac(affT[:, rows], psAT[:, :], [psAT], [affT])
        tk.barrier()

    def stage_I(l, affT):
        with ExitStack() as es:
            lo = tk.sb(es, "loI", [16, 2], F32)
            hi = tk.sb(es, "hiI", [16, 2], F32)
            mid = tk.sb(es, "midI", [16, 2], F32)
            cnt = tk.sb(es, "cntI", [16, 2], F32)
            ge = tk.sb(es, "geI", [16, 2], F32)
            d1 = tk.sb(es, "d1I", [16, 2], F32)
            junk = tk.sb(es, "junkI", [16, S], F32)
            dg = tk.sb(es, "dgI", [16, 2, 16], F32)
            thrb = tk.sb(es, "thrbI", [128, 2, 16], F32)
            sel = tk.sb(es, "selI", [128, NT, 16], F32)
            selb = tk.sb(es, "selbI", [128, NT, 16], BF16)
            tmp = tk.sb(es, "tmpI", [128, 16], F32)
            psb = tk.ps(es, "psbI", [128, 16])
            pss = [tk.ps(es, f"pssI{i}", [128, 16]) for i in range(2)]
            pst = [tk.ps(es, f"pstI{i}", [16, 128]) for i in range(2)]
            sidxT = tk.sb(es, "sidxTI", [16, T], F32)
            gateT = tk.sb(es, "gateTI", [16, T], F32)
            dve(lambda: V.memset(lo[:, :], 0.0), [], [lo])
            dve(lambda: V.memset(hi[:, :], 1.0), [], [hi])
            sets = [(0, LC, CAP_C), (LC, S, CAP_L)]
            for it in range(34):
                dve(lambda: V.tensor_tensor(mid[:, :], lo[:, :], hi[:, :], ALU.add), [lo, hi], [mid])
                dve(lambda: V.tensor_scalar(mid[:, :], mid[:, :], 0.5, None, ALU.mult), [mid], [mid])
                for si, (c0, n, cap) in enumerate(sets):
                    dve(lambda si=si, c0=c0, n=n: V.tensor_scalar(junk[:, 0:n], affT[:, c0:c0 + n], mid[:, si:si + 1], None, ALU.is_ge),
                        [affT, mid], [junk])
                    dve(lambda si=si, n=n: V.reduce_sum(cnt[:, si:si + 1], junk[:, 0:n], axis=AX.X), [junk], [cnt])
                    dve(lambda si=si, cap=cap: V.tensor_scalar(ge[:, si:si + 1], cnt[:, si:si + 1], float(cap) - 0.5, None, ALU.is_ge), [cnt], [ge])
                dve(lambda: V.tensor_tensor(d1[:, :], mid[:, :], lo[:, :], ALU.subtract), [mid, lo], [d1])
                dve(lambda: V.tensor_tensor(d1[:, :], d1[:, :], ge[:, :], ALU.mult), [d1, ge], [d1])
                dve(lambda: V.tensor_tensor(lo[:, :], lo[:, :], d1[:, :], ALU.add), [lo, d1], [lo])
                dve(lambda: V.tensor_tensor(d1[:, :], hi[:, :], mid[:, :], ALU.subtract), [hi, mid], [d1])
                dve(lambda: V.tensor_tensor(d1[:, :], d1[:, :], ge[:, :], ALU.mult), [d1, ge], [d1])
                dve(lambda: V.tensor_tensor(hi[:, :], mid[:, :], d1[:, :], ALU.add), [mid, d1], [hi])
            for si in range(2):
                dve(lambda si=si: V.tensor_scalar(dg[:, si, :], identf[0:16, 0:16], lo[:, si:si + 1], None, ALU.mult), [identf, lo], [dg])
                mm(psb[:, :], onesf[0:16, :], dg[:, si, :], True, True, [onesf, dg], [psb])
                dve(lambda si=si: V.tensor_copy(thrb[:, si, :], psb[:, :]), [psb], [thrb])
            dve(lambda: V.tensor_tensor(sel[:, 0:2, :], aff_all[:, 0:2, :], thrb[:, 0, :].unsqueeze(1).to_broadcast([128, 2, 16]), ALU.is_ge),
                [aff_all, thrb], [sel])
            dve(lambda: V.tensor_tensor(sel[:, 2:NT, :], aff_all[:, 2:NT, :], thrb[:, 1, :].unsqueeze(1).to_broadcast([128, NT - 2, 16]), ALU.is_ge),
                [aff_all, thrb], [sel])
            dve(lambda: V.tensor_tensor(gate_all[:, :, :], aff_all[:, :, :], sel[:, :, :], ALU.mult), [aff_all, sel], [gate_all])
            dve(lambda: V.tensor_copy(selb[:, :, :], sel[:, :, :]), [sel], [selb])
            k = 0
            for (t0, t1, off) in ((0, 2, float(CAP_L)), (2, NT, 0.0)):
                for t in range(t0, t1):
                    ps = pss[k % 2]
                    k += 1
                    for c in range(t0, t):
                        mm(ps[:, :], onesb[:, :], selb[:, c, :], c == t0, False, [onesb, selb], [ps])
                    mm(ps[:, :], ustrb[:, :], selb[:, t, :], t == t0, True, [ustrb, selb], [ps])
                    dve(lambda ps=ps, off=off: V.tensor_scalar(tmp[:, :], ps[:, :], off + 1.0, None, ALU.add), [ps], [tmp])
                    dve(lambda t=t: V.tensor_tensor(tmp[:, :], tmp[:, :], sel[:, t, :], ALU.mult), [tmp, sel], [tmp])
                    dve(lambda t=t: V.tensor_scalar(sidx_all[:, t, :], tmp[:, :], -1.0, None, ALU.add), [tmp], [sidx_all])
            for t in range(NT):
                rows = slice(t * 128, (t + 1) * 128)
                mm(pst[0][:, :], sidx_all[:, t, :], identf[:, :], True, True, [sidx_all, identf], [pst[0]])
                dve(lambda rows=rows: V.tensor_copy(sidxT[:, rows], pst[0][:, :]), [pst[0]], [sidxT])
                mm(pst[1][:, :], gate_all[:, t, :], identf[:, :], True, True, [gate_all, identf], [pst[1]])
                act(lambda rows=rows: A.copy(gateT[:, rows], pst[1][:, :]), [pst[1]], [gateT])
            lo_ = tk.sb(es, "loTI", [16, T], F32)
            hib = tk.sb(es, "hibI", [16, T], BF16)
            lob = tk.sb(es, "lobI", [16, T], BF16)
            gtb = tk.sb(es, "gtbI", [16, T], BF16)
            dve(lambda: V.tensor_scalar(sidxT[:, :], sidxT[:, :], 1.0, None, ALU.add), [sidxT], [sidxT])
            cb = tk.sb(es, "cbI", [16, T], BF16)
            dve(lambda: V.tensor_scalar(hib[:, :], sidxT[:, :], 256.0, None, ALU.min), [sidxT], [hib])
            dve(lambda: V.tensor_scalar(lo_[:, :], sidxT[:, :], 256.0, None, ALU.min), [sidxT], [lo_])
            dve(lambda: V.tensor_tensor(sidxT[:, :], sidxT[:, :], lo_[:, :], ALU.subtract), [sidxT, lo_], [sidxT])
            dve(lambda: V.tensor_scalar(lob[:, :], sidxT[:, :], 256.0, None, ALU.min), [sidxT], [lob])
            dve(lambda: V.tensor_scalar(lo_[:, :], sidxT[:, :], 256.0, None, ALU.min), [sidxT], [lo_])
            dve(lambda: V.tensor_tensor(cb[:, :], sidxT[:, :], lo_[:, :], ALU.subtract), [sidxT, lo_], [cb])
            act(lambda: A.copy(gtb[:, :], gateT[:, :]), [gateT], [gtb])
            tk.dma("sp", shl_d[0:16, :], hib[:, :], [hib], [shl_d])
            tk.dma("sp", shl_d[16:32, :], lob[:, :], [lob], [shl_d])
            tk.dma("sp", shl_d[32:48, :], cb[:, :], [cb], [shl_d])
            tk.dma("sp", gtb_d[:, :], gtb[:, :], [gtb], [gtb_d])
        tk.barrier()

    def stage_J(l):
        with ExitStack() as es:
            h2b = tk.sb(es, "h2bJ", [128, NT, 512], BF16)
            P = [tk.sb(es, f"PJ{i}", [128, 512], BF16) for i in range(3)]
            psG = [tk.ps(es, f"psGJ{i}", [128, 512]) for i in range(4)]
            psC = tk.ps(es, "psCJ", [128, 4, 32])
            xg = [tk.sb(es, f"xgJ{i}", [128, 4, NSL], BF16) for i in range(2)]
            k = 0
            for db in range(4):
                tk.dma("sp", h2b[:, :, :], h2_d[:, db * 512:(db + 1) * 512].rearrange("(t p) d -> p t d", p=128), [h2_d], [h2b])
                for e in range(NE):
                    xo = xg[e % 2]
                    for t in range(NT):
                        p = P[k % 3]
                        k += 1
                        if t < 2:
                            dve(lambda p=p, t=t, e=e: V.tensor_scalar(p[:, 0:32], iota5[:, 0:32], float(CAP_L), sidx_all[:, t, e:e + 1],
                                                                      ALU.add, ALU.is_equal), [iota5, sidx_all], [p])
                            for j in range(4):
                                mm(psC[:, j, :], h2b[:, t, j * 128:(j + 1) * 128], p[:, 0:32], t == 0, t == 1, [h2b, p], [psC])
                        else:
                            dve(lambda p=p, t=t, e=e: V.tensor_scalar(p[:, :], iota5[:, :], sidx_all[:, t, e:e + 1], None, ALU.is_equal),
                                [iota5, sidx_all], [p])
                            for j in range(4):
                                mm(psG[j][:, :], h2b[:, t, j * 128:(j + 1) * 128], p[:, :], t == 2, t == NT - 1, [h2b, p], [psG[j]])
                    for j in range(4):
                        evac(xo[:, j, 0:CAP_L], psG[j][:, :], [psG[j]], [xo])
                    evac(xo[:, :, CAP_L:NSL], psC[:, :, :], [psC], [xo])
                    tk.dma("sp", xg_d[e * D + db * 512:e * D + (db + 1) * 512, :].rearrange("(j p) s -> p j s", p=128), xo[:, :, :], [xo], [xg_d])
        tk.barrier()

    def stage_K(l):
        with ExitStack() as es:
            wgs = [tk.sb(es, f"wgK{i}", [128, 16, FF], BF16) for i in range(2)]
            wus = [tk.sb(es, f"wuK{i}", [128, 16, FF], BF16) for i in range(2)]
            wd = tk.sb(es, "wdK", [128, 8, D], BF16)
            xg0 = tk.sb(es, "xgK", [128, 16, NSL], BF16)
            gT = tk.sb(es, "gTK", [128, 8, NSL], BF16)
            sa = [tk.sb(es, f"saK{i}", [128, NSL], F32) for i in range(2)]
            psA = [tk.ps(es, f"psAK{i}", [128, 512]) for i in range(2)]
            psU = [tk.ps(es, f"psUK{i}", [128, 512]) for i in range(2)]
            psA2 = tk.ps(es, "psA2K", [128, 2, 32])
            psY = [tk.ps(es, f"psYK{i}", [128, 512]) for i in range(2)]
            yo = [tk.sb(es, f"yoK{i}", [128, 512], BF16) for i in range(2)]
            ky = 0

            def load_gu(e):
                r0 = (l * NE + e) * D
                for hf in range(2):
                    fs = slice(hf * 512, (hf + 1) * 512)
                    tk.dma("pool", wgs[e % 2][:, :, fs], w_gate[r0:r0 + D, fs].rearrange("(c p) f -> p c f", p=128), [w_gate], [wgs[e % 2]])
                    tk.dma("pool", wus[e % 2][:, :, fs], w_up[r0:r0 + D, fs].rearrange("(c p) f -> p c f", p=128), [w_up], [wus[e % 2]])

            def load_d(e):
                r1 = (l * NE + e) * FF
                for nb in range(4):
                    cs = slice(nb * 512, (nb + 1) * 512)
                    tk.dma("pool", wd[:, :, cs], w_down[r1:r1 + FF, cs].rearrange("(c p) n -> p c n", p=128), [w_down], [wd])

            load_gu(0)
            load_d(0)
            for e in range(NE):
                wg, wu = wgs[e % 2], wus[e % 2]
                x_ = xg0
                tk.dma("sp", x_[:, :, :], xg_d[e * D:(e + 1) * D, :].rearrange("(c p) s -> p c s", p=128), [xg_d], [x_])
                if e + 1 < NE:
                    load_gu(e + 1)
                for fc in range(8):
                    pa, pu, s_ = psA[fc % 2], psU[fc % 2], sa[fc % 2]
                    fsl = slice(fc * 128, (fc + 1) * 128)
                    for c in range(16):
                        mm(pa[:, :], wg[:, c, fsl], x_[:, c, 0:CAP_L], c == 0, c == 15, [wg, x_], [pa])
                    for c in range(16):
                        mm(pu[:, :], wu[:, c, fsl], x_[:, c, 0:CAP_L], c == 0, c == 15, [wu, x_], [pu])
                    for c in range(16):
                        mm(psA2[:, 0, :], wg[:, c, fsl], x_[:, c, CAP_L:NSL], c == 0, c == 15, [wg, x_], [psA2])
                    for c in range(16):
                        mm(psA2[:, 1, :], wu[:, c, fsl], x_[:, c, CAP_L:NSL], c == 0, c == 15, [wu, x_], [psA2])
                    act(lambda: A.activation(s_[:, 0:CAP_L], pa[:, :], AF.Silu), [pa], [s_])
                    act(lambda: A.activation(s_[:, CAP_L:NSL], psA2[:, 0, :], AF.Silu), [psA2], [s_])
                    dve(lambda fc=fc: V.tensor_tensor(gT[:, fc, 0:CAP_L], s_[:, 0:CAP_L], pu[:, :], ALU.mult), [s_, pu], [gT])
                    dve(lambda fc=fc: V.tensor_tensor(gT[:, fc, CAP_L:NSL], s_[:, CAP_L:NSL], psA2[:, 1, :], ALU.mult), [s_, psA2], [gT])
                for stl in range(5):
                    s0 = stl * 128
                    m = 128 if stl < 4 else CAP_C
                    for nb in range(4):
                        ps, y_ = psY[ky % 2], yo[ky % 2]
                        ky += 1
                        for fc in range(8):
                            mm(ps[0:m, :], gT[:, fc, s0:s0 + m], wd[:, fc, nb * 512:(nb + 1) * 512], fc == 0, fc == 7, [gT, wd], [ps])
                        evac(y_[0:m, :], ps[0:m, :], [ps], [y_])
                        tk.dma("sp", y_d[e * NSL + s0:e * NSL + s0 + m, nb * 512:(nb + 1) * 512], y_[0:m, :], [y_], [y_d])
                if e + 1 < NE:
                    load_d(e + 1)
        tk.barrier()

    def stage_L(l, last):
        with ExitStack() as es:
            yb = tk.sb(es, "ybL", [128, NE, 5, 512], BF16)
            GT2 = [tk.sb(es, f"GT2{w}", [128, D], F32) for w in range(2)]
            psS = [tk.ps(es, f"psSL{i}", [128, 512]) for i in range(2)]
            psG = [tk.ps(es, f"psGL{i}", [128, 512]) for i in range(2)]
            psX = [tk.ps(es, f"psXL{i}", [128, 512]) for i in range(4)]
            eq = [tk.sb(es, f"eqL{i}", [128, 512], F32) for i in range(3)]
            ST = [tk.sb(es, f"STL{i}", [128, 512], BF16) for i in range(8)]
            xt = [tk.sb(es, f"xtL{i}", [128, 512], F32) for i in range(2)]
            tmp = [tk.sb(es, f"tmpL{i}", [128, 512], F32) for i in range(2)]
            eselHL = tk.sb(es, "eselHL", [48, NE, 128], BF16)
            eselG = tk.sb(es, "eselG", [16, NE, 128], BF16)
            shl = tk.sb(es, "shlL", [48, T], BF16)
            gtb = tk.sb(es, "gtbL", [16, T], BF16)
            iop1 = tk.sb(es, "iop1L", [128, 8], F32)
            tk.dma("pool", eselHL[:, :, :], t_esel[:, :].rearrange("k (e m) -> k e m", e=NE), [], [eselHL])
            dve(lambda: V.tensor_copy(eselG[:, :, :], identf[0:16, 0:16].unsqueeze(2).to_broadcast([16, 16, 128])), [identf], [eselG])
            dve(lambda: V.tensor_scalar(iop1[:, :], iotap[:, :], 1.0, None, ALU.add), [iotap], [iop1])
            tk.dma("sp", shl[:, :], shl_d[:, :], [shl_d], [shl])
            tk.dma("sp", gtb[:, :], gtb_d[:, :], [gtb_d], [gtb])
            for w in range(2):
                tk.dma("sp", GT2[w][:, :], modrow(l, w, 5), [mod_d], [GT2[w]])
            dve(lambda: V.memset(yb[:, :, :, :], 0.0), [], [yb])
            ks = 0
            kx = 0
            for db in range(4):
                dcs = slice(db * 512, (db + 1) * 512)
                for e in range(NE):
                    tk.dma("sp", yb[:, e, 0:4, :], y_d[e * NSL:e * NSL + CAP_L, dcs].rearrange("(c p) d -> p c d", p=128), [y_d], [yb])
                    tk.dma("sp", yb[0:CAP_C, e, 4, :], y_d[e * NSL + CAP_L:(e + 1) * NSL, dcs], [y_d], [yb])
                blocks = [(0, 256, True)] + [(256 + qb * 512, 512, False) for qb in range(8)]
                for (c0, ntok, isctx) in blocks:
                    if last and isctx:
                        continue
                    nsub = ntok // 128
                    chunks = [4] if isctx else [0, 1, 2, 3]
                    np_ = CAP_C if isctx else 128

                    def bc(e):
                        mm(psS[e % 2][:, 0:ntok], eselHL[:, e, :], shl[:, c0:c0 + ntok], True, True, [eselHL, shl], [psS[e % 2]])
                        mm(psG[e % 2][:, 0:ntok], eselG[:, e, :], gtb[:, c0:c0 + ntok], True, True, [eselG, gtb], [psG[e % 2]])
                    bc(0)
                    for e in range(NE):
                        if e + 1 < NE:
                            bc(e + 1)
                        pS, pG = psS[e % 2], psG[e % 2]
                        gq_ = eq[e % 3]
                        act(lambda: A.copy(gq_[0:np_, 0:ntok], pG[0:np_, 0:ntok]), [pG], [gq_])
                        sts = []
                        for ci, c in enumerate(chunks):
                            s_ = ST[ks % 8]
                            ks += 1
                            dve(lambda s_=s_, c=c: V.scalar_tensor_tensor(
                                s_[0:np_, 0:ntok], pS[0:np_, 0:ntok], iop1[0:np_, c:c + 1], gq_[0:np_, 0:ntok], ALU.is_equal, ALU.mult),
                                [pS, iop1, gq_], [s_])
                            sts.append(s_)
                        for ci, c in enumerate(chunks):
                            s_ = sts[ci]
                            first = (e == 0 and ci == 0)
                            lastm = (e == NE - 1 and ci == len(chunks) - 1)
                            for m in range(nsub):
                                mm(psX[m][:, :], s_[0:np_, m * 128:(m + 1) * 128], yb[0:np_, e, c, :], first, lastm, [s_, yb], [psX[m]])
                    w = 1 if isctx else 0
                    for m in range(nsub):
                        x_, t_ = xt[kx % 2], tmp[kx % 2]
                        kx += 1
                        rows = slice(c0 + m * 128, c0 + (m + 1) * 128)
                        tk.dma("sp", x_[:, :], xcur[rows, dcs], [xcur], [x_])
                        dve(lambda m=m, t_=t_, w=w: V.tensor_tensor(t_[:, :], psX[m][:, :], GT2[w][:, dcs], ALU.mult), [psX[m], GT2[w]], [t_])
                        dve(lambda x_=x_, t_=t_: V.tensor_tensor(x_[:, :], x_[:, :], t_[:, :], ALU.add), [x_, t_], [x_])
                        if last:
                            tk.dma("sp", out_d[c0 - LC + m * 128:c0 - LC + (m + 1) * 128, dcs], x_[:, :], [x_], [out_d])
                        else:
                            tk.dma("sp", xcur[rows, dcs], x_[:, :], [x_], [xcur])
        tk.barrier()

    tk.marks = []

    def mark(name):
        tk.marks.append((name, dict(tk.ccnt)))
    stage_adaln()
    mark("ada")
    for l in range(depth):
        stage_A(l)
        mark(f"A{l}")
        stage_B(l)
        mark(f"B{l}")
        stage_C(l)
        mark(f"C{l}")
        stage_D(l)
        mark(f"D{l}")
        stage_E(l)
        mark(f"E{l}")
        stage_F(l)
        mark(f"F{l}")
        stage_G(l)
        mark(f"G{l}")
        with ExitStack() as esHI:
            affT = tk.sb(esHI, "affT", [16, T], F32)
            stage_H(l, affT)
            mark(f"H{l}")
            stage_I(l, affT)
            mark(f"I{l}")
        stage_J(l)
        mark(f"J{l}")
        stage_K(l)
        mark(f"K{l}")
        stage_L(l, l == depth - 1)
        mark(f"L{l}")
    tk.barrier()
    top.close()
    return nc, tk


def _tables():
    f32 = np.float32
    bf = ml_dtypes.bfloat16
    tb = {}
    tb["t_ident"] = np.eye(128, dtype=f32)
    tb["t_ustr"] = np.triu(np.ones((128, 128), f32), 1)
    es_ = np.zeros((48, 16, 128), f32)
    for e in range(16):
        es_[e, e, :] = 1.0
        es_[16 + e, e, :] = 1.0
        es_[32 + e, e, :] = 1.0
    tb["t_esel"] = es_.reshape(48, 16 * 128)

    def rope_tab(pos, half):
        freqs = (f32(10000.0) ** (-np.arange(half, dtype=f32) / f32(half))).astype(f32)
        ang = (pos.astype(f32)[:, None] * freqs[None, :]).astype(f32)
        return np.cos(ang).astype(f32), np.sin(ang).astype(f32)

    rows = np.repeat(np.arange(S // 64), 64)
    cols = np.tile(np.arange(64), S // 64)
    cr, sr = rope_tab(rows, 32)
    cc_, sc_ = rope_tab(cols, 32)
    cosA = np.concatenate([cr, cr, cc_, cc_], axis=1)
    sinA = np.concatenate([-sr, sr, -sc_, sc_], axis=1)
    A_ = np.zeros((T, 256), f32)
    A_[:LC, :128] = 1.0
    A_[LC:, :128] = cosA
    A_[LC:, 128:] = sinA
    tb["t_ropeA"] = A_
    posF = np.arange(T)
    posB = np.concatenate([LC - 1 - np.arange(LC), LC + (S - 1 - np.arange(S))])
    for nm, pos in (("t_ropeF", posF), ("t_ropeB", posB)):
        c_, s_ = rope_tab(pos, 64)
        tb[nm] = np.concatenate([c_, c_, -s_, s_], axis=1).astype(f32)

    def dft(n):
        idx = np.arange(n, dtype=np.int64)
        m = (idx[:, None] * idx[None, :]) % n
        ang = 2.0 * np.pi * m.astype(np.float64) / n
        return np.cos(ang), np.sin(ang)
    c, s = dft(S)
    tb["t_CL"] = (c / np.sqrt(S)).astype(bf)
    tb["t_SL"] = (s / np.sqrt(S)).astype(bf)
    c, s = dft(LC)
    tb["t_CL2"] = (c / np.sqrt(LC)).astype(bf)
    tb["t_SL2"] = (s / np.sqrt(LC)).astype(bf)
    c, s = dft(128)
    tb["t_CC"] = np.concatenate([c / np.sqrt(128.0), -s / np.sqrt(128.0)], axis=1).astype(bf)
    j = np.arange(128, dtype=f32)[:, None]
    i = np.arange(128, dtype=f32)[None, :]
    Df = np.maximum(i - j, 0.0)
    Mf = (i >= j).astype(f32)
    Db = np.maximum(j - i, 0.0)
    Mb = (j >= i).astype(f32)
    rowf = np.broadcast_to(i + 1.0, (128, 128))
    rowb = np.broadcast_to(128.0 - i, (128, 128))
    colf = 127.0 - j
    colb = j + 0.0
    tb["t_ret"] = np.ascontiguousarray(np.concatenate([Df, Mf, Db, Mb, rowf, rowb, colf, colb], axis=1).astype(f32))
    return tb


_CACHE = {}


def kernel(x, c, ctx, c_ctx, w_ada, b_ada, norm_mix, norm_ffn, w_in, q_norm, k_norm,
           ret_log_decay, w_out, w_router, w_gate, w_up, w_down):
    depth = int(os.environ.get("MK_DEPTH", w_ada.shape[0]))
    ncores = int(os.environ.get("MK_CORES", x.shape[0]))
    f = np.ascontiguousarray
    tb = _tables()
    shared = {
        "w_ada": f(w_ada[:depth].reshape(depth * D, 6 * D)),
        "b_ada": f(b_ada[:depth]),
        "norm_mix": f(norm_mix[:depth]),
        "norm_ffn": f(norm_ffn[:depth]),
        "w_in": f(w_in[:depth].reshape(depth * D, INW)),
        "q_norm": f(q_norm[:depth]),
        "k_norm": f(k_norm[:depth]),
        "rld": f(ret_log_decay[:depth].reshape(depth, 8)),
        "w_out": f(w_out[:depth].reshape(depth * D, D)),
        "w_router": f(w_router[:depth].reshape(depth * D, NE)),
        "w_gate": f(w_gate[:depth].reshape(depth * NE * D, FF)),
        "w_up": f(w_up[:depth].reshape(depth * NE * D, FF)),
        "w_down": f(w_down[:depth].reshape(depth * NE * FF, D)),
    }
    shared.update(tb)
    in_maps = []
    for b in range(ncores):
        m = dict(shared)
        m["x"] = f(x[b])
        m["ctx"] = f(ctx[b])
        m["cvec"] = f(np.stack([c[b], c_ctx], axis=0))
        in_maps.append(m)
    nc, tk = build(depth)
    res = run_bass_kernel_spmd(nc, in_maps, core_ids=list(range(ncores)))
    out = np.stack([np.asarray(res.results[b]["out"]) for b in range(ncores)], axis=0)
    return out.astype(np.float32)
```

```python
import os
import numpy as np
import ml_dtypes
from contextlib import ExitStack
import concourse.bass as bass
import concourse.mybir as mybir
from concourse.bass_utils import run_bass_kernel_spmd

F32 = mybir.dt.float32
BF16 = mybir.dt.bfloat16
ALU = mybir.AluOpType
AF = mybir.ActivationFunctionType
AX = mybir.AxisListType

D = 2048
S = 4096
LC = 256
T = S + LC
NT = T // 128
NE = 16
FF = 1024
INW = 4608
CAP_L = 512
CAP_C = 32
NSL = CAP_L + CAP_C
EPS = 1e-6


class Buf:
    __slots__ = ("t", "w", "r", "name")

    def __init__(self, t, name=""):
        self.t = t
        self.w = None
        self.r = {}
        self.name = name

    def __getitem__(self, idx):
        return self.t[idx]


class TK:
    NDMA = 8

    def __init__(self, nc, es):
        self.nc = nc
        self.es = es
        self.eng = {"pe": nc.tensor, "act": nc.scalar, "dve": nc.vector, "pool": nc.gpsimd, "sp": nc.sync}
        self.csem = {k: es.enter_context(nc.semaphore("s_" + k)) for k in ("pe", "act", "dve", "pool")}
        self.ccnt = {k: 0 for k in self.csem}
        self.dsem = {q: [es.enter_context(nc.semaphore(f"d_{q}{i}")) for i in range(self.NDMA)] for q in ("sp", "pool")}
        self.dcnt = {q: [0] * self.NDMA for q in self.dsem}
        self.dnext = {q: 0 for q in self.dsem}
        self.waited = {}
        self.n_inst = 0
        self.uid = 0

    def sb(self, es, name, shape, dt):
        self.uid += 1
        return Buf(es.enter_context(self.nc.sbuf_tensor(f"{name}_{self.uid}", shape, dt)), name)

    def ps(self, es, name, shape, dt=F32):
        self.uid += 1
        return Buf(es.enter_context(self.nc.psum_tensor(f"{name}_{self.uid}", shape, dt)), name)

    def dram(self, name, shape, dt, kind="Internal"):
        return Buf(self.nc.dram_tensor(name, shape, dt, kind=kind), name)

    def _wait(self, e, tok):
        sem, val = tok[0], tok[1]
        key = (e, id(sem))
        if self.waited.get(key, 0) >= val:
            return
        self.waited[key] = val
        self.eng[e].wait_ge(sem, val)

    def _deps(self, e, reads, writes, skip_same):
        for b in list(reads) + list(writes):
            if b.w is not None and not (skip_same and b.w[2] == e):
                self._wait(e, b.w)
        for b in writes:
            for tok in b.r.values():
                if not (skip_same and tok[2] == e):
                    self._wait(e, tok)

    def _commit(self, tok, reads, writes):
        for b in writes:
            b.w = tok
            b.r = {}
        for b in reads:
            if b in writes:
                continue
            b.r[id(tok[0])] = tok

    def op(self, e, fn, reads=(), writes=(), skip_same=False):
        self._deps(e, reads, writes, skip_same)
        ins = fn()
        self.ccnt[e] += 1
        ins.then_inc(self.csem[e], 1)
        tok = (self.csem[e], self.ccnt[e], e)
        self._commit(tok, reads, writes)
        self.n_inst += 1
        return tok

    def dma(self, q, out, in_, reads=(), writes=(), **kw):
        i = self.dnext[q]
        self.dnext[q] = (i + 1) % self.NDMA
        sem = self.dsem[q][i]
        if self.dcnt[q][i] > 0:
            self._wait(q, (sem, 16 * self.dcnt[q][i]))
        self._deps(q, reads, writes, False)
        ins = self.eng[q].dma_start(out=out, in_=in_, **kw)
        self.dcnt[q][i] += 1
        ins.then_inc(sem, 16)
        tok = (sem, 16 * self.dcnt[q][i], "dma_" + q)
        self._commit(tok, reads, writes)
        self.n_inst += 1
        return tok

    def barrier(self):
        for e in ("pe", "act", "dve", "pool", "sp"):
            for k in self.csem:
                if self.ccnt[k] > 0:
                    self._wait(e, (self.csem[k], self.ccnt[k]))
            for q in self.dsem:
                for i in range(self.NDMA):
                    if self.dcnt[q][i] > 0:
                        self._wait(e, (self.dsem[q][i], 16 * self.dcnt[q][i]))


def build(depth):
    nc = bass.Bass("TRN2", target_bir_lowering=False)
    top = ExitStack()
    tk = TK(nc, top)
    V, A, PE = nc.vector, nc.scalar, nc.tensor

    def dve(fn, r, w):
        return tk.op("dve", fn, r, w)

    def act(fn, r, w):
        return tk.op("act", fn, r, w)

    def mm(ps_ap, lhsT, rhs, start, stop, r, w):
        return tk.op("pe", lambda: PE.matmul(ps_ap, lhsT, rhs, start=start, stop=stop), r, w, skip_same=True)

    flip = [0]

    def evac(out_ap, in_ap, r, w):
        flip[0] ^= 1
        if flip[0]:
            return dve(lambda: V.tensor_copy(out_ap, in_ap), r, w)
        return act(lambda: A.copy(out_ap, in_ap), r, w)

    EI = "ExternalInput"
    x_in = tk.dram("x", [S, D], F32, EI)
    ctx_in = tk.dram("ctx", [LC, D], F32, EI)
    cvec = tk.dram("cvec", [2, D], F32, EI)
    w_ada = tk.dram("w_ada", [depth * D, 6 * D], F32, EI)
    b_ada = tk.dram("b_ada", [depth, 6 * D], F32, EI)
    norm_mix = tk.dram("norm_mix", [depth, D], F32, EI)
    norm_ffn = tk.dram("norm_ffn", [depth, D], F32, EI)
    w_in = tk.dram("w_in", [depth * D, INW], F32, EI)
    q_norm = tk.dram("q_norm", [depth, 128], F32, EI)
    k_norm = tk.dram("k_norm", [depth, 128], F32, EI)
    rld = tk.dram("rld", [depth, 8], F32, EI)
    w_out = tk.dram("w_out", [depth * D, D], F32, EI)
    w_router = tk.dram("w_router", [depth * D, NE], F32, EI)
    w_gate = tk.dram("w_gate", [depth * NE * D, FF], F32, EI)
    w_up = tk.dram("w_up", [depth * NE * D, FF], F32, EI)
    w_down = tk.dram("w_down", [depth * NE * FF, D], F32, EI)
    t_ident = tk.dram("t_ident", [128, 128], F32, EI)
    t_ropeA = tk.dram("t_ropeA", [T, 256], F32, EI)
    t_ropeF = tk.dram("t_ropeF", [T, 256], F32, EI)
    t_ropeB = tk.dram("t_ropeB", [T, 256], F32, EI)
    t_CL = tk.dram("t_CL", [S, S], BF16, EI)
    t_SL = tk.dram("t_SL", [S, S], BF16, EI)
    t_CL2 = tk.dram("t_CL2", [LC, LC], BF16, EI)
    t_SL2 = tk.dram("t_SL2", [LC, LC], BF16, EI)
    t_CC = tk.dram("t_CC", [128, 256], BF16, EI)
    t_ret = tk.dram("t_ret", [128, 6 * 128 + 2], F32, EI)
    t_ustr = tk.dram("t_ustr", [128, 128], F32, EI)
    out_d = tk.dram("out", [S, D], F32, "ExternalOutput")

    xcur = tk.dram("xcur", [T, D], F32)
    mod_d = tk.dram("mod_d", [depth * 12 * 128, D], F32)
    hT_d = tk.dram("hT_d", [T, D], BF16)
    p_d = tk.dram("p_d", [T, INW], F32)
    qT_d = tk.dram("qT_d", [1024, T], BF16)
    kT_d = tk.dram("kT_d", [256, T], BF16)
    v_d = tk.dram("v_d", [T, 256], BF16)
    f_d = tk.dram("f_d", [T, 512], BF16)
    rT_d = tk.dram("rT_d", [4 * 512, T], BF16)
    rk_d = tk.dram("rk_d", [T, 1024], BF16)
    rv_d = tk.dram("rv_d", [T, 512], BF16)
    g_d = tk.dram("g_d", [T, 1024], F32)
    ro_d = tk.dram("ro_d", [T, 1024], F32)
    mixT_d = tk.dram("mixT_d", [D, T], BF16)
    h2_d = tk.dram("h2_d", [T, D], BF16)
    xg_d = tk.dram("xg_d", [NE * D, NSL], BF16)
    y_d = tk.dram("y_d", [NE * NSL, D], BF16)

    identf = tk.sb(top, "identf", [128, 128], F32)
    identb = tk.sb(top, "identb", [128, 128], BF16)
    onesf = tk.sb(top, "onesf", [128, 128], F32)
    onesb = tk.sb(top, "onesb", [128, 128], BF16)
    ustrb = tk.sb(top, "ustrb", [128, 128], BF16)
    iota5 = tk.sb(top, "iota5", [128, 512], F32)
    iotap = tk.sb(top, "iotap", [128, 8], F32)
    tk.dma("sp", identf[:, :], t_ident[:, :], [t_ident], [identf])
    tk.dma("pool", identb[:, :], t_ident[:, :], [t_ident], [identb])
    tk.dma("pool", ustrb[:, :], t_ustr[:, :], [t_ustr], [ustrb])
    tk.op("pool", lambda: nc.gpsimd.memset(onesf[:, :], 1.0), [], [onesf])
    tk.op("pool", lambda: nc.gpsimd.memset(onesb[:, :], 1.0), [], [onesb])
    tk.op("pool", lambda: nc.gpsimd.iota(iota5[:, :], [[1, 512]], base=0, channel_multiplier=0,
                                          allow_small_or_imprecise_dtypes=True), [], [iota5])
    tk.op("pool", lambda: nc.gpsimd.iota(iotap[:, :], [[128, 8]], base=0, channel_multiplier=1,
                                          allow_small_or_imprecise_dtypes=True), [], [iotap])
    aff_all = tk.sb(top, "aff_all", [128, NT, 16], F32)
    sidx_all = tk.sb(top, "sidx_all", [128, NT, 16], F32)
    gate_all = tk.sb(top, "gate_all", [128, NT, 16], F32)
    shl_d = tk.dram("shl_d", [48, T], BF16)
    gtb_d = tk.dram("gtb_d", [16, T], BF16)
    t_esel = tk.dram("t_esel", [48, 16 * 128], F32, "ExternalInput")

    def modrow(l, w, i):
        r0 = ((l * 12) + w * 6 + i) * 128
        return mod_d[r0:r0 + 128, :]

    def stage_adaln():
        with ExitStack() as es:
            condT = tk.sb(es, "condT", [128, 2, 16], F32)
            scb = [tk.sb(es, f"scb{w}", [128, 16, 128], F32) for w in range(2)]
            wts = [tk.sb(es, f"wada{i}", [128, 16, 512], F32) for i in range(2)]
            brow = [tk.sb(es, f"brow{i}", [1, 512], F32) for i in range(2)]
            pss = [tk.ps(es, f"psada{i}", [128, 512]) for i in range(2)]
            ot = [tk.sb(es, f"oada{i}", [128, 512], F32) for i in range(2)]
            tk.dma("sp", condT[:, :, :], cvec[:, :].rearrange("w (c p) -> p w c", p=128), [cvec], [condT],
                   allow_slow_non_contiguous=True)
            act(lambda: A.activation(condT[:, :, :], condT[:, :, :], AF.Silu), [condT], [condT])
            for w in range(2):
                for c in range(16):
                    dve(lambda w=w, c=c: V.tensor_scalar(scb[w][:, c, :], onesf[:, :], condT[:, w, c:c + 1], None, ALU.mult),
                        [onesf, condT], [scb[w]])
            k = 0
            for l in range(depth):
                for nb in range(24):
                    wt = wts[nb % 2]
                    br = brow[nb % 2]
                    tk.dma("sp", wt[:, :, :], w_ada[l * D:(l + 1) * D, nb * 512:(nb + 1) * 512].rearrange("(c p) n -> p c n", p=128),
                           [w_ada], [wt])
                    tk.dma("sp", br[:, :], b_ada[l:l + 1, nb * 512:(nb + 1) * 512], [b_ada], [br])
                    piece = nb // 4
                    for w in range(2):
                        ps = pss[k % 2]
                        o = ot[k % 2]
                        k += 1
                        for c in range(16):
                            mm(ps[:, :], scb[w][:, c, :], wt[:, c, :], c == 0, False, [scb[w], wt], [ps])
                        mm(ps[:, :], onesf[0:1, :], br[0:1, :], False, True, [onesf, br], [ps])
                        if piece in (1, 4):
                            dve(lambda ps=ps, o=o: V.tensor_scalar(o[:, :], ps[:, :], 1.0, None, ALU.add), [ps], [o])
                        else:
                            evac(o[:, :], ps[:, :], [ps], [o])
                        tk.dma("sp", modrow(l, w, piece)[:, (nb % 4) * 512:(nb % 4 + 1) * 512], o[:, :], [o], [mod_d])
        tk.barrier()

    def rms_rstd(es_bufs, xt, width):
        junk, st = es_bufs
        act(lambda: A.activation(junk[:, 0:width], xt[:, 0:width], AF.Square, accum_out=st[:, 0:1]), [xt], [junk, st])
        dve(lambda: V.tensor_scalar(st[:, 1:2], st[:, 0:1], 1.0 / width, EPS, ALU.mult, ALU.add), [st], [st])
        act(lambda: A.activation(st[:, 2:3], st[:, 1:2], AF.Sqrt), [st], [st])
        dve(lambda: V.reciprocal(st[:, 3:4], st[:, 2:3]), [st], [st])
        return st

    def stage_A(l):
        with ExitStack() as es:
            gm = tk.sb(es, "gm", [128, D], F32)
            G = [tk.sb(es, f"G{w}", [128, D], F32) for w in range(2)]
            SH = [tk.sb(es, f"SH{w}", [128, D], F32) for w in range(2)]
            xts = [tk.sb(es, f"xt{i}", [128, D], F32) for i in range(2)]
            junk = tk.sb(es, "junk", [128, D], BF16)
            tmp = tk.sb(es, "tmpA", [128, D], F32)
            hb = [tk.sb(es, f"hb{i}", [128, D], BF16) for i in range(2)]
            hTt = [tk.sb(es, f"hTt{i}", [128, 16, 128], BF16) for i in range(2)]
            sts = [tk.sb(es, f"stA{i}", [128, 4], F32) for i in range(2)]
            pst = [tk.ps(es, f"pstA{i}", [128, 512]) for i in range(4)]
            tk.dma("sp", gm[:, :], norm_mix[l, :].partition_broadcast(128), [norm_mix], [gm])
            for w in range(2):
                tk.dma("sp", G[w][:, :], modrow(l, w, 1), [mod_d], [G[w]])
                tk.dma("sp", SH[w][:, :], modrow(l, w, 0), [mod_d], [SH[w]])
                dve(lambda w=w: V.tensor_tensor(G[w][:, :], G[w][:, :], gm[:, :], ALU.mult), [G[w], gm], [G[w]])
            for t in range(NT):
                w = 1 if t < 2 else 0
                xt = xts[t % 2]
                if l == 0:
                    src = ctx_in[t * 128:(t + 1) * 128, :] if t < 2 else x_in[(t - 2) * 128:(t - 1) * 128, :]
                    tk.dma("sp", xt[:, :], src, [], [xt])
                    tk.dma("sp", xcur[t * 128:(t + 1) * 128, :], xt[:, :], [xt], [xcur])
                else:
                    tk.dma("sp", xt[:, :], xcur[t * 128:(t + 1) * 128, :], [xcur], [xt])
                st = rms_rstd((junk, sts[t % 2]), xt, D)
                h = hb[t % 2]
                dve(lambda xt=xt, st=st, w=w: V.scalar_tensor_tensor(tmp[:, :], xt[:, :], st[:, 3:4], G[w][:, :], ALU.mult, ALU.mult),
                    [xt, st, G[w]], [tmp])
                dve(lambda h=h, w=w: V.tensor_tensor(h[:, :], tmp[:, :], SH[w][:, :], ALU.add), [tmp, SH[w]], [h])
                ht = hTt[t % 2]
                for g4 in range(4):
                    ps = pst[g4]
                    for j in range(4):
                        c = g4 * 4 + j
                        mm(ps[:, j * 128:(j + 1) * 128], h[:, c * 128:(c + 1) * 128], identb[:, :], True, True, [h, identb], [ps])
                    evac(ht[:, g4 * 4:(g4 + 1) * 4, :], ps[:, :].rearrange("p (j n) -> p j n", j=4), [ps], [ht])
                tk.dma("sp", hT_d[t * 128:(t + 1) * 128, :], ht[:, :, :].rearrange("p c n -> p (c n)"), [ht], [hT_d])
        tk.barrier()

    def stage_B(l):
        with ExitStack() as es:
            hTall = tk.sb(es, "hTall", [128, NT, D], BF16)
            hts = [Buf(hTall.t, f"hTall{t}") for t in range(NT)]
            wg = [tk.sb(es, f"wgB{i}", [128, 16, 512], BF16) for i in range(2)]
            pss = [tk.ps(es, f"psB{i}", [128, 512]) for i in range(4)]
            ot = [tk.sb(es, f"oB{i}", [128, 512], F32) for i in range(4)]
            for t in range(NT):
                tk.dma("sp", hTall[:, t, :], hT_d[t * 128:(t + 1) * 128, :], [hT_d], [hts[t]])
            k = 0
            for g in range(9):
                w = wg[g % 2]
                tk.dma("pool", w[:, :, :], w_in[l * D:(l + 1) * D, g * 512:(g + 1) * 512].rearrange("(c p) n -> p c n", p=128),
                       [w_in], [w])
                for t in range(NT):
                    ps = pss[k % 4]
                    o = ot[k % 4]
                    k += 1
                    for c in range(16):
                        mm(ps[:, :], hTall[:, t, c * 128:(c + 1) * 128], w[:, c, :], c == 0, c == 15, [hts[t], w], [ps])
                    evac(o[:, :], ps[:, :], [ps], [o])
                    tk.dma("sp", p_d[t * 128:(t + 1) * 128, g * 512:(g + 1) * 512], o[:, :], [o], [p_d])
        tk.barrier()

    def stage_C(l):
        with ExitStack() as es:
            gqk = tk.sb(es, "gqk", [128, 10, 128], F32)
            pts = [tk.sb(es, f"ptC{i}", [128, INW], F32) for i in range(2)]
            ropes = [tk.sb(es, f"ropeC{i}", [128, 3, 256], F32) for i in range(2)]
            junks = [tk.sb(es, f"junkC{i}", [128, 1280], F32) for i in range(2)]
            sts = [tk.sb(es, f"stC{i}", [128, 40], F32) for i in range(2)]
            qn = tk.sb(es, "qn", [128, 10, 128], F32)
            t1 = tk.sb(es, "t1C", [128, 10, 128], F32)
            t2 = tk.sb(es, "t2C", [128, 10, 128], F32)
            qrs = [tk.sb(es, f"qr{i}", [128, 1280], BF16) for i in range(2)]
            rrs = [tk.sb(es, f"rr{i}", [128, 8, 128], F32) for i in range(2)]
            rfbs = [[tk.sb(es, f"rfb{i}_{j}", [128, 1024], BF16) for i in range(2)] for j in range(2)]
            gss = [tk.sb(es, f"gsC{i}", [128, 1024], F32) for i in range(2)]
            trs = [tk.sb(es, f"trC{i}", [128, 512], BF16) for i in range(2)]
            pst = [tk.ps(es, f"pstC{i}", [128, 512]) for i in range(2)]
            for h in range(10):
                src = q_norm if h < 8 else k_norm
                tk.dma("sp", gqk[:, h, :], src[l, :].partition_broadcast(128), [src], [gqk])
            kk = [0]

            def transpose_out(srcbuf, col0, nblk, dst, row0, t):
                ps = pst[kk[0] % 2]
                tr = trs[kk[0] % 2]
                kk[0] += 1
                for j in range(nblk):
                    mm(ps[:, j * 128:(j + 1) * 128], srcbuf[:, col0 + j * 128:col0 + (j + 1) * 128], identb[:, :], True, True,
                       [srcbuf, identb], [ps])
                evac(tr[:, 0:nblk * 128], ps[:, 0:nblk * 128], [ps], [tr])
                tk.dma("sp", dst[row0:row0 + nblk * 128, t * 128:(t + 1) * 128].rearrange("(j p) n -> p j n", p=128),
                       tr[:, 0:nblk * 128].rearrange("p (j n) -> p j n", j=nblk), [tr], [dst])

            def rope(src3, nh, half, cs, out3, rb, wb):
                nb = 128 // (2 * half)
                cosb = cs[:, 0:128].unsqueeze(1).to_broadcast([128, nh, 128])
                dve(lambda: V.tensor_tensor(t1[:, 0:nh, :], src3, cosb, ALU.mult), rb, [t1])
                sv = src3.rearrange("p h (b s w) -> p h b s w", b=nb, s=2)
                tv = t2[:, 0:nh, :].rearrange("p h (b s w) -> p h b s w", b=nb, s=2)
                sn = cs[:, 128:256].rearrange("p (b s w) -> p b s w", b=nb, s=2)
                for b in range(nb):
                    for s_ in range(2):
                        sb_ = sn[:, b, s_, :].unsqueeze(1).to_broadcast([128, nh, half])
                        dve(lambda b=b, s_=s_, sb_=sb_: V.tensor_tensor(tv[:, :, b, s_, :], sv[:, :, b, 1 - s_, :], sb_, ALU.mult),
                            rb, [t2])
                dve(lambda: V.tensor_tensor(out3, t1[:, 0:nh, :], t2[:, 0:nh, :], ALU.add), [t1, t2], [wb])

            for t in range(NT):
                pt = pts[t % 2]
                rp = ropes[t % 2]
                junk, st, qr, rr, rfb, gs = junks[t % 2], sts[t % 2], qrs[t % 2], rrs[t % 2], rfbs[t % 2], gss[t % 2]
                tk.dma("sp", pt[:, :], p_d[t * 128:(t + 1) * 128, :], [p_d], [pt])
                tk.dma("sp", rp[:, 0, :], t_ropeA[t * 128:(t + 1) * 128, :], [], [rp])
                tk.dma("sp", rp[:, 1, :], t_ropeF[t * 128:(t + 1) * 128, :], [], [rp])
                tk.dma("sp", rp[:, 2, :], t_ropeB[t * 128:(t + 1) * 128, :], [], [rp])
                qk3 = pt[:, 0:1280].rearrange("p (h d) -> p h d", h=10)
                act(lambda: A.activation(junk[:, :], pt[:, 0:1280], AF.Square), [pt], [junk])
                dve(lambda: V.tensor_reduce(st[:, 0:10], junk[:, :].rearrange("p (h d) -> p h d", h=10), AX.X, ALU.add), [junk], [st])
                dve(lambda: V.tensor_scalar(st[:, 10:20], st[:, 0:10], 1.0 / 128, EPS, ALU.mult, ALU.add), [st], [st])
                act(lambda: A.activation(st[:, 20:30], st[:, 10:20], AF.Sqrt), [st], [st])
                dve(lambda: V.reciprocal(st[:, 30:40], st[:, 20:30]), [st], [st])
                dve(lambda: V.tensor_tensor(qn[:, :, :], qk3, st[:, 30:40].unsqueeze(2).to_broadcast([128, 10, 128]), ALU.mult),
                    [pt, st], [qn])
                dve(lambda: V.tensor_tensor(qn[:, :, :], qn[:, :, :], gqk[:, :, :], ALU.mult), [qn, gqk], [qn])
                rope(qn[:, :, :], 10, 32, rp[:, 0, :], qr[:, :].rearrange("p (h d) -> p h d", h=10), [qn, rp], qr)
                transpose_out(qr, 0, 4, qT_d, 0, t)
                transpose_out(qr, 512, 4, qT_d, 512, t)
                transpose_out(qr, 1024, 2, kT_d, 0, t)
                tk.dma("pool", v_d[t * 128:(t + 1) * 128, :], pt[:, 1280:1536], [pt], [v_d])
                tk.dma("pool", f_d[t * 128:(t + 1) * 128, :], pt[:, 1536:2048], [pt], [f_d])
                tk.dma("pool", rv_d[t * 128:(t + 1) * 128, :], pt[:, 3072:3584], [pt], [rv_d])
                dve(lambda: V.tensor_copy(rr[:, 0:4, :], pt[:, 2048:2560].rearrange("p (h d) -> p h d", h=4)), [pt], [rr])
                dve(lambda: V.tensor_scalar(rr[:, 4:8, :], pt[:, 2560:3072].rearrange("p (h d) -> p h d", h=4), 128.0 ** -0.5, None, ALU.mult),
                    [pt], [rr])
                for di in range(2):
                    rope(rr[:, :, :], 8, 64, rp[:, 1 + di, :], rfb[di][:, :].rearrange("p (h d) -> p h d", h=8), [rr, rp], rfb[di])
                    transpose_out(rfb[di], 0, 4, rT_d, di * 1024, t)
                    transpose_out(rfb[di], 512, 4, rT_d, di * 1024 + 512, t)
                    tk.dma("sp", rk_d[t * 128:(t + 1) * 128, di * 512:(di + 1) * 512], rfb[di][:, 512:1024], [rfb[di]], [rk_d])
                act(lambda: A.activation(gs[:, :], pt[:, 3584:4608], AF.Silu), [pt], [gs])
                tk.dma("sp", g_d[t * 128:(t + 1) * 128, :], gs[:, :], [gs], [g_d])
        tk.barrier()

    def stage_D(l):
        with ExitStack() as es:
            gq = tk.sb(es, "gqD", [128, 256], F32)
            stb = tk.sb(es, "stD", [128, 4], F32)
            kT = tk.sb(es, "kTD", [128, T], BF16)
            vv = tk.sb(es, "vD", [128, NT, 128], BF16)
            qTb = [[tk.sb(es, f"qTb{s_}_{i}", [128, 512], BF16) for i in range(2)] for s_ in range(2)]
            pT = [[tk.sb(es, f"pT{s_}_{i}", [128, 512], BF16) for i in range(2)] for s_ in range(2)]
            psS = [[tk.ps(es, f"psS{s_}_{i}", [128, 512]) for i in range(2)] for s_ in range(2)]
            psO = [tk.ps(es, f"psO{s_}", [128, 512]) for s_ in range(2)]
            psZ = [tk.ps(es, f"psZ{s_}", [128, 512]) for s_ in range(2)]
            accZ = [tk.sb(es, f"accZ{s_}", [128, 512], F32) for s_ in range(2)]
            rz = [tk.sb(es, f"rzD{s_}", [128, 512], F32) for s_ in range(2)]
            ob = [tk.sb(es, f"obD{s_}", [128, 512], BF16) for s_ in range(2)]
            tk.dma("sp", gq[:, 0:128], q_norm[l, :].partition_broadcast(128), [q_norm], [gq])
            tk.dma("sp", gq[:, 128:256], k_norm[l, :].partition_broadcast(128), [k_norm], [gq])
            dve(lambda: V.tensor_reduce(stb[:, 0:2], gq[:, :].rearrange("p (a d) -> p a d", a=2), AX.X, ALU.max, apply_absolute_value=True),
                [gq], [stb])
            dve(lambda: V.tensor_tensor(stb[:, 2:3], stb[:, 0:1], stb[:, 1:2], ALU.mult), [stb], [stb])
            dve(lambda: V.tensor_scalar(stb[:, 3:4], stb[:, 2:3], -(128.0 ** 0.5), None, ALU.mult), [stb], [stb])
            scale = 128.0 ** -0.5
            kq = 0
            for kvh in range(2):
                tk.dma("sp", kT[:, :], kT_d[kvh * 128:(kvh + 1) * 128, :], [kT_d], [kT])
                tk.dma("sp", vv[:, :, :], v_d[:, kvh * 128:(kvh + 1) * 128].rearrange("(c p) d -> p c d", p=128), [v_d], [vv])
                for hp in range(2):
                    heads = [kvh * 4 + 2 * hp, kvh * 4 + 2 * hp + 1]
                    blocks = [(0, 256, [0, 1])] + [(256 + qb * 512, 512, list(range(NT))) for qb in range(8)]
                    for (c0, nq, chunks) in blocks:
                        qbs = [qTb[s_][kq % 2] for s_ in range(2)]
                        kq += 1
                        for s_ in range(2):
                            hq = heads[s_]
                            tk.dma("sp", qbs[s_][:, 0:nq], qT_d[hq * 128:(hq + 1) * 128, c0:c0 + nq], [qT_d], [qbs[s_]])
                        nch = len(chunks)

                        def s_mm(s_, i):
                            kc = chunks[i]
                            mm(psS[s_][i % 2][:, 0:nq], kT[:, kc * 128:(kc + 1) * 128], qbs[s_][:, 0:nq], True, True,
                               [kT, qbs[s_]], [psS[s_][i % 2]])
                        s_mm(0, 0)
                        s_mm(1, 0)
                        for i in range(nch):
                            kc = chunks[i]
                            if i + 1 < nch:
                                s_mm(0, i + 1)
                                s_mm(1, i + 1)
                            for s_ in range(2):
                                p = pT[s_][i % 2]
                                act(lambda s_=s_, p=p: A.activation(p[:, 0:nq], psS[s_][i % 2][:, 0:nq], AF.Exp, bias=stb[:, 3:4], scale=scale),
                                    [psS[s_][i % 2], stb], [p])
                            for s_ in range(2):
                                p = pT[s_][i % 2]
                                mm(psO[s_][:, 0:nq], vv[:, kc, :], p[:, 0:nq], i == 0, i == nch - 1, [vv, p], [psO[s_]])
                                az = accZ[s_]
                                if i == 0:
                                    dve(lambda p=p, az=az: V.tensor_copy(az[:, 0:nq], p[:, 0:nq]), [p], [az])
                                else:
                                    dve(lambda p=p, az=az: V.tensor_tensor(az[:, 0:nq], az[:, 0:nq], p[:, 0:nq], ALU.add), [az, p], [az])
                        for s_ in range(2):
                            hq = heads[s_]
                            mm(psZ[s_][:, 0:nq], onesf[:, :], accZ[s_][:, 0:nq], True, True, [onesf, accZ[s_]], [psZ[s_]])
                            dve(lambda s_=s_: V.reciprocal(rz[s_][:, 0:nq], psZ[s_][:, 0:nq]), [psZ[s_]], [rz[s_]])
                            dve(lambda s_=s_: V.tensor_tensor(ob[s_][:, 0:nq], psO[s_][:, 0:nq], rz[s_][:, 0:nq], ALU.mult), [psO[s_], rz[s_]], [ob[s_]])
                            tk.dma("sp", mixT_d[hq * 128:(hq + 1) * 128, c0:c0 + nq], ob[s_][:, 0:nq], [ob[s_]], [mixT_d])
        tk.barrier()

    def stage_E(l):
        with ExitStack() as es:
            cc = tk.sb(es, "ccE", [128, 256], BF16)
            tk.dma("sp", cc[:, :], t_CC[:, :], [], [cc])
            for (tok0, n, CLt, SLt, kbw) in ((LC, S, t_CL, t_SL, 512), (0, LC, t_CL2, t_SL2, 256)):
                nch = n // 128
                with ExitStack() as es2:
                    xf = tk.sb(es2, "xfE", [128, nch, 512], BF16)
                    CLp = tk.sb(es2, "CLp", [128, nch, kbw], BF16)
                    SLp = tk.sb(es2, "SLp", [128, nch, kbw], BF16)
                    psA = tk.ps(es2, "psAE", [128, 512])
                    psB = tk.ps(es2, "psBE", [128, 512])
                    psY = tk.ps(es2, "psYE", [128, 512])
                    aT = tk.sb(es2, "aTE", [128, 512], BF16)
                    bT = tk.sb(es2, "bTE", [128, 512], BF16)
                    yT = [tk.sb(es2, f"yTE{i}", [128, 512], BF16) for i in range(2)]
                    tk.dma("sp", xf[:, :, :], f_d[tok0:tok0 + n, :].rearrange("(c p) d -> p c d", p=128), [f_d], [xf])
                    k = 0
                    for kb in range(n // kbw):
                        tk.dma("sp", CLp[:, :, :], CLt[:, kb * kbw:(kb + 1) * kbw].rearrange("(c p) k -> p c k", p=128), [], [CLp])
                        tk.dma("sp", SLp[:, :, :], SLt[:, kb * kbw:(kb + 1) * kbw].rearrange("(c p) k -> p c k", p=128), [], [SLp])
                        for g in range(4):
                            for c in range(nch):
                                mm(psA[:, 0:kbw], xf[:, c, g * 128:(g + 1) * 128], CLp[:, c, :], c == 0, c == nch - 1, [xf, CLp], [psA])
                            for c in range(nch):
                                mm(psB[:, 0:kbw], xf[:, c, g * 128:(g + 1) * 128], SLp[:, c, :], c == 0, c == nch - 1, [xf, SLp], [psB])
                            dve(lambda: V.tensor_copy(aT[:, 0:kbw], psA[:, 0:kbw]), [psA], [aT])
                            act(lambda: A.copy(bT[:, 0:kbw], psB[:, 0:kbw]), [psB], [bT])
                            mm(psY[:, 0:kbw], cc[:, 0:128], aT[:, 0:kbw], True, False, [cc, aT], [psY])
                            mm(psY[:, 0:kbw], cc[:, 128:256], bT[:, 0:kbw], False, True, [cc, bT], [psY])
                            y = yT[k % 2]
                            k += 1
                            evac(y[:, 0:kbw], psY[:, 0:kbw], [psY], [y])
                            tk.dma("sp", mixT_d[1024 + g * 128:1024 + (g + 1) * 128, tok0 + kb * kbw:tok0 + (kb + 1) * kbw],
                                   y[:, 0:kbw], [y], [mixT_d])
                tk.barrier()

    def stage_F(l):
        with ExitStack() as es:
            rt = tk.sb(es, "rtF", [128, 6 * 128 + 2], F32)
            lgb = tk.sb(es, "lgb", [128, 8], F32)
            tk.dma("sp", rt[:, :], t_ret[:, :], [], [rt])
            tk.dma("sp", lgb[:, :], rld[l, :].partition_broadcast(128), [rld], [lgb])
            maskT = tk.sb(es, "maskT", [128, 128], F32)
            qdr = tk.sb(es, "qdr", [128, 128], F32)
            kdc = tk.sb(es, "kdc", [128, 2], F32)
            rqT = tk.sb(es, "rqTF", [128, NT, 128], BF16)
            rkT = tk.sb(es, "rkTF", [128, NT, 128], BF16)
            rkt = tk.sb(es, "rktF", [128, NT, 128], BF16)
            rvt = tk.sb(es, "rvtF", [128, NT, 128], BF16)
            gt = tk.sb(es, "gtF", [128, NT, 128], F32)
            smA = tk.sb(es, "smAF", [128, NT, 128], BF16)
            Uall = tk.sb(es, "UallF", [128, NT, 128], F32)
            SbA = tk.sb(es, "SbAF", [128, NT, 128], BF16)
            obuf = tk.sb(es, "obufF", [128, NT, 128], F32)
            Sf = tk.sb(es, "SfF", [128, 128], F32)
            st = tk.sb(es, "stF", [128, 6, NT], F32)
            psg = [tk.ps(es, f"psgF{i}", [128, 4, 128]) for i in range(4)]
            kp = [0]
            groups = [list(range(g, min(g + 4, NT))) for g in range(0, NT, 4)]
            for di in range(2):
                for h in range(4):
                    li = di * 4 + h
                    act(lambda: A.activation(maskT[:, :], rt[:, (2 * di) * 128:(2 * di + 1) * 128], AF.Exp, scale=lgb[:, li:li + 1]),
                        [rt, lgb], [maskT])
                    dve(lambda: V.tensor_tensor(maskT[:, :], maskT[:, :], rt[:, (2 * di + 1) * 128:(2 * di + 2) * 128], ALU.mult),
                        [maskT, rt], [maskT])
                    act(lambda: A.activation(qdr[:, :], rt[:, (4 + di) * 128:(5 + di) * 128], AF.Exp, scale=lgb[:, li:li + 1]),
                        [rt, lgb], [qdr])
                    act(lambda: A.activation(kdc[:, 0:1], rt[:, 768 + di:769 + di], AF.Exp, scale=lgb[:, li:li + 1]), [rt, lgb], [kdc])
                    act(lambda: A.activation(kdc[:, 1:2], lgb[:, li:li + 1], AF.Exp, scale=128.0), [lgb], [kdc])
                    tk.dma("sp", rqT[:, :, :], rT_d[di * 1024 + h * 128:di * 1024 + (h + 1) * 128, :].rearrange("p (c n) -> p c n", n=128),
                           [rT_d], [rqT])
                    tk.dma("sp", rkT[:, :, :], rT_d[di * 1024 + 512 + h * 128:di * 1024 + 512 + (h + 1) * 128, :].rearrange("p (c n) -> p c n", n=128),
                           [rT_d], [rkT])
                    tk.dma("sp", rkt[:, :, :], rk_d[:, di * 512 + h * 128:di * 512 + (h + 1) * 128].rearrange("(c p) d -> p c d", p=128),
                           [rk_d], [rkt])
                    if di == 0 or True:
                        tk.dma("sp", rvt[:, :, :], rv_d[:, h * 128:(h + 1) * 128].rearrange("(c p) d -> p c d", p=128), [rv_d], [rvt])
                    tk.dma("sp", gt[:, :, :], g_d[:, di * 512 + h * 128:di * 512 + (h + 1) * 128].rearrange("(c p) d -> p c d", p=128),
                           [g_d], [gt])
                    dve(lambda: V.tensor_scalar(rkt[:, :, :], rkt[:, :, :], kdc[:, 0:1], None, ALU.mult), [rkt, kdc], [rkt])
                    for grp in groups:
                        n = len(grp)
                        c0 = grp[0]
                        ps = psg[kp[0] % 4]
                        kp[0] += 1
                        for j, c in enumerate(grp):
                            mm(ps[:, j, :], rkT[:, c, :], rqT[:, c, :], True, True, [rkT, rqT], [ps])
                        dve(lambda ps=ps, n=n, c0=c0: V.tensor_tensor(smA[:, c0:c0 + n, :], ps[:, 0:n, :],
                                                                     maskT[:, :].unsqueeze(1).to_broadcast([128, n, 128]), ALU.mult),
                            [ps, maskT], [smA])
                        ps = psg[kp[0] % 4]
                        kp[0] += 1
                        for j, c in enumerate(grp):
                            mm(ps[:, j, :], rkt[:, c, :], rvt[:, c, :], True, True, [rkt, rvt], [ps])
                        act(lambda ps=ps, n=n, c0=c0: A.copy(Uall[:, c0:c0 + n, :], ps[:, 0:n, :]), [ps], [Uall])
                    dve(lambda: V.tensor_tensor(rqT[:, :, :], rqT[:, :, :], qdr[:, :].unsqueeze(1).to_broadcast([128, NT, 128]), ALU.mult),
                        [rqT, qdr], [rqT])
                    order = list(range(NT)) if di == 0 else [1, 0] + list(range(NT - 1, 1, -1))
                    dve(lambda: V.memset(Sf[:, :], 0.0), [], [Sf])
                    dve(lambda: V.memset(SbA[:, order[0], :], 0.0), [], [SbA])
                    for k in range(NT - 1):
                        c = order[k]
                        cn = order[k + 1]
                        dve(lambda c=c: V.scalar_tensor_tensor(Sf[:, :], Sf[:, :], kdc[:, 1:2], Uall[:, c, :], ALU.mult, ALU.add),
                            [Sf, kdc, Uall], [Sf])
                        dve(lambda cn=cn: V.tensor_copy(SbA[:, cn, :], Sf[:, :]), [Sf], [SbA])
                    for grp in groups:
                        n = len(grp)
                        c0 = grp[0]
                        ps = psg[kp[0] % 4]
                        kp[0] += 1
                        for j, c in enumerate(grp):
                            mm(ps[:, j, :], smA[:, c, :], rvt[:, c, :], True, False, [smA, rvt], [ps])
                            mm(ps[:, j, :], rqT[:, c, :], SbA[:, c, :], False, True, [rqT, SbA], [ps])
                        evac(obuf[:, c0:c0 + n, :], ps[:, 0:n, :], [ps], [obuf])
                    dve(lambda: V.tensor_reduce(st[:, 0, :], obuf[:, :, :], AX.X, ALU.add), [obuf], [st])
                    dve(lambda: V.tensor_scalar(st[:, 1, :], st[:, 0, :], -1.0 / 128, None, ALU.mult), [st], [st])
                    dve(lambda: V.tensor_tensor(obuf[:, :, :], obuf[:, :, :], st[:, 1, :].unsqueeze(2).to_broadcast([128, NT, 128]), ALU.add),
                        [obuf, st], [obuf])
                    act(lambda: A.activation(Uall[:, :, :], obuf[:, :, :], AF.Square), [obuf], [Uall])
                    dve(lambda: V.tensor_reduce(st[:, 2, :], Uall[:, :, :], AX.X, ALU.add), [Uall], [st])
                    dve(lambda: V.tensor_scalar(st[:, 3, :], st[:, 2, :], 1.0 / 128, EPS, ALU.mult, ALU.add), [st], [st])
                    act(lambda: A.activation(st[:, 4, :], st[:, 3, :], AF.Sqrt), [st], [st])
                    dve(lambda: V.reciprocal(st[:, 5, :], st[:, 4, :]), [st], [st])
                    dve(lambda: V.tensor_tensor(obuf[:, :, :], obuf[:, :, :], st[:, 5, :].unsqueeze(2).to_broadcast([128, NT, 128]), ALU.mult),
                        [obuf, st], [obuf])
                    dve(lambda: V.tensor_tensor(obuf[:, :, :], obuf[:, :, :], gt[:, :, :], ALU.mult), [obuf, gt], [obuf])
                    tk.dma("sp", ro_d[:, di * 512 + h * 128:di * 512 + (h + 1) * 128].rearrange("(c p) e -> p c e", p=128), obuf[:, :, :],
                           [obuf], [ro_d])
        tk.barrier()

    def stage_G(l):
        with ExitStack() as es:
            ro = [tk.sb(es, f"roG{i}", [128, 1024], F32) for i in range(2)]
            rb = [tk.sb(es, f"rbG{i}", [128, 512], BF16) for i in range(2)]
            tr = [tk.sb(es, f"trG{i}", [128, 512], BF16) for i in range(2)]
            pst = [tk.ps(es, f"pstG{i}", [128, 512]) for i in range(2)]
            for t in range(NT):
                r_, b_, t_, ps = ro[t % 2], rb[t % 2], tr[t % 2], pst[t % 2]
                tk.dma("sp", r_[:, :], ro_d[t * 128:(t + 1) * 128, :], [ro_d], [r_])
                dve(lambda: V.tensor_tensor(b_[:, :], r_[:, 0:512], r_[:, 512:1024], ALU.add), [r_], [b_])
                for j in range(4):
                    mm(ps[:, j * 128:(j + 1) * 128], b_[:, j * 128:(j + 1) * 128], identb[:, :], True, True, [b_, identb], [ps])
                evac(t_[:, :], ps[:, :], [ps], [t_])
                tk.dma("sp", mixT_d[1536:2048, t * 128:(t + 1) * 128].rearrange("(j p) n -> p j n", p=128),
                       t_[:, :].rearrange("p (j n) -> p j n", j=4), [t_], [mixT_d])
        tk.barrier()

    def stage_H(l, affT):
        with ExitStack() as es:
            wo = tk.sb(es, "woH", [128, 16, D], BF16)
            gm = tk.sb(es, "gmH", [128, D], F32)
            GT1 = tk.sb(es, "GT1", [128, D], F32)
            G2_ = tk.sb(es, "G2", [128, D], F32)
            SH2_ = tk.sb(es, "SH2", [128, D], F32)
            GT = [GT1, GT1]
            G2 = [G2_, G2_]
            SH2 = [SH2_, SH2_]
            wr = tk.sb(es, "wrH", [128, 16, 16], F32)
            xts = [tk.sb(es, f"xtH{i}", [128, D], F32) for i in range(2)]
            mT = [tk.sb(es, f"mTH{i}", [128, 16, 128], BF16) for i in range(2)]
            tmps = [tk.sb(es, f"tmpH{i}", [128, D], F32) for i in range(2)]
            h2bs = [tk.sb(es, f"h2bH{i}", [128, D], BF16) for i in range(2)]
            h2Th = tk.sb(es, "h2ThH", [128, 16, 128], BF16)
            h2Tl = tk.sb(es, "h2TlH", [128, 16, 128], BF16)
            h2lo = tk.sb(es, "h2loH", [128, D], BF16)
            wrh = tk.sb(es, "wrhH", [128, 16, 16], BF16)
            wrl = tk.sb(es, "wrlH", [128, 16, 16], BF16)
            st = tk.sb(es, "stH", [128, 8], F32)
            lg = tk.sb(es, "lgH", [128, 16], F32)
            psY = [tk.ps(es, f"psYH{i}", [128, 512]) for i in range(2)]
            psT = [tk.ps(es, f"psTH{i}", [128, 512]) for i in range(2)]
            psL = tk.ps(es, "psLH", [128, 16])
            psAT = tk.ps(es, "psATH", [16, 128])
            for nb in range(4):
                tk.dma("pool", wo[:, :, nb * 512:(nb + 1) * 512],
                       w_out[l * D:(l + 1) * D, nb * 512:(nb + 1) * 512].rearrange("(c p) n -> p c n", p=128), [w_out], [wo])
            tk.dma("sp", gm[:, :], norm_ffn[l, :].partition_broadcast(128), [norm_ffn], [gm])
            tk.dma("sp", wr[:, :, :], w_router[l * D:(l + 1) * D, :].rearrange("(c p) e -> p c e", p=128), [w_router], [wr])
            dve(lambda: V.tensor_copy(wrh[:, :, :], wr[:, :, :]), [wr], [wrh])
            dve(lambda: V.tensor_tensor(wrl[:, :, :], wr[:, :, :], wrh[:, :, :], ALU.subtract), [wr, wrh], [wrl])
            for t in range(NT):
                w = 1 if t < 2 else 0
                if t in (0, 2):
                    tk.dma("sp", GT[w][:, :], modrow(l, w, 2), [mod_d], [GT[w]])
                    tk.dma("sp", G2[w][:, :], modrow(l, w, 4), [mod_d], [G2[w]])
                    tk.dma("sp", SH2[w][:, :], modrow(l, w, 3), [mod_d], [SH2[w]])
                    dve(lambda w=w: V.tensor_tensor(G2[w][:, :], G2[w][:, :], gm[:, :], ALU.mult), [G2[w], gm], [G2[w]])
                m = mT[t % 2]
                xt, tmp, h2b = xts[t % 2], tmps[t % 2], h2bs[t % 2]
                junk = h2b
                h2f = tmp
                rows = slice(t * 128, (t + 1) * 128)
                tk.dma("sp", xt[:, :], xcur[rows, :], [xcur], [xt])
                tk.dma("sp", m[:, :, :], mixT_d[:, rows].rearrange("(c p) n -> p c n", p=128), [mixT_d], [m])
                for nb in range(4):
                    ps = psY[nb % 2]
                    cs = slice(nb * 512, (nb + 1) * 512)
                    for c in range(16):
                        mm(ps[:, :], m[:, c, :], wo[:, c, cs], c == 0, c == 15, [m, wo], [ps])
                    dve(lambda ps=ps, cs=cs, w=w: V.tensor_tensor(tmp[:, cs], ps[:, :], GT[w][:, cs], ALU.mult), [ps, GT[w]], [tmp])
                    dve(lambda cs=cs: V.tensor_tensor(xt[:, cs], xt[:, cs], tmp[:, cs], ALU.add), [xt, tmp], [xt])
                tk.dma("sp", xcur[rows, :], xt[:, :], [xt], [xcur])
                rms_rstd((junk, st), xt, D)
                dve(lambda w=w: V.scalar_tensor_tensor(tmp[:, :], xt[:, :], st[:, 3:4], G2[w][:, :], ALU.mult, ALU.mult), [xt, st, G2[w]], [tmp])
                dve(lambda w=w: V.tensor_tensor(h2f[:, :], tmp[:, :], SH2[w][:, :], ALU.add), [tmp, SH2[w]], [h2f])
                act(lambda: A.copy(h2b[:, :], h2f[:, :]), [h2f], [h2b])
                tk.dma("sp", h2_d[rows, :], h2b[:, :], [h2b], [h2_d])
                dve(lambda: V.tensor_tensor(h2lo[:, :], h2f[:, :], h2b[:, :], ALU.subtract), [h2f, h2b], [h2lo])
                kt_ = 0
                for (srcb, dstT) in ((h2b, h2Th), (h2lo, h2Tl)):
                    for g4 in range(4):
                        ps = psT[kt_ % 2]
                        kt_ += 1
                        for j in range(4):
                            c = g4 * 4 + j
                            mm(ps[:, j * 128:(j + 1) * 128], srcb[:, c * 128:(c + 1) * 128], identb[:, :], True, True, [srcb, identb], [ps])
                        evac(dstT[:, g4 * 4:(g4 + 1) * 4, :], ps[:, :].rearrange("p (j n) -> p j n", j=4), [ps], [dstT])
                combos = [(h2Th, wrh), (h2Th, wrl), (h2Tl, wrh)]
                for ci_, (hT_, w_) in enumerate(combos):
                    for c in range(16):
                        mm(psL[:, :], hT_[:, c, :], w_[:, c, :], ci_ == 0 and c == 0, ci_ == 2 and c == 15, [hT_, w_], [psL])
                dve(lambda: V.reduce_max(st[:, 4:5], psL[:, :], axis=AX.X), [psL], [st])
                dve(lambda: V.tensor_scalar(st[:, 5:6], st[:, 4:5], -1.0, None, ALU.mult), [st], [st])
                act(lambda: A.activation(lg[:, :], psL[:, :], AF.Exp, bias=st[:, 5:6], scale=1.0, accum_out=st[:, 6:7]), [psL, st], [lg, st])
                dve(lambda: V.reciprocal(st[:, 7:8], st[:, 6:7]), [st], [st])
                dve(lambda t=t: V.tensor_scalar(aff_all[:, t, :], lg[:, :], st[:, 7:8], None, ALU.mult), [lg, st], [aff_all])
                mm(psAT[:, :], aff_all[:, t, :], identf[:, :], True, True, [aff_all, identf], [psAT])
                evac(affT[:, rows], psAT[:, :], [psAT], [affT])
        tk.barrier()

    def stage_I(l, affT):
        with ExitStack() as es:
            lo = tk.sb(es, "loI", [16, 2], F32)
            hi = tk.sb(es, "hiI", [16, 2], F32)
            mid = tk.sb(es, "midI", [16, 2], F32)
            cnt = tk.sb(es, "cntI", [16, 2], F32)
            ge = tk.sb(es, "geI", [16, 2], F32)
            d1 = tk.sb(es, "d1I", [16, 2], F32)
            junk = tk.sb(es, "junkI", [16, S], F32)
            dg = tk.sb(es, "dgI", [16, 2, 16], F32)
            thrb = tk.sb(es, "thrbI", [128, 2, 16], F32)
            sel = tk.sb(es, "selI", [128, NT, 16], F32)
            selb = tk.sb(es, "selbI", [128, NT, 16], BF16)
            tmp = tk.sb(es, "tmpI", [128, 16], F32)
            psb = tk.ps(es, "psbI", [128, 16])
            pss = [tk.ps(es, f"pssI{i}", [128, 16]) for i in range(2)]
            pst = [tk.ps(es, f"pstI{i}", [16, 128]) for i in range(2)]
            sidxT = tk.sb(es, "sidxTI", [16, T], F32)
            gateT = tk.sb(es, "gateTI", [16, T], F32)
            dve(lambda: V.memset(lo[:, :], 0.0), [], [lo])
            dve(lambda: V.memset(hi[:, :], 1.0), [], [hi])
            sets = [(0, LC, CAP_C), (LC, S, CAP_L)]
            for it in range(34):
                dve(lambda: V.tensor_tensor(mid[:, :], lo[:, :], hi[:, :], ALU.add), [lo, hi], [mid])
                dve(lambda: V.tensor_scalar(mid[:, :], mid[:, :], 0.5, None, ALU.mult), [mid], [mid])
                for si, (c0, n, cap) in enumerate(sets):
                    dve(lambda si=si, c0=c0, n=n: V.tensor_scalar(junk[:, 0:n], affT[:, c0:c0 + n], mid[:, si:si + 1], None, ALU.is_ge),
                        [affT, mid], [junk])
                    dve(lambda si=si, n=n: V.reduce_sum(cnt[:, si:si + 1], junk[:, 0:n], axis=AX.X), [junk], [cnt])
                    dve(lambda si=si, cap=cap: V.tensor_scalar(ge[:, si:si + 1], cnt[:, si:si + 1], float(cap) - 0.5, None, ALU.is_ge), [cnt], [ge])
                dve(lambda: V.tensor_tensor(d1[:, :], mid[:, :], lo[:, :], ALU.subtract), [mid, lo], [d1])
                dve(lambda: V.tensor_tensor(d1[:, :], d1[:, :], ge[:, :], ALU.mult), [d1, ge], [d1])
                dve(lambda: V.tensor_tensor(lo[:, :], lo[:, :], d1[:, :], ALU.add), [lo, d1], [lo])
                dve(lambda: V.tensor_tensor(d1[:, :], hi[:, :], mid[:, :], ALU.subtract), [hi, mid], [d1])
                dve(lambda: V.tensor_tensor(d1[:, :], d1[:, :], ge[:, :], ALU.mult), [d1, ge], [d1])
                dve(lambda: V.tensor_tensor(hi[:, :], mid[:, :], d1[:, :], ALU.add), [mid, d1], [hi])
            for si in range(2):
                dve(lambda si=si: V.tensor_scalar(dg[:, si, :], identf[0:16, 0:16], lo[:, si:si + 1], None, ALU.mult), [identf, lo], [dg])
                mm(psb[:, :], onesf[0:16, :], dg[:, si, :], True, True, [onesf, dg], [psb])
                dve(lambda si=si: V.tensor_copy(thrb[:, si, :], psb[:, :]), [psb], [thrb])
            dve(lambda: V.tensor_tensor(sel[:, 0:2, :], aff_all[:, 0:2, :], thrb[:, 0, :].unsqueeze(1).to_broadcast([128, 2, 16]), ALU.is_ge),
                [aff_all, thrb], [sel])
            dve(lambda: V.tensor_tensor(sel[:, 2:NT, :], aff_all[:, 2:NT, :], thrb[:, 1, :].unsqueeze(1).to_broadcast([128, NT - 2, 16]), ALU.is_ge),
                [aff_all, thrb], [sel])
            dve(lambda: V.tensor_tensor(gate_all[:, :, :], aff_all[:, :, :], sel[:, :, :], ALU.mult), [aff_all, sel], [gate_all])
            dve(lambda: V.tensor_copy(selb[:, :, :], sel[:, :, :]), [sel], [selb])
            k = 0
            for (t0, t1, off) in ((0, 2, float(CAP_L)), (2, NT, 0.0)):
                for t in range(t0, t1):
                    ps = pss[k % 2]
                    k += 1
                    for c in range(t0, t):
                        mm(ps[:, :], onesb[:, :], selb[:, c, :], c == t0, False, [onesb, selb], [ps])
                    mm(ps[:, :], ustrb[:, :], selb[:, t, :], t == t0, True, [ustrb, selb], [ps])
                    dve(lambda ps=ps, off=off: V.tensor_scalar(tmp[:, :], ps[:, :], off + 1.0, None, ALU.add), [ps], [tmp])
                    dve(lambda t=t: V.tensor_tensor(tmp[:, :], tmp[:, :], sel[:, t, :], ALU.mult), [tmp, sel], [tmp])
                    dve(lambda t=t: V.tensor_scalar(sidx_all[:, t, :], tmp[:, :], -1.0, None, ALU.add), [tmp], [sidx_all])
            for t in range(NT):
                rows = slice(t * 128, (t + 1) * 128)
                mm(pst[0][:, :], sidx_all[:, t, :], identf[:, :], True, True, [sidx_all, identf], [pst[0]])
                dve(lambda rows=rows: V.tensor_copy(sidxT[:, rows], pst[0][:, :]), [pst[0]], [sidxT])
                mm(pst[1][:, :], gate_all[:, t, :], identf[:, :], True, True, [gate_all, identf], [pst[1]])
                act(lambda rows=rows: A.copy(gateT[:, rows], pst[1][:, :]), [pst[1]], [gateT])
            lo_ = tk.sb(es, "loTI", [16, T], F32)
            hib = tk.sb(es, "hibI", [16, T], BF16)
            lob = tk.sb(es, "lobI", [16, T], BF16)
            gtb = tk.sb(es, "gtbI", [16, T], BF16)
            dve(lambda: V.tensor_scalar(sidxT[:, :], sidxT[:, :], 1.0, None, ALU.add), [sidxT], [sidxT])
            cb = tk.sb(es, "cbI", [16, T], BF16)
            dve(lambda: V.tensor_scalar(hib[:, :], sidxT[:, :], 256.0, None, ALU.min), [sidxT], [hib])
            dve(lambda: V.tensor_scalar(lo_[:, :], sidxT[:, :], 256.0, None, ALU.min), [sidxT], [lo_])
            dve(lambda: V.tensor_tensor(sidxT[:, :], sidxT[:, :], lo_[:, :], ALU.subtract), [sidxT, lo_], [sidxT])
            dve(lambda: V.tensor_scalar(lob[:, :], sidxT[:, :], 256.0, None, ALU.min), [sidxT], [lob])
            dve(lambda: V.tensor_scalar(lo_[:, :], sidxT[:, :], 256.0, None, ALU.min), [sidxT], [lo_])
            dve(lambda: V.tensor_tensor(cb[:, :], sidxT[:, :], lo_[:, :], ALU.subtract), [sidxT, lo_], [cb])
            act(lambda: A.copy(gtb[:, :], gateT[:, :]), [gateT], [gtb])
            tk.dma("sp", shl_d[0:16, :], hib[:, :], [hib], [shl_d])
            tk.dma("sp", shl_d[16:32, :], lob[:, :], [lob], [shl_d])
            tk.dma("sp", shl_d[32:48, :], cb[:, :], [cb], [shl_d])
            tk.dma("sp", gtb_d[:, :], gtb[:, :], [gtb], [gtb_d])
        tk.barrier()

    def stage_J(l):
        with ExitStack() as es:
            h2b = tk.sb(es, "h2bJ", [128, NT, 512], BF16)
            P = [tk.sb(es, f"PJ{i}", [128, 512], BF16) for i in range(3)]
            psG = [tk.ps(es, f"psGJ{i}", [128, 512]) for i in range(4)]
            psC = tk.ps(es, "psCJ", [128, 4, 32])
            xg = [tk.sb(es, f"xgJ{i}", [128, 4, NSL], BF16) for i in range(2)]
            k = 0
            for db in range(4):
                tk.dma("sp", h2b[:, :, :], h2_d[:, db * 512:(db + 1) * 512].rearrange("(t p) d -> p t d", p=128), [h2_d], [h2b])
                for e in range(NE):
                    xo = xg[e % 2]
                    for t in range(NT):
                        p = P[k % 3]
                        k += 1
                        if t < 2:
                            dve(lambda p=p, t=t, e=e: V.tensor_scalar(p[:, 0:32], iota5[:, 0:32], float(CAP_L), sidx_all[:, t, e:e + 1],
                                                                      ALU.add, ALU.is_equal), [iota5, sidx_all], [p])
                            for j in range(4):
                                mm(psC[:, j, :], h2b[:, t, j * 128:(j + 1) * 128], p[:, 0:32], t == 0, t == 1, [h2b, p], [psC])
                        else:
                            dve(lambda p=p, t=t, e=e: V.tensor_scalar(p[:, :], iota5[:, :], sidx_all[:, t, e:e + 1], None, ALU.is_equal),
                                [iota5, sidx_all], [p])
                            for j in range(4):
                                mm(psG[j][:, :], h2b[:, t, j * 128:(j + 1) * 128], p[:, :], t == 2, t == NT - 1, [h2b, p], [psG[j]])
                    for j in range(4):
                        evac(xo[:, j, 0:CAP_L], psG[j][:, :], [psG[j]], [xo])
                    evac(xo[:, :, CAP_L:NSL], psC[:, :, :], [psC], [xo])
                    tk.dma("sp", xg_d[e * D + db * 512:e * D + (db + 1) * 512, :].rearrange("(j p) s -> p j s", p=128), xo[:, :, :], [xo], [xg_d])
        tk.barrier()

    def stage_K(l):
        with ExitStack() as es:
            wgs = [tk.sb(es, f"wgK{i}", [128, 16, FF], BF16) for i in range(2)]
            wus = [tk.sb(es, f"wuK{i}", [128, 16, FF], BF16) for i in range(2)]
            wd = tk.sb(es, "wdK", [128, 8, D], BF16)
            xg0 = tk.sb(es, "xgK", [128, 16, NSL], BF16)
            gT = tk.sb(es, "gTK", [128, 8, NSL], BF16)
            sa = [tk.sb(es, f"saK{i}", [128, NSL], F32) for i in range(2)]
            psA = [tk.ps(es, f"psAK{i}", [128, 512]) for i in range(2)]
            psU = [tk.ps(es, f"psUK{i}", [128, 512]) for i in range(2)]
            psA2 = tk.ps(es, "psA2K", [128, 2, 32])
            psY = [tk.ps(es, f"psYK{i}", [128, 512]) for i in range(2)]
            yo = [tk.sb(es, f"yoK{i}", [128, 512], BF16) for i in range(2)]
            ky = 0

            def load_gu(e):
                r0 = (l * NE + e) * D
                for hf in range(2):
                    fs = slice(hf * 512, (hf + 1) * 512)
                    tk.dma("pool", wgs[e % 2][:, :, fs], w_gate[r0:r0 + D, fs].rearrange("(c p) f -> p c f", p=128), [w_gate], [wgs[e % 2]])
                    tk.dma("pool", wus[e % 2][:, :, fs], w_up[r0:r0 + D, fs].rearrange("(c p) f -> p c f", p=128), [w_up], [wus[e % 2]])

            def load_d(e):
                r1 = (l * NE + e) * FF
                for nb in range(4):
                    cs = slice(nb * 512, (nb + 1) * 512)
                    tk.dma("pool", wd[:, :, cs], w_down[r1:r1 + FF, cs].rearrange("(c p) n -> p c n", p=128), [w_down], [wd])

            load_gu(0)
            load_d(0)
            for e in range(NE):
                wg, wu = wgs[e % 2], wus[e % 2]
                x_ = xg0
                tk.dma("sp", x_[:, :, :], xg_d[e * D:(e + 1) * D, :].rearrange("(c p) s -> p c s", p=128), [xg_d], [x_])
                if e + 1 < NE:
                    load_gu(e + 1)
                for fc in range(8):
                    pa, pu, s_ = psA[fc % 2], psU[fc % 2], sa[fc % 2]
                    fsl = slice(fc * 128, (fc + 1) * 128)
                    for c in range(16):
                        mm(pa[:, :], wg[:, c, fsl], x_[:, c, 0:CAP_L], c == 0, c == 15, [wg, x_], [pa])
                    for c in range(16):
                        mm(pu[:, :], wu[:, c, fsl], x_[:, c, 0:CAP_L], c == 0, c == 15, [wu, x_], [pu])
                    for c in range(16):
                        mm(psA2[:, 0, :], wg[:, c, fsl], x_[:, c, CAP_L:NSL], c == 0, c == 15, [wg, x_], [psA2])
                    for c in range(16):
                        mm(psA2[:, 1, :], wu[:, c, fsl], x_[:, c, CAP_L:NSL], c == 0, c == 15, [wu, x_], [psA2])
                    act(lambda: A.activation(s_[:, 0:CAP_L], pa[:, :], AF.Silu), [pa], [s_])
                    act(lambda: A.activation(s_[:, CAP_L:NSL], psA2[:, 0, :], AF.Silu), [psA2], [s_])
                    dve(lambda fc=fc: V.tensor_tensor(gT[:, fc, 0:CAP_L], s_[:, 0:CAP_L], pu[:, :], ALU.mult), [s_, pu], [gT])
                    dve(lambda fc=fc: V.tensor_tensor(gT[:, fc, CAP_L:NSL], s_[:, CAP_L:NSL], psA2[:, 1, :], ALU.mult), [s_, psA2], [gT])
                for stl in range(5):
                    s0 = stl * 128
                    m = 128 if stl < 4 else CAP_C
                    for nb in range(4):
                        ps, y_ = psY[ky % 2], yo[ky % 2]
                        ky += 1
                        for fc in range(8):
                            mm(ps[0:m, :], gT[:, fc, s0:s0 + m], wd[:, fc, nb * 512:(nb + 1) * 512], fc == 0, fc == 7, [gT, wd], [ps])
                        evac(y_[0:m, :], ps[0:m, :], [ps], [y_])
                        tk.dma("sp", y_d[e * NSL + s0:e * NSL + s0 + m, nb * 512:(nb + 1) * 512], y_[0:m, :], [y_], [y_d])
                if e + 1 < NE:
                    load_d(e + 1)
        tk.barrier()

    def stage_L(l, last):
        with ExitStack() as es:
            yb = tk.sb(es, "ybL", [128, NE, 5, 512], BF16)
            GT2 = [tk.sb(es, f"GT2{w}", [128, D], F32) for w in range(2)]
            psS = [tk.ps(es, f"psSL{i}", [128, 512]) for i in range(2)]
            psG = [tk.ps(es, f"psGL{i}", [128, 512]) for i in range(2)]
            psX = [tk.ps(es, f"psXL{i}", [128, 512]) for i in range(4)]
            eq = [tk.sb(es, f"eqL{i}", [128, 512], F32) for i in range(3)]
            ST = [tk.sb(es, f"STL{i}", [128, 512], BF16) for i in range(8)]
            xt = [tk.sb(es, f"xtL{i}", [128, 512], F32) for i in range(2)]
            tmp = [tk.sb(es, f"tmpL{i}", [128, 512], F32) for i in range(2)]
            eselHL = tk.sb(es, "eselHL", [48, NE, 128], BF16)
            eselG = tk.sb(es, "eselG", [16, NE, 128], BF16)
            shl = tk.sb(es, "shlL", [48, T], BF16)
            gtb = tk.sb(es, "gtbL", [16, T], BF16)
            iop1 = tk.sb(es, "iop1L", [128, 8], F32)
            tk.dma("pool", eselHL[:, :, :], t_esel[:, :].rearrange("k (e m) -> k e m", e=NE), [], [eselHL])
            dve(lambda: V.tensor_copy(eselG[:, :, :], identf[0:16, 0:16].unsqueeze(2).to_broadcast([16, 16, 128])), [identf], [eselG])
            dve(lambda: V.tensor_scalar(iop1[:, :], iotap[:, :], 1.0, None, ALU.add), [iotap], [iop1])
            tk.dma("sp", shl[:, :], shl_d[:, :], [shl_d], [shl])
            tk.dma("sp", gtb[:, :], gtb_d[:, :], [gtb_d], [gtb])
            for w in range(2):
                tk.dma("sp", GT2[w][:, :], modrow(l, w, 5), [mod_d], [GT2[w]])
            dve(lambda: V.memset(yb[:, :, :, :], 0.0), [], [yb])
            ks = 0
            kx = 0
            for db in range(4):
                dcs = slice(db * 512, (db + 1) * 512)
                for e in range(NE):
                    tk.dma("sp", yb[:, e, 0:4, :], y_d[e * NSL:e * NSL + CAP_L, dcs].rearrange("(c p) d -> p c d", p=128), [y_d], [yb])
                    tk.dma("sp", yb[0:CAP_C, e, 4, :], y_d[e * NSL + CAP_L:(e + 1) * NSL, dcs], [y_d], [yb])
                blocks = [(0, 256, True)] + [(256 + qb * 512, 512, False) for qb in range(8)]
                for (c0, ntok, isctx) in blocks:
                    if last and isctx:
                        continue
                    nsub = ntok // 128
                    chunks = [4] if isctx else [0, 1, 2, 3]
                    np_ = CAP_C if isctx else 128

                    def bc(e):
                        mm(psS[e % 2][:, 0:ntok], eselHL[:, e, :], shl[:, c0:c0 + ntok], True, True, [eselHL, shl], [psS[e % 2]])
                        mm(psG[e % 2][:, 0:ntok], eselG[:, e, :], gtb[:, c0:c0 + ntok], True, True, [eselG, gtb], [psG[e % 2]])
                    bc(0)
                    for e in range(NE):
                        if e + 1 < NE:
                            bc(e + 1)
                        pS, pG = psS[e % 2], psG[e % 2]
                        gq_ = eq[e % 3]
                        act(lambda: A.copy(gq_[0:np_, 0:ntok], pG[0:np_, 0:ntok]), [pG], [gq_])
                        sts = []
                        for ci, c in enumerate(chunks):
                            s_ = ST[ks % 8]
                            ks += 1
                            dve(lambda s_=s_, c=c: V.scalar_tensor_tensor(
                                s_[0:np_, 0:ntok], pS[0:np_, 0:ntok], iop1[0:np_, c:c + 1], gq_[0:np_, 0:ntok], ALU.is_equal, ALU.mult),
                                [pS, iop1, gq_], [s_])
                            sts.append(s_)
                        for ci, c in enumerate(chunks):
                            s_ = sts[ci]
                            first = (e == 0 and ci == 0)
                            lastm = (e == NE - 1 and ci == len(chunks) - 1)
                            for m in range(nsub):
                                mm(psX[m][:, :], s_[0:np_, m * 128:(m + 1) * 128], yb[0:np_, e, c, :], first, lastm, [s_, yb], [psX[m]])
                    w = 1 if isctx else 0
                    for m in range(nsub):
                        x_, t_ = xt[kx % 2], tmp[kx % 2]
                        kx += 1
                        rows = slice(c0 + m * 128, c0 + (m + 1) * 128)
                        tk.dma("sp", x_[:, :], xcur[rows, dcs], [xcur], [x_])
                        dve(lambda m=m, t_=t_, w=w: V.tensor_tensor(t_[:, :], psX[m][:, :], GT2[w][:, dcs], ALU.mult), [psX[m], GT2[w]], [t_])
                        dve(lambda x_=x_, t_=t_: V.tensor_tensor(x_[:, :], x_[:, :], t_[:, :], ALU.add), [x_, t_], [x_])
                        if last:
                            tk.dma("sp", out_d[c0 - LC + m * 128:c0 - LC + (m + 1) * 128, dcs], x_[:, :], [x_], [out_d])
                        else:
                            tk.dma("sp", xcur[rows, dcs], x_[:, :], [x_], [xcur])
        tk.barrier()

    tk.marks = []

    def mark(name):
        tk.marks.append((name, dict(tk.ccnt)))
    stage_adaln()
    mark("ada")
    for l in range(depth):
        stage_A(l)
        mark(f"A{l}")
        stage_B(l)
        mark(f"B{l}")
        stage_C(l)
        mark(f"C{l}")
        stage_D(l)
        mark(f"D{l}")
        stage_E(l)
        mark(f"E{l}")
        stage_F(l)
        mark(f"F{l}")
        stage_G(l)
        mark(f"G{l}")
        with ExitStack() as esHI:
            affT = tk.sb(esHI, "affT", [16, T], F32)
            stage_H(l, affT)
            mark(f"H{l}")
            stage_I(l, affT)
            mark(f"I{l}")
        stage_J(l)
        mark(f"J{l}")
        stage_K(l)
        mark(f"K{l}")
        stage_L(l, l == depth - 1)
        mark(f"L{l}")
    tk.barrier()
    top.close()
    return nc, tk


def _tables():
    f32 = np.float32
    bf = ml_dtypes.bfloat16
    tb = {}
    tb["t_ident"] = np.eye(128, dtype=f32)
    tb["t_ustr"] = np.triu(np.ones((128, 128), f32), 1)
    es_ = np.zeros((48, 16, 128), f32)
    for e in range(16):
        es_[e, e, :] = 1.0
        es_[16 + e, e, :] = 1.0
        es_[32 + e, e, :] = 1.0
    tb["t_esel"] = es_.reshape(48, 16 * 128)

    def rope_tab(pos, half):
        freqs = (f32(10000.0) ** (-np.arange(half, dtype=f32) / f32(half))).astype(f32)
        ang = (pos.astype(f32)[:, None] * freqs[None, :]).astype(f32)
        return np.cos(ang).astype(f32), np.sin(ang).astype(f32)

    rows = np.repeat(np.arange(S // 64), 64)
    cols = np.tile(np.arange(64), S // 64)
    cr, sr = rope_tab(rows, 32)
    cc_, sc_ = rope_tab(cols, 32)
    cosA = np.concatenate([cr, cr, cc_, cc_], axis=1)
    sinA = np.concatenate([-sr, sr, -sc_, sc_], axis=1)
    A_ = np.zeros((T, 256), f32)
    A_[:LC, :128] = 1.0
    A_[LC:, :128] = cosA
    A_[LC:, 128:] = sinA
    tb["t_ropeA"] = A_
    posF = np.arange(T)
    posB = np.concatenate([LC - 1 - np.arange(LC), LC + (S - 1 - np.arange(S))])
    for nm, pos in (("t_ropeF", posF), ("t_ropeB", posB)):
        c_, s_ = rope_tab(pos, 64)
        tb[nm] = np.concatenate([c_, c_, -s_, s_], axis=1).astype(f32)

    def dft(n):
        idx = np.arange(n, dtype=np.int64)
        m = (idx[:, None] * idx[None, :]) % n
        ang = 2.0 * np.pi * m.astype(np.float64) / n
        return np.cos(ang), np.sin(ang)
    c, s = dft(S)
    tb["t_CL"] = (c / np.sqrt(S)).astype(bf)
    tb["t_SL"] = (s / np.sqrt(S)).astype(bf)
    c, s = dft(LC)
    tb["t_CL2"] = (c / np.sqrt(LC)).astype(bf)
    tb["t_SL2"] = (s / np.sqrt(LC)).astype(bf)
    c, s = dft(128)
    tb["t_CC"] = np.concatenate([c / np.sqrt(128.0), -s / np.sqrt(128.0)], axis=1).astype(bf)
    j = np.arange(128, dtype=f32)[:, None]
    i = np.arange(128, dtype=f32)[None, :]
    Df = np.maximum(i - j, 0.0)
    Mf = (i >= j).astype(f32)
    Db = np.maximum(j - i, 0.0)
    Mb = (j >= i).astype(f32)
    rowf = np.broadcast_to(i + 1.0, (128, 128))
    rowb = np.broadcast_to(128.0 - i, (128, 128))
    colf = 127.0 - j
    colb = j + 0.0
    tb["t_ret"] = np.ascontiguousarray(np.concatenate([Df, Mf, Db, Mb, rowf, rowb, colf, colb], axis=1).astype(f32))
    return tb


_CACHE = {}


def kernel(x, c, ctx, c_ctx, w_ada, b_ada, norm_mix, norm_ffn, w_in, q_norm, k_norm,
           ret_log_decay, w_out, w_router, w_gate, w_up, w_down):
    depth = int(os.environ.get("MK_DEPTH", w_ada.shape[0]))
    ncores = int(os.environ.get("MK_CORES", x.shape[0]))
    f = np.ascontiguousarray
    tb = _tables()
    shared = {
        "w_ada": f(w_ada[:depth].reshape(depth * D, 6 * D)),
        "b_ada": f(b_ada[:depth]),
        "norm_mix": f(norm_mix[:depth]),
        "norm_ffn": f(norm_ffn[:depth]),
        "w_in": f(w_in[:depth].reshape(depth * D, INW)),
        "q_norm": f(q_norm[:depth]),
        "k_norm": f(k_norm[:depth]),
        "rld": f(ret_log_decay[:depth].reshape(depth, 8)),
        "w_out": f(w_out[:depth].reshape(depth * D, D)),
        "w_router": f(w_router[:depth].reshape(depth * D, NE)),
        "w_gate": f(w_gate[:depth].reshape(depth * NE * D, FF)),
        "w_up": f(w_up[:depth].reshape(depth * NE * D, FF)),
        "w_down": f(w_down[:depth].reshape(depth * NE * FF, D)),
    }
    shared.update(tb)
    in_maps = []
    for b in range(ncores):
        m = dict(shared)
        m["x"] = f(x[b])
        m["ctx"] = f(ctx[b])
        m["cvec"] = f(np.stack([c[b], c_ctx], axis=0))
        in_maps.append(m)
    nc, tk = build(depth)
    res = run_bass_kernel_spmd(nc, in_maps, core_ids=list(range(ncores)))
    out = np.stack([np.asarray(res.results[b]["out"]) for b in range(ncores)], axis=0)
    return out.astype(np.float32)
```

```python
import os
import numpy as np
import ml_dtypes
from contextlib import ExitStack
import concourse.bass as bass
import concourse.mybir as mybir
from concourse.bass_utils import run_bass_kernel_spmd

F32 = mybir.dt.float32
BF16 = mybir.dt.bfloat16
ALU = mybir.AluOpType
AF = mybir.ActivationFunctionType
AX = mybir.AxisListType

D = 2048
S = 4096
LC = 256
T = S + LC
NT = T // 128
NE = 16
FF = 1024
INW = 4608
CAP_L = 512
CAP_C = 32
NSL = CAP_L + CAP_C
EPS = 1e-6


class Buf:
    __slots__ = ("t", "w", "r", "name")

    def __init__(self, t, name=""):
        self.t = t
        self.w = None
        self.r = {}
        self.name = name

    def __getitem__(self, idx):
        return self.t[idx]


class TK:
    NDMA = 8

    def __init__(self, nc, es):
        self.nc = nc
        self.es = es
        self.eng = {"pe": nc.tensor, "act": nc.scalar, "dve": nc.vector, "pool": nc.gpsimd, "sp": nc.sync}
        self.csem = {k: es.enter_context(nc.semaphore("s_" + k)) for k in ("pe", "act", "dve", "pool")}
        self.ccnt = {k: 0 for k in self.csem}
        self.dsem = {q: [es.enter_context(nc.semaphore(f"d_{q}{i}")) for i in range(self.NDMA)] for q in ("sp", "pool")}
        self.dcnt = {q: [0] * self.NDMA for q in self.dsem}
        self.dnext = {q: 0 for q in self.dsem}
        self.waited = {}
        self.n_inst = 0
        self.uid = 0

    def sb(self, es, name, shape, dt):
        self.uid += 1
        return Buf(es.enter_context(self.nc.sbuf_tensor(f"{name}_{self.uid}", shape, dt)), name)

    def ps(self, es, name, shape, dt=F32):
        self.uid += 1
        return Buf(es.enter_context(self.nc.psum_tensor(f"{name}_{self.uid}", shape, dt)), name)

    def dram(self, name, shape, dt, kind="Internal"):
        return Buf(self.nc.dram_tensor(name, shape, dt, kind=kind), name)

    def _wait(self, e, tok):
        sem, val = tok[0], tok[1]
        key = (e, id(sem))
        if self.waited.get(key, 0) >= val:
            return
        self.waited[key] = val
        self.eng[e].wait_ge(sem, val)

    def _deps(self, e, reads, writes, skip_same):
        for b in list(reads) + list(writes):
            if b.w is not None and not (skip_same and b.w[2] == e):
                self._wait(e, b.w)
        for b in writes:
            for tok in b.r.values():
                if not (skip_same and tok[2] == e):
                    self._wait(e, tok)

    def _commit(self, tok, reads, writes):
        for b in writes:
            b.w = tok
            b.r = {}
        for b in reads:
            if b in writes:
                continue
            b.r[id(tok[0])] = tok

    def op(self, e, fn, reads=(), writes=(), skip_same=False):
        self._deps(e, reads, writes, skip_same)
        ins = fn()
        self.ccnt[e] += 1
        ins.then_inc(self.csem[e], 1)
        tok = (self.csem[e], self.ccnt[e], e)
        self._commit(tok, reads, writes)
        self.n_inst += 1
        return tok

    def dma(self, q, out, in_, reads=(), writes=(), **kw):
        i = self.dnext[q]
        self.dnext[q] = (i + 1) % self.NDMA
        sem = self.dsem[q][i]
        if self.dcnt[q][i] > 0:
            self._wait(q, (sem, 16 * self.dcnt[q][i]))
        self._deps(q, reads, writes, False)
        ins = self.eng[q].dma_start(out=out, in_=in_, **kw)
        self.dcnt[q][i] += 1
        ins.then_inc(sem, 16)
        tok = (sem, 16 * self.dcnt[q][i], "dma_" + q)
        self._commit(tok, reads, writes)
        self.n_inst += 1
        return tok

    def barrier(self):
        for e in ("pe", "act", "dve", "pool", "sp"):
            for k in self.csem:
                if self.ccnt[k] > 0:
                    self._wait(e, (self.csem[k], self.ccnt[k]))
            for q in self.dsem:
                for i in range(self.NDMA):
                    if self.dcnt[q][i] > 0:
                        self._wait(e, (self.dsem[q][i], 16 * self.dcnt[q][i]))


def build(depth):
    nc = bass.Bass("TRN2", target_bir_lowering=False)
    top = ExitStack()
    tk = TK(nc, top)
    V, A, PE = nc.vector, nc.scalar, nc.tensor

    def dve(fn, r, w):
        return tk.op("dve", fn, r, w)

    def act(fn, r, w):
        return tk.op("act", fn, r, w)

    def mm(ps_ap, lhsT, rhs, start, stop, r, w):
        return tk.op("pe", lambda: PE.matmul(ps_ap, lhsT, rhs, start=start, stop=stop), r, w, skip_same=True)

    flip = [0]

    def evac(out_ap, in_ap, r, w):
        flip[0] ^= 1
        if flip[0]:
            return dve(lambda: V.tensor_copy(out_ap, in_ap), r, w)
        return act(lambda: A.copy(out_ap, in_ap), r, w)

    EI = "ExternalInput"
    x_in = tk.dram("x", [S, D], F32, EI)
    ctx_in = tk.dram("ctx", [LC, D], F32, EI)
    cvec = tk.dram("cvec", [2, D], F32, EI)
    w_ada = tk.dram("w_ada", [depth * D, 6 * D], F32, EI)
    b_ada = tk.dram("b_ada", [depth, 6 * D], F32, EI)
    norm_mix = tk.dram("norm_mix", [depth, D], F32, EI)
    norm_ffn = tk.dram("norm_ffn", [depth, D], F32, EI)
    w_in = tk.dram("w_in", [depth * D, INW], F32, EI)
    q_norm = tk.dram("q_norm", [depth, 128], F32, EI)
    k_norm = tk.dram("k_norm", [depth, 128], F32, EI)
    rld = tk.dram("rld", [depth, 8], F32, EI)
    w_out = tk.dram("w_out", [depth * D, D], F32, EI)
    w_router = tk.dram("w_router", [depth * D, NE], F32, EI)
    w_gate = tk.dram("w_gate", [depth * NE * D, FF], F32, EI)
    w_up = tk.dram("w_up", [depth * NE * D, FF], F32, EI)
    w_down = tk.dram("w_down", [depth * NE * FF, D], F32, EI)
    t_ident = tk.dram("t_ident", [128, 128], F32, EI)
    t_ropeA = tk.dram("t_ropeA", [T, 256], F32, EI)
    t_ropeF = tk.dram("t_ropeF", [T, 256], F32, EI)
    t_ropeB = tk.dram("t_ropeB", [T, 256], F32, EI)
    t_CL = tk.dram("t_CL", [S, S], BF16, EI)
    t_SL = tk.dram("t_SL", [S, S], BF16, EI)
    t_CL2 = tk.dram("t_CL2", [LC, LC], BF16, EI)
    t_SL2 = tk.dram("t_SL2", [LC, LC], BF16, EI)
    t_CC = tk.dram("t_CC", [128, 256], BF16, EI)
    t_ret = tk.dram("t_ret", [128, 6 * 128 + 2], F32, EI)
    t_ustr = tk.dram("t_ustr", [128, 128], F32, EI)
    out_d = tk.dram("out", [S, D], F32, "ExternalOutput")

    xcur = tk.dram("xcur", [T, D], F32)
    mod_d = tk.dram("mod_d", [depth * 12 * 128, D], F32)
    hT_d = tk.dram("hT_d", [T, D], BF16)
    p_d = tk.dram("p_d", [T, INW], F32)
    qT_d = tk.dram("qT_d", [1024, T], BF16)
    kT_d = tk.dram("kT_d", [256, T], BF16)
    v_d = tk.dram("v_d", [T, 256], BF16)
    f_d = tk.dram("f_d", [T, 512], BF16)
    rT_d = tk.dram("rT_d", [4 * 512, T], BF16)
    rk_d = tk.dram("rk_d", [T, 1024], BF16)
    rv_d = tk.dram("rv_d", [T, 512], BF16)
    g_d = tk.dram("g_d", [T, 1024], F32)
    ro_d = tk.dram("ro_d", [T, 1024], F32)
    mixT_d = tk.dram("mixT_d", [D, T], BF16)
    h2_d = tk.dram("h2_d", [T, D], BF16)
    xg_d = tk.dram("xg_d", [NE * D, NSL], BF16)
    y_d = tk.dram("y_d", [NE * NSL, D], BF16)

    identf = tk.sb(top, "identf", [128, 128], F32)
    identb = tk.sb(top, "identb", [128, 128], BF16)
    onesf = tk.sb(top, "onesf", [128, 128], F32)
    onesb = tk.sb(top, "onesb", [128, 128], BF16)
    ustrb = tk.sb(top, "ustrb", [128, 128], BF16)
    iota5 = tk.sb(top, "iota5", [128, 512], F32)
    iotap = tk.sb(top, "iotap", [128, 8], F32)
    tk.dma("sp", identf[:, :], t_ident[:, :], [t_ident], [identf])
    tk.dma("pool", identb[:, :], t_ident[:, :], [t_ident], [identb])
    tk.dma("pool", ustrb[:, :], t_ustr[:, :], [t_ustr], [ustrb])
    tk.op("pool", lambda: nc.gpsimd.memset(onesf[:, :], 1.0), [], [onesf])
    tk.op("pool", lambda: nc.gpsimd.memset(onesb[:, :], 1.0), [], [onesb])
    tk.op("pool", lambda: nc.gpsimd.iota(iota5[:, :], [[1, 512]], base=0, channel_multiplier=0,
                                          allow_small_or_imprecise_dtypes=True), [], [iota5])
    tk.op("pool", lambda: nc.gpsimd.iota(iotap[:, :], [[128, 8]], base=0, channel_multiplier=1,
                                          allow_small_or_imprecise_dtypes=True), [], [iotap])
    aff_all = tk.sb(top, "aff_all", [128, NT, 16], F32)
    sidx_all = tk.sb(top, "sidx_all", [128, NT, 16], F32)
    gate_all = tk.sb(top, "gate_all", [128, NT, 16], F32)
    shl_d = tk.dram("shl_d", [48, T], BF16)
    gtb_d = tk.dram("gtb_d", [16, T], BF16)
    t_esel = tk.dram("t_esel", [48, 16 * 128], F32, "ExternalInput")

    def modrow(l, w, i):
        r0 = ((l * 12) + w * 6 + i) * 128
        return mod_d[r0:r0 + 128, :]

    def stage_adaln():
        with ExitStack() as es:
            condT = tk.sb(es, "condT", [128, 2, 16], F32)
            scb = [tk.sb(es, f"scb{w}", [128, 16, 128], F32) for w in range(2)]
            wts = [tk.sb(es, f"wada{i}", [128, 16, 512], F32) for i in range(2)]
            brow = [tk.sb(es, f"brow{i}", [1, 512], F32) for i in range(2)]
            pss = [tk.ps(es, f"psada{i}", [128, 512]) for i in range(2)]
            ot = [tk.sb(es, f"oada{i}", [128, 512], F32) for i in range(2)]
            tk.dma("sp", condT[:, :, :], cvec[:, :].rearrange("w (c p) -> p w c", p=128), [cvec], [condT],
                   allow_slow_non_contiguous=True)
            act(lambda: A.activation(condT[:, :, :], condT[:, :, :], AF.Silu), [condT], [condT])
            for w in range(2):
                for c in range(16):
                    dve(lambda w=w, c=c: V.tensor_scalar(scb[w][:, c, :], onesf[:, :], condT[:, w, c:c + 1], None, ALU.mult),
                        [onesf, condT], [scb[w]])
            k = 0
            for l in range(depth):
                for nb in range(24):
                    wt = wts[nb % 2]
                    br = brow[nb % 2]
                    tk.dma("sp" if nb % 2 == 0 else "pool", wt[:, :, :],
                           w_ada[l * D:(l + 1) * D, nb * 512:(nb + 1) * 512].rearrange("(c p) n -> p c n", p=128),
                           [w_ada], [wt])
                    tk.dma("sp", br[:, :], b_ada[l:l + 1, nb * 512:(nb + 1) * 512], [b_ada], [br])
                    piece = nb // 4
                    for w in range(2):
                        ps = pss[k % 2]
                        o = ot[k % 2]
                        k += 1
                        for c in range(16):
                            mm(ps[:, :], scb[w][:, c, :], wt[:, c, :], c == 0, False, [scb[w], wt], [ps])
                        mm(ps[:, :], onesf[0:1, :], br[0:1, :], False, True, [onesf, br], [ps])
                        if piece in (1, 4):
                            dve(lambda ps=ps, o=o: V.tensor_scalar(o[:, :], ps[:, :], 1.0, None, ALU.add), [ps], [o])
                        else:
                            evac(o[:, :], ps[:, :], [ps], [o])
                        tk.dma("sp", modrow(l, w, piece)[:, (nb % 4) * 512:(nb % 4 + 1) * 512], o[:, :], [o], [mod_d])
        tk.barrier()

    def rms_rstd(es_bufs, xt, width):
        junk, st = es_bufs
        act(lambda: A.activation(junk[:, 0:width], xt[:, 0:width], AF.Square, accum_out=st[:, 0:1]), [xt], [junk, st])
        dve(lambda: V.tensor_scalar(st[:, 1:2], st[:, 0:1], 1.0 / width, EPS, ALU.mult, ALU.add), [st], [st])
        act(lambda: A.activation(st[:, 2:3], st[:, 1:2], AF.Sqrt), [st], [st])
        dve(lambda: V.reciprocal(st[:, 3:4], st[:, 2:3]), [st], [st])
        return st

    def stage_A(l):
        with ExitStack() as es:
            gm = tk.sb(es, "gm", [128, D], F32)
            G = [tk.sb(es, f"G{w}", [128, D], F32) for w in range(2)]
            SH = [tk.sb(es, f"SH{w}", [128, D], F32) for w in range(2)]
            xts = [tk.sb(es, f"xt{i}", [128, D], F32) for i in range(2)]
            junk = tk.sb(es, "junk", [128, D], BF16)
            tmp = tk.sb(es, "tmpA", [128, D], F32)
            hb = [tk.sb(es, f"hb{i}", [128, D], BF16) for i in range(2)]
            hTt = [tk.sb(es, f"hTt{i}", [128, 16, 128], BF16) for i in range(2)]
            sts = [tk.sb(es, f"stA{i}", [128, 4], F32) for i in range(2)]
            pst = [tk.ps(es, f"pstA{i}", [128, 512]) for i in range(4)]
            tk.dma("sp", gm[:, :], norm_mix[l, :].partition_broadcast(128), [norm_mix], [gm])
            for w in range(2):
                tk.dma("sp", G[w][:, :], modrow(l, w, 1), [mod_d], [G[w]])
                tk.dma("sp", SH[w][:, :], modrow(l, w, 0), [mod_d], [SH[w]])
                dve(lambda w=w: V.tensor_tensor(G[w][:, :], G[w][:, :], gm[:, :], ALU.mult), [G[w], gm], [G[w]])
            for t in range(NT):
                w = 1 if t < 2 else 0
                xt = xts[t % 2]
                if l == 0:
                    src = ctx_in[t * 128:(t + 1) * 128, :] if t < 2 else x_in[(t - 2) * 128:(t - 1) * 128, :]
                    tk.dma("sp", xt[:, :], src, [], [xt])
                    tk.dma("sp", xcur[t * 128:(t + 1) * 128, :], xt[:, :], [xt], [xcur])
                else:
                    tk.dma("sp", xt[:, :], xcur[t * 128:(t + 1) * 128, :], [xcur], [xt])
                st = rms_rstd((junk, sts[t % 2]), xt, D)
                h = hb[t % 2]
                dve(lambda xt=xt, st=st, w=w: V.scalar_tensor_tensor(tmp[:, :], xt[:, :], st[:, 3:4], G[w][:, :], ALU.mult, ALU.mult),
                    [xt, st, G[w]], [tmp])
                dve(lambda h=h, w=w: V.tensor_tensor(h[:, :], tmp[:, :], SH[w][:, :], ALU.add), [tmp, SH[w]], [h])
                ht = hTt[t % 2]
                for g4 in range(4):
                    ps = pst[g4]
                    for j in range(4):
                        c = g4 * 4 + j
                        mm(ps[:, j * 128:(j + 1) * 128], h[:, c * 128:(c + 1) * 128], identb[:, :], True, True, [h, identb], [ps])
                    evac(ht[:, g4 * 4:(g4 + 1) * 4, :], ps[:, :].rearrange("p (j n) -> p j n", j=4), [ps], [ht])
                tk.dma("sp", hT_d[t * 128:(t + 1) * 128, :], ht[:, :, :].rearrange("p c n -> p (c n)"), [ht], [hT_d])
        tk.barrier()

    def stage_B(l):
        with ExitStack() as es:
            hTall = tk.sb(es, "hTall", [128, NT, D], BF16)
            hts = [Buf(hTall.t, f"hTall{t}") for t in range(NT)]
            wg = [tk.sb(es, f"wgB{i}", [128, 16, 512], BF16) for i in range(2)]
            pss = [tk.ps(es, f"psB{i}", [128, 512]) for i in range(4)]
            ot = [tk.sb(es, f"oB{i}", [128, 512], F32) for i in range(4)]
            for t in range(NT):
                tk.dma("sp", hTall[:, t, :], hT_d[t * 128:(t + 1) * 128, :], [hT_d], [hts[t]])
            k = 0
            for g in range(9):
                w = wg[g % 2]
                tk.dma("pool", w[:, :, :], w_in[l * D:(l + 1) * D, g * 512:(g + 1) * 512].rearrange("(c p) n -> p c n", p=128),
                       [w_in], [w])
                for t in range(NT):
                    ps = pss[k % 4]
                    o = ot[k % 4]
                    k += 1
                    for c in range(16):
                        mm(ps[:, :], hTall[:, t, c * 128:(c + 1) * 128], w[:, c, :], c == 0, c == 15, [hts[t], w], [ps])
                    evac(o[:, :], ps[:, :], [ps], [o])
                    tk.dma("sp", p_d[t * 128:(t + 1) * 128, g * 512:(g + 1) * 512], o[:, :], [o], [p_d])
        tk.barrier()

    def stage_C(l):
        with ExitStack() as es:
            gqk = tk.sb(es, "gqk", [128, 10, 128], F32)
            pts = [tk.sb(es, f"ptC{i}", [128, INW], F32) for i in range(2)]
            ropes = [tk.sb(es, f"ropeC{i}", [128, 3, 256], F32) for i in range(2)]
            junks = [tk.sb(es, f"junkC{i}", [128, 1280], F32) for i in range(2)]
            sts = [tk.sb(es, f"stC{i}", [128, 40], F32) for i in range(2)]
            qn = tk.sb(es, "qn", [128, 10, 128], F32)
            t1 = tk.sb(es, "t1C", [128, 10, 128], F32)
            t2 = tk.sb(es, "t2C", [128, 10, 128], F32)
            qrs = [tk.sb(es, f"qr{i}", [128, 1280], BF16) for i in range(2)]
            rrs = [tk.sb(es, f"rr{i}", [128, 8, 128], F32) for i in range(2)]
            rfbs = [[tk.sb(es, f"rfb{i}_{j}", [128, 1024], BF16) for i in range(2)] for j in range(2)]
            gss = [tk.sb(es, f"gsC{i}", [128, 1024], F32) for i in range(2)]
            trs = [tk.sb(es, f"trC{i}", [128, 512], BF16) for i in range(2)]
            pst = [tk.ps(es, f"pstC{i}", [128, 512]) for i in range(2)]
            for h in range(10):
                src = q_norm if h < 8 else k_norm
                tk.dma("sp", gqk[:, h, :], src[l, :].partition_broadcast(128), [src], [gqk])
            kk = [0]

            def transpose_out(srcbuf, col0, nblk, dst, row0, t):
                ps = pst[kk[0] % 2]
                tr = trs[kk[0] % 2]
                kk[0] += 1
                for j in range(nblk):
                    mm(ps[:, j * 128:(j + 1) * 128], srcbuf[:, col0 + j * 128:col0 + (j + 1) * 128], identb[:, :], True, True,
                       [srcbuf, identb], [ps])
                evac(tr[:, 0:nblk * 128], ps[:, 0:nblk * 128], [ps], [tr])
                tk.dma("sp", dst[row0:row0 + nblk * 128, t * 128:(t + 1) * 128].rearrange("(j p) n -> p j n", p=128),
                       tr[:, 0:nblk * 128].rearrange("p (j n) -> p j n", j=nblk), [tr], [dst])

            def rope(src3, nh, half, cs, out3, rb, wb):
                nb = 128 // (2 * half)
                cosb = cs[:, 0:128].unsqueeze(1).to_broadcast([128, nh, 128])
                dve(lambda: V.tensor_tensor(t1[:, 0:nh, :], src3, cosb, ALU.mult), rb, [t1])
                sv = src3.rearrange("p h (b s w) -> p h b s w", b=nb, s=2)
                tv = t2[:, 0:nh, :].rearrange("p h (b s w) -> p h b s w", b=nb, s=2)
                sn = cs[:, 128:256].rearrange("p (b s w) -> p b s w", b=nb, s=2)
                for b in range(nb):
                    for s_ in range(2):
                        sb_ = sn[:, b, s_, :].unsqueeze(1).to_broadcast([128, nh, half])
                        dve(lambda b=b, s_=s_, sb_=sb_: V.tensor_tensor(tv[:, :, b, s_, :], sv[:, :, b, 1 - s_, :], sb_, ALU.mult),
                            rb, [t2])
                dve(lambda: V.tensor_tensor(out3, t1[:, 0:nh, :], t2[:, 0:nh, :], ALU.add), [t1, t2], [wb])

            for t in range(NT):
                pt = pts[t % 2]
                rp = ropes[t % 2]
                junk, st, qr, rr, rfb, gs = junks[t % 2], sts[t % 2], qrs[t % 2], rrs[t % 2], rfbs[t % 2], gss[t % 2]
                tk.dma("sp", pt[:, :], p_d[t * 128:(t + 1) * 128, :], [p_d], [pt])
                tk.dma("sp", rp[:, 0, :], t_ropeA[t * 128:(t + 1) * 128, :], [], [rp])
                tk.dma("sp", rp[:, 1, :], t_ropeF[t * 128:(t + 1) * 128, :], [], [rp])
                tk.dma("sp", rp[:, 2, :], t_ropeB[t * 128:(t + 1) * 128, :], [], [rp])
                qk3 = pt[:, 0:1280].rearrange("p (h d) -> p h d", h=10)
                act(lambda: A.activation(junk[:, :], pt[:, 0:1280], AF.Square), [pt], [junk])
                dve(lambda: V.tensor_reduce(st[:, 0:10], junk[:, :].rearrange("p (h d) -> p h d", h=10), AX.X, ALU.add), [junk], [st])
                dve(lambda: V.tensor_scalar(st[:, 10:20], st[:, 0:10], 1.0 / 128, EPS, ALU.mult, ALU.add), [st], [st])
                act(lambda: A.activation(st[:, 20:30], st[:, 10:20], AF.Sqrt), [st], [st])
                dve(lambda: V.reciprocal(st[:, 30:40], st[:, 20:30]), [st], [st])
                dve(lambda: V.tensor_tensor(qn[:, :, :], qk3, st[:, 30:40].unsqueeze(2).to_broadcast([128, 10, 128]), ALU.mult),
                    [pt, st], [qn])
                dve(lambda: V.tensor_tensor(qn[:, :, :], qn[:, :, :], gqk[:, :, :], ALU.mult), [qn, gqk], [qn])
                rope(qn[:, :, :], 10, 32, rp[:, 0, :], qr[:, :].rearrange("p (h d) -> p h d", h=10), [qn, rp], qr)
                transpose_out(qr, 0, 4, qT_d, 0, t)
                transpose_out(qr, 512, 4, qT_d, 512, t)
                transpose_out(qr, 1024, 2, kT_d, 0, t)
                tk.dma("pool", v_d[t * 128:(t + 1) * 128, :], pt[:, 1280:1536], [pt], [v_d])
                tk.dma("pool", f_d[t * 128:(t + 1) * 128, :], pt[:, 1536:2048], [pt], [f_d])
                tk.dma("pool", rv_d[t * 128:(t + 1) * 128, :], pt[:, 3072:3584], [pt], [rv_d])
                dve(lambda: V.tensor_copy(rr[:, 0:4, :], pt[:, 2048:2560].rearrange("p (h d) -> p h d", h=4)), [pt], [rr])
                dve(lambda: V.tensor_scalar(rr[:, 4:8, :], pt[:, 2560:3072].rearrange("p (h d) -> p h d", h=4), 128.0 ** -0.5, None, ALU.mult),
                    [pt], [rr])
                for di in range(2):
                    rope(rr[:, :, :], 8, 64, rp[:, 1 + di, :], rfb[di][:, :].rearrange("p (h d) -> p h d", h=8), [rr, rp], rfb[di])
                    transpose_out(rfb[di], 0, 4, rT_d, di * 1024, t)
                    transpose_out(rfb[di], 512, 4, rT_d, di * 1024 + 512, t)
                    tk.dma("sp", rk_d[t * 128:(t + 1) * 128, di * 512:(di + 1) * 512], rfb[di][:, 512:1024], [rfb[di]], [rk_d])
                act(lambda: A.activation(gs[:, :], pt[:, 3584:4608], AF.Silu), [pt], [gs])
                tk.dma("sp", g_d[t * 128:(t + 1) * 128, :], gs[:, :], [gs], [g_d])
        tk.barrier()

    def stage_D(l):
        with ExitStack() as es:
            gq = tk.sb(es, "gqD", [128, 256], F32)
            stb = tk.sb(es, "stD", [128, 4], F32)
            kT = tk.sb(es, "kTD", [128, T], BF16)
            vv = tk.sb(es, "vD", [128, NT, 128], BF16)
            qTb = [[tk.sb(es, f"qTb{s_}_{i}", [128, 512], BF16) for i in range(2)] for s_ in range(2)]
            pT = [[tk.sb(es, f"pT{s_}_{i}", [128, 512], BF16) for i in range(2)] for s_ in range(2)]
            psS = [[tk.ps(es, f"psS{s_}_{i}", [128, 512]) for i in range(2)] for s_ in range(2)]
            psO = [tk.ps(es, f"psO{s_}", [128, 512]) for s_ in range(2)]
            psZ = [tk.ps(es, f"psZ{s_}", [128, 512]) for s_ in range(2)]
            accZ = [tk.sb(es, f"accZ{s_}", [128, 512], F32) for s_ in range(2)]
            rz = [tk.sb(es, f"rzD{s_}", [128, 512], F32) for s_ in range(2)]
            ob = [tk.sb(es, f"obD{s_}", [128, 512], BF16) for s_ in range(2)]
            tk.dma("sp", gq[:, 0:128], q_norm[l, :].partition_broadcast(128), [q_norm], [gq])
            tk.dma("sp", gq[:, 128:256], k_norm[l, :].partition_broadcast(128), [k_norm], [gq])
            dve(lambda: V.tensor_reduce(stb[:, 0:2], gq[:, :].rearrange("p (a d) -> p a d", a=2), AX.X, ALU.max, apply_absolute_value=True),
                [gq], [stb])
            dve(lambda: V.tensor_tensor(stb[:, 2:3], stb[:, 0:1], stb[:, 1:2], ALU.mult), [stb], [stb])
            dve(lambda: V.tensor_scalar(stb[:, 3:4], stb[:, 2:3], -(128.0 ** 0.5), None, ALU.mult), [stb], [stb])
            scale = 128.0 ** -0.5
            kq = 0
            for kvh in range(2):
                tk.dma("sp", kT[:, :], kT_d[kvh * 128:(kvh + 1) * 128, :], [kT_d], [kT])
                tk.dma("sp", vv[:, :, :], v_d[:, kvh * 128:(kvh + 1) * 128].rearrange("(c p) d -> p c d", p=128), [v_d], [vv])
                for hp in range(2):
                    heads = [kvh * 4 + 2 * hp, kvh * 4 + 2 * hp + 1]
                    blocks = [(0, 256, [0, 1])] + [(256 + qb * 512, 512, list(range(NT))) for qb in range(8)]
                    for (c0, nq, chunks) in blocks:
                        qbs = [qTb[s_][kq % 2] for s_ in range(2)]
                        kq += 1
                        for s_ in range(2):
                            hq = heads[s_]
                            tk.dma("sp", qbs[s_][:, 0:nq], qT_d[hq * 128:(hq + 1) * 128, c0:c0 + nq], [qT_d], [qbs[s_]])
                        nch = len(chunks)

                        def s_mm(s_, i):
                            kc = chunks[i]
                            mm(psS[s_][i % 2][:, 0:nq], kT[:, kc * 128:(kc + 1) * 128], qbs[s_][:, 0:nq], True, True,
                               [kT, qbs[s_]], [psS[s_][i % 2]])
                        s_mm(0, 0)
                        s_mm(1, 0)
                        for i in range(nch):
                            kc = chunks[i]
                            if i + 1 < nch:
                                s_mm(0, i + 1)
                                s_mm(1, i + 1)
                            for s_ in range(2):
                                p = pT[s_][i % 2]
                                act(lambda s_=s_, p=p: A.activation(p[:, 0:nq], psS[s_][i % 2][:, 0:nq], AF.Exp, bias=stb[:, 3:4], scale=scale),
                                    [psS[s_][i % 2], stb], [p])
                            for s_ in range(2):
                                p = pT[s_][i % 2]
                                mm(psO[s_][:, 0:nq], vv[:, kc, :], p[:, 0:nq], i == 0, i == nch - 1, [vv, p], [psO[s_]])
                                az = accZ[s_]
                                if i == 0:
                                    dve(lambda p=p, az=az: V.tensor_copy(az[:, 0:nq], p[:, 0:nq]), [p], [az])
                                else:
                                    dve(lambda p=p, az=az: V.tensor_tensor(az[:, 0:nq], az[:, 0:nq], p[:, 0:nq], ALU.add), [az, p], [az])
                        for s_ in range(2):
                            hq = heads[s_]
                            mm(psZ[s_][:, 0:nq], onesf[:, :], accZ[s_][:, 0:nq], True, True, [onesf, accZ[s_]], [psZ[s_]])
                            dve(lambda s_=s_: V.reciprocal(rz[s_][:, 0:nq], psZ[s_][:, 0:nq]), [psZ[s_]], [rz[s_]])
                            dve(lambda s_=s_: V.tensor_tensor(ob[s_][:, 0:nq], psO[s_][:, 0:nq], rz[s_][:, 0:nq], ALU.mult), [psO[s_], rz[s_]], [ob[s_]])
                            tk.dma("sp", mixT_d[hq * 128:(hq + 1) * 128, c0:c0 + nq], ob[s_][:, 0:nq], [ob[s_]], [mixT_d])
        tk.barrier()

    def stage_E(l):
        with ExitStack() as es:
            cc = tk.sb(es, "ccE", [128, 256], BF16)
            tk.dma("sp", cc[:, :], t_CC[:, :], [], [cc])
            for (tok0, n, CLt, SLt, kbw) in ((LC, S, t_CL, t_SL, 512), (0, LC, t_CL2, t_SL2, 256)):
                nch = n // 128
                with ExitStack() as es2:
                    xf = tk.sb(es2, "xfE", [128, nch, 512], BF16)
                    CLp = tk.sb(es2, "CLp", [128, nch, kbw], BF16)
                    SLp = tk.sb(es2, "SLp", [128, nch, kbw], BF16)
                    psA = tk.ps(es2, "psAE", [128, 512])
                    psB = tk.ps(es2, "psBE", [128, 512])
                    psY = tk.ps(es2, "psYE", [128, 512])
                    aT = tk.sb(es2, "aTE", [128, 512], BF16)
                    bT = tk.sb(es2, "bTE", [128, 512], BF16)
                    yT = [tk.sb(es2, f"yTE{i}", [128, 512], BF16) for i in range(2)]
                    tk.dma("sp", xf[:, :, :], f_d[tok0:tok0 + n, :].rearrange("(c p) d -> p c d", p=128), [f_d], [xf])
                    k = 0
                    for kb in range(n // kbw):
                        tk.dma("sp", CLp[:, :, :], CLt[:, kb * kbw:(kb + 1) * kbw].rearrange("(c p) k -> p c k", p=128), [], [CLp])
                        tk.dma("sp", SLp[:, :, :], SLt[:, kb * kbw:(kb + 1) * kbw].rearrange("(c p) k -> p c k", p=128), [], [SLp])
                        for g in range(4):
                            for c in range(nch):
                                mm(psA[:, 0:kbw], xf[:, c, g * 128:(g + 1) * 128], CLp[:, c, :], c == 0, c == nch - 1, [xf, CLp], [psA])
                            for c in range(nch):
                                mm(psB[:, 0:kbw], xf[:, c, g * 128:(g + 1) * 128], SLp[:, c, :], c == 0, c == nch - 1, [xf, SLp], [psB])
                            dve(lambda: V.tensor_copy(aT[:, 0:kbw], psA[:, 0:kbw]), [psA], [aT])
                            act(lambda: A.copy(bT[:, 0:kbw], psB[:, 0:kbw]), [psB], [bT])
                            mm(psY[:, 0:kbw], cc[:, 0:128], aT[:, 0:kbw], True, False, [cc, aT], [psY])
                            mm(psY[:, 0:kbw], cc[:, 128:256], bT[:, 0:kbw], False, True, [cc, bT], [psY])
                            y = yT[k % 2]
                            k += 1
                            evac(y[:, 0:kbw], psY[:, 0:kbw], [psY], [y])
                            tk.dma("sp", mixT_d[1024 + g * 128:1024 + (g + 1) * 128, tok0 + kb * kbw:tok0 + (kb + 1) * kbw],
                                   y[:, 0:kbw], [y], [mixT_d])
                tk.barrier()

    def stage_F(l):
        with ExitStack() as es:
            rt = tk.sb(es, "rtF", [128, 6 * 128 + 2], F32)
            lgb = tk.sb(es, "lgb", [128, 8], F32)
            tk.dma("sp", rt[:, :], t_ret[:, :], [], [rt])
            tk.dma("sp", lgb[:, :], rld[l, :].partition_broadcast(128), [rld], [lgb])
            maskT = tk.sb(es, "maskT", [128, 128], F32)
            qdr = tk.sb(es, "qdr", [128, 128], F32)
            kdc = tk.sb(es, "kdc", [128, 2], F32)
            rqT = tk.sb(es, "rqTF", [128, NT, 128], BF16)
            rkT = tk.sb(es, "rkTF", [128, NT, 128], BF16)
            rkt = tk.sb(es, "rktF", [128, NT, 128], BF16)
            rvt = tk.sb(es, "rvtF", [128, NT, 128], BF16)
            gt = tk.sb(es, "gtF", [128, NT, 128], F32)
            smA = tk.sb(es, "smAF", [128, NT, 128], BF16)
            Uall = tk.sb(es, "UallF", [128, NT, 128], F32)
            SbA = tk.sb(es, "SbAF", [128, NT, 128], BF16)
            obuf = tk.sb(es, "obufF", [128, NT, 128], F32)
            Sf = tk.sb(es, "SfF", [128, 128], F32)
            st = tk.sb(es, "stF", [128, 6, NT], F32)
            psg = [tk.ps(es, f"psgF{i}", [128, 4, 128]) for i in range(4)]
            kp = [0]
            groups = [list(range(g, min(g + 4, NT))) for g in range(0, NT, 4)]
            for di in range(2):
                for h in range(4):
                    li = di * 4 + h
                    act(lambda: A.activation(maskT[:, :], rt[:, (2 * di) * 128:(2 * di + 1) * 128], AF.Exp, scale=lgb[:, li:li + 1]),
                        [rt, lgb], [maskT])
                    dve(lambda: V.tensor_tensor(maskT[:, :], maskT[:, :], rt[:, (2 * di + 1) * 128:(2 * di + 2) * 128], ALU.mult),
                        [maskT, rt], [maskT])
                    act(lambda: A.activation(qdr[:, :], rt[:, (4 + di) * 128:(5 + di) * 128], AF.Exp, scale=lgb[:, li:li + 1]),
                        [rt, lgb], [qdr])
                    act(lambda: A.activation(kdc[:, 0:1], rt[:, 768 + di:769 + di], AF.Exp, scale=lgb[:, li:li + 1]), [rt, lgb], [kdc])
                    act(lambda: A.activation(kdc[:, 1:2], lgb[:, li:li + 1], AF.Exp, scale=128.0), [lgb], [kdc])
                    tk.dma("sp", rqT[:, :, :], rT_d[di * 1024 + h * 128:di * 1024 + (h + 1) * 128, :].rearrange("p (c n) -> p c n", n=128),
                           [rT_d], [rqT])
                    tk.dma("sp", rkT[:, :, :], rT_d[di * 1024 + 512 + h * 128:di * 1024 + 512 + (h + 1) * 128, :].rearrange("p (c n) -> p c n", n=128),
                           [rT_d], [rkT])
                    tk.dma("sp", rkt[:, :, :], rk_d[:, di * 512 + h * 128:di * 512 + (h + 1) * 128].rearrange("(c p) d -> p c d", p=128),
                           [rk_d], [rkt])
                    if di == 0 or True:
                        tk.dma("sp", rvt[:, :, :], rv_d[:, h * 128:(h + 1) * 128].rearrange("(c p) d -> p c d", p=128), [rv_d], [rvt])
                    tk.dma("sp", gt[:, :, :], g_d[:, di * 512 + h * 128:di * 512 + (h + 1) * 128].rearrange("(c p) d -> p c d", p=128),
                           [g_d], [gt])
                    dve(lambda: V.tensor_scalar(rkt[:, :, :], rkt[:, :, :], kdc[:, 0:1], None, ALU.mult), [rkt, kdc], [rkt])
                    for grp in groups:
                        n = len(grp)
                        c0 = grp[0]
                        ps = psg[kp[0] % 4]
                        kp[0] += 1
                        for j, c in enumerate(grp):
                            mm(ps[:, j, :], rkT[:, c, :], rqT[:, c, :], True, True, [rkT, rqT], [ps])
                        dve(lambda ps=ps, n=n, c0=c0: V.tensor_tensor(smA[:, c0:c0 + n, :], ps[:, 0:n, :],
                                                                     maskT[:, :].unsqueeze(1).to_broadcast([128, n, 128]), ALU.mult),
                            [ps, maskT], [smA])
                        ps = psg[kp[0] % 4]
                        kp[0] += 1
                        for j, c in enumerate(grp):
                            mm(ps[:, j, :], rkt[:, c, :], rvt[:, c, :], True, True, [rkt, rvt], [ps])
                        act(lambda ps=ps, n=n, c0=c0: A.copy(Uall[:, c0:c0 + n, :], ps[:, 0:n, :]), [ps], [Uall])
                    dve(lambda: V.tensor_tensor(rqT[:, :, :], rqT[:, :, :], qdr[:, :].unsqueeze(1).to_broadcast([128, NT, 128]), ALU.mult),
                        [rqT, qdr], [rqT])
                    order = list(range(NT)) if di == 0 else [1, 0] + list(range(NT - 1, 1, -1))
                    dve(lambda: V.memset(Sf[:, :], 0.0), [], [Sf])
                    dve(lambda: V.memset(SbA[:, order[0], :], 0.0), [], [SbA])
                    for k in range(NT - 1):
                        c = order[k]
                        cn = order[k + 1]
                        dve(lambda c=c: V.scalar_tensor_tensor(Sf[:, :], Sf[:, :], kdc[:, 1:2], Uall[:, c, :], ALU.mult, ALU.add),
                            [Sf, kdc, Uall], [Sf])
                        dve(lambda cn=cn: V.tensor_copy(SbA[:, cn, :], Sf[:, :]), [Sf], [SbA])
                    for grp in groups:
                        n = len(grp)
                        c0 = grp[0]
                        ps = psg[kp[0] % 4]
                        kp[0] += 1
                        for j, c in enumerate(grp):
                            mm(ps[:, j, :], smA[:, c, :], rvt[:, c, :], True, False, [smA, rvt], [ps])
                            mm(ps[:, j, :], rqT[:, c, :], SbA[:, c, :], False, True, [rqT, SbA], [ps])
                        evac(obuf[:, c0:c0 + n, :], ps[:, 0:n, :], [ps], [obuf])
                    dve(lambda: V.tensor_reduce(st[:, 0, :], obuf[:, :, :], AX.X, ALU.add), [obuf], [st])
                    dve(lambda: V.tensor_scalar(st[:, 1, :], st[:, 0, :], -1.0 / 128, None, ALU.mult), [st], [st])
                    dve(lambda: V.tensor_tensor(obuf[:, :, :], obuf[:, :, :], st[:, 1, :].unsqueeze(2).to_broadcast([128, NT, 128]), ALU.add),
                        [obuf, st], [obuf])
                    act(lambda: A.activation(Uall[:, :, :], obuf[:, :, :], AF.Square), [obuf], [Uall])
                    dve(lambda: V.tensor_reduce(st[:, 2, :], Uall[:, :, :], AX.X, ALU.add), [Uall], [st])
                    dve(lambda: V.tensor_scalar(st[:, 3, :], st[:, 2, :], 1.0 / 128, EPS, ALU.mult, ALU.add), [st], [st])
                    act(lambda: A.activation(st[:, 4, :], st[:, 3, :], AF.Sqrt), [st], [st])
                    dve(lambda: V.reciprocal(st[:, 5, :], st[:, 4, :]), [st], [st])
                    dve(lambda: V.tensor_tensor(obuf[:, :, :], obuf[:, :, :], st[:, 5, :].unsqueeze(2).to_broadcast([128, NT, 128]), ALU.mult),
                        [obuf, st], [obuf])
                    dve(lambda: V.tensor_tensor(obuf[:, :, :], obuf[:, :, :], gt[:, :, :], ALU.mult), [obuf, gt], [obuf])
                    tk.dma("sp", ro_d[:, di * 512 + h * 128:di * 512 + (h + 1) * 128].rearrange("(c p) e -> p c e", p=128), obuf[:, :, :],
                           [obuf], [ro_d])
        tk.barrier()

    def stage_G(l):
        with ExitStack() as es:
            ro = [tk.sb(es, f"roG{i}", [128, 1024], F32) for i in range(2)]
            rb = [tk.sb(es, f"rbG{i}", [128, 512], BF16) for i in range(2)]
            tr = [tk.sb(es, f"trG{i}", [128, 512], BF16) for i in range(2)]
            pst = [tk.ps(es, f"pstG{i}", [128, 512]) for i in range(2)]
            for t in range(NT):
                r_, b_, t_, ps = ro[t % 2], rb[t % 2], tr[t % 2], pst[t % 2]
                tk.dma("sp", r_[:, :], ro_d[t * 128:(t + 1) * 128, :], [ro_d], [r_])
                dve(lambda: V.tensor_tensor(b_[:, :], r_[:, 0:512], r_[:, 512:1024], ALU.add), [r_], [b_])
                for j in range(4):
                    mm(ps[:, j * 128:(j + 1) * 128], b_[:, j * 128:(j + 1) * 128], identb[:, :], True, True, [b_, identb], [ps])
                evac(t_[:, :], ps[:, :], [ps], [t_])
                tk.dma("sp", mixT_d[1536:2048, t * 128:(t + 1) * 128].rearrange("(j p) n -> p j n", p=128),
                       t_[:, :].rearrange("p (j n) -> p j n", j=4), [t_], [mixT_d])
        tk.barrier()

    def stage_H(l, affT):
        with ExitStack() as es:
            wo = tk.sb(es, "woH", [128, 16, D], BF16)
            gm = tk.sb(es, "gmH", [128, D], F32)
            GT1 = tk.sb(es, "GT1", [128, D], F32)
            G2_ = tk.sb(es, "G2", [128, D], F32)
            SH2_ = tk.sb(es, "SH2", [128, D], F32)
            GT = [GT1, GT1]
            G2 = [G2_, G2_]
            SH2 = [SH2_, SH2_]
            wr = tk.sb(es, "wrH", [128, 16, 16], F32)
            xts = [tk.sb(es, f"xtH{i}", [128, D], F32) for i in range(2)]
            mT = [tk.sb(es, f"mTH{i}", [128, 16, 128], BF16) for i in range(2)]
            tmps = [tk.sb(es, f"tmpH{i}", [128, D], F32) for i in range(2)]
            h2bs = [tk.sb(es, f"h2bH{i}", [128, D], BF16) for i in range(2)]
            h2Th = tk.sb(es, "h2ThH", [128, 16, 128], BF16)
            h2Tl = tk.sb(es, "h2TlH", [128, 16, 128], BF16)
            h2lo = tk.sb(es, "h2loH", [128, D], BF16)
            wrh = tk.sb(es, "wrhH", [128, 16, 16], BF16)
            wrl = tk.sb(es, "wrlH", [128, 16, 16], BF16)
            st = tk.sb(es, "stH", [128, 8], F32)
            lg = tk.sb(es, "lgH", [128, 16], F32)
            psY = [tk.ps(es, f"psYH{i}", [128, 512]) for i in range(2)]
            psT = [tk.ps(es, f"psTH{i}", [128, 512]) for i in range(2)]
            psL = tk.ps(es, "psLH", [128, 16])
            psAT = tk.ps(es, "psATH", [16, 128])
            for nb in range(4):
                tk.dma("pool", wo[:, :, nb * 512:(nb + 1) * 512],
                       w_out[l * D:(l + 1) * D, nb * 512:(nb + 1) * 512].rearrange("(c p) n -> p c n", p=128), [w_out], [wo])
            tk.dma("sp", gm[:, :], norm_ffn[l, :].partition_broadcast(128), [norm_ffn], [gm])
            tk.dma("sp", wr[:, :, :], w_router[l * D:(l + 1) * D, :].rearrange("(c p) e -> p c e", p=128), [w_router], [wr])
            dve(lambda: V.tensor_copy(wrh[:, :, :], wr[:, :, :]), [wr], [wrh])
            dve(lambda: V.tensor_tensor(wrl[:, :, :], wr[:, :, :], wrh[:, :, :], ALU.subtract), [wr, wrh], [wrl])
            for t in range(NT):
                w = 1 if t < 2 else 0
                if t in (0, 2):
                    tk.dma("sp", GT[w][:, :], modrow(l, w, 2), [mod_d], [GT[w]])
                    tk.dma("sp", G2[w][:, :], modrow(l, w, 4), [mod_d], [G2[w]])
                    tk.dma("sp", SH2[w][:, :], modrow(l, w, 3), [mod_d], [SH2[w]])
                    dve(lambda w=w: V.tensor_tensor(G2[w][:, :], G2[w][:, :], gm[:, :], ALU.mult), [G2[w], gm], [G2[w]])
                m = mT[t % 2]
                xt, tmp, h2b = xts[t % 2], tmps[t % 2], h2bs[t % 2]
                junk = h2b
                h2f = tmp
                rows = slice(t * 128, (t + 1) * 128)
                tk.dma("sp", xt[:, :], xcur[rows, :], [xcur], [xt])
                tk.dma("sp", m[:, :, :], mixT_d[:, rows].rearrange("(c p) n -> p c n", p=128), [mixT_d], [m])
                for nb in range(4):
                    ps = psY[nb % 2]
                    cs = slice(nb * 512, (nb + 1) * 512)
                    for c in range(16):
                        mm(ps[:, :], m[:, c, :], wo[:, c, cs], c == 0, c == 15, [m, wo], [ps])
                    dve(lambda ps=ps, cs=cs, w=w: V.tensor_tensor(tmp[:, cs], ps[:, :], GT[w][:, cs], ALU.mult), [ps, GT[w]], [tmp])
                    dve(lambda cs=cs: V.tensor_tensor(xt[:, cs], xt[:, cs], tmp[:, cs], ALU.add), [xt, tmp], [xt])
                tk.dma("sp", xcur[rows, :], xt[:, :], [xt], [xcur])
                rms_rstd((junk, st), xt, D)
                dve(lambda w=w: V.scalar_tensor_tensor(tmp[:, :], xt[:, :], st[:, 3:4], G2[w][:, :], ALU.mult, ALU.mult), [xt, st, G2[w]], [tmp])
                dve(lambda w=w: V.tensor_tensor(h2f[:, :], tmp[:, :], SH2[w][:, :], ALU.add), [tmp, SH2[w]], [h2f])
                act(lambda: A.copy(h2b[:, :], h2f[:, :]), [h2f], [h2b])
                tk.dma("sp", h2_d[rows, :], h2b[:, :], [h2b], [h2_d])
                dve(lambda: V.tensor_tensor(h2lo[:, :], h2f[:, :], h2b[:, :], ALU.subtract), [h2f, h2b], [h2lo])
                kt_ = 0
                for (srcb, dstT) in ((h2b, h2Th), (h2lo, h2Tl)):
                    for g4 in range(4):
                        ps = psT[kt_ % 2]
                        kt_ += 1
                        for j in range(4):
                            c = g4 * 4 + j
                            mm(ps[:, j * 128:(j + 1) * 128], srcb[:, c * 128:(c + 1) * 128], identb[:, :], True, True, [srcb, identb], [ps])
                        evac(dstT[:, g4 * 4:(g4 + 1) * 4, :], ps[:, :].rearrange("p (j n) -> p j n", j=4), [ps], [dstT])
                combos = [(h2Th, wrh), (h2Th, wrl), (h2Tl, wrh)]
                for ci_, (hT_, w_) in enumerate(combos):
                    for c in range(16):
                        mm(psL[:, :], hT_[:, c, :], w_[:, c, :], ci_ == 0 and c == 0, ci_ == 2 and c == 15, [hT_, w_], [psL])
                dve(lambda: V.reduce_max(st[:, 4:5], psL[:, :], axis=AX.X), [psL], [st])
                dve(lambda: V.tensor_scalar(st[:, 5:6], st[:, 4:5], -1.0, None, ALU.mult), [st], [st])
                act(lambda: A.activation(lg[:, :], psL[:, :], AF.Exp, bias=st[:, 5:6], scale=1.0, accum_out=st[:, 6:7]), [psL, st], [lg, st])
                dve(lambda: V.reciprocal(st[:, 7:8], st[:, 6:7]), [st], [st])
                dve(lambda t=t: V.tensor_scalar(aff_all[:, t, :], lg[:, :], st[:, 7:8], None, ALU.mult), [lg, st], [aff_all])
                mm(psAT[:, :], aff_all[:, t, :], identf[:, :], True, True, [aff_all, identf], [psAT])
                evac(affT[:, rows], psAT[:, :], [psAT], [affT])
        tk.barrier()

    def stage_I(l, affT):
        with ExitStack() as es:
            lo = tk.sb(es, "loI", [16, 2], F32)
            hi = tk.sb(es, "hiI", [16, 2], F32)
            mid = tk.sb(es, "midI", [16, 2], F32)
            cnt = tk.sb(es, "cntI", [16, 2], F32)
            ge = tk.sb(es, "geI", [16, 2], F32)
            d1 = tk.sb(es, "d1I", [16, 2], F32)
            junk = tk.sb(es, "junkI", [16, S], F32)
            dg = tk.sb(es, "dgI", [16, 2, 16], F32)
            thrb = tk.sb(es, "thrbI", [128, 2, 16], F32)
            sel = tk.sb(es, "selI", [128, NT, 16], F32)
            selb = tk.sb(es, "selbI", [128, NT, 16], BF16)
            tmp = tk.sb(es, "tmpI", [128, 16], F32)
            psb = tk.ps(es, "psbI", [128, 16])
            pss = [tk.ps(es, f"pssI{i}", [128, 16]) for i in range(2)]
            pst = [tk.ps(es, f"pstI{i}", [16, 128]) for i in range(2)]
            sidxT = tk.sb(es, "sidxTI", [16, T], F32)
            gateT = tk.sb(es, "gateTI", [16, T], F32)
            dve(lambda: V.memset(lo[:, :], 0.0), [], [lo])
            dve(lambda: V.memset(hi[:, :], 1.0), [], [hi])
            sets = [(0, LC, CAP_C), (LC, S, CAP_L)]
            for it in range(34):
                dve(lambda: V.tensor_tensor(mid[:, :], lo[:, :], hi[:, :], ALU.add), [lo, hi], [mid])
                dve(lambda: V.tensor_scalar(mid[:, :], mid[:, :], 0.5, None, ALU.mult), [mid], [mid])
                for si, (c0, n, cap) in enumerate(sets):
                    dve(lambda si=si, c0=c0, n=n: V.tensor_scalar(junk[:, 0:n], affT[:, c0:c0 + n], mid[:, si:si + 1], None, ALU.is_ge),
                        [affT, mid], [junk])
                    dve(lambda si=si, n=n: V.reduce_sum(cnt[:, si:si + 1], junk[:, 0:n], axis=AX.X), [junk], [cnt])
                    dve(lambda si=si, cap=cap: V.tensor_scalar(ge[:, si:si + 1], cnt[:, si:si + 1], float(cap) - 0.5, None, ALU.is_ge), [cnt], [ge])
                dve(lambda: V.tensor_tensor(d1[:, :], mid[:, :], lo[:, :], ALU.subtract), [mid, lo], [d1])
                dve(lambda: V.tensor_tensor(d1[:, :], d1[:, :], ge[:, :], ALU.mult), [d1, ge], [d1])
                dve(lambda: V.tensor_tensor(lo[:, :], lo[:, :], d1[:, :], ALU.add), [lo, d1], [lo])
                dve(lambda: V.tensor_tensor(d1[:, :], hi[:, :], mid[:, :], ALU.subtract), [hi, mid], [d1])
                dve(lambda: V.tensor_tensor(d1[:, :], d1[:, :], ge[:, :], ALU.mult), [d1, ge], [d1])
                dve(lambda: V.tensor_tensor(hi[:, :], mid[:, :], d1[:, :], ALU.add), [mid, d1], [hi])
            for si in range(2):
                dve(lambda si=si: V.tensor_scalar(dg[:, si, :], identf[0:16, 0:16], lo[:, si:si + 1], None, ALU.mult), [identf, lo], [dg])
                mm(psb[:, :], onesf[0:16, :], dg[:, si, :], True, True, [onesf, dg], [psb])
                dve(lambda si=si: V.tensor_copy(thrb[:, si, :], psb[:, :]), [psb], [thrb])
            dve(lambda: V.tensor_tensor(sel[:, 0:2, :], aff_all[:, 0:2, :], thrb[:, 0, :].unsqueeze(1).to_broadcast([128, 2, 16]), ALU.is_ge),
                [aff_all, thrb], [sel])
            dve(lambda: V.tensor_tensor(sel[:, 2:NT, :], aff_all[:, 2:NT, :], thrb[:, 1, :].unsqueeze(1).to_broadcast([128, NT - 2, 16]), ALU.is_ge),
                [aff_all, thrb], [sel])
            dve(lambda: V.tensor_tensor(gate_all[:, :, :], aff_all[:, :, :], sel[:, :, :], ALU.mult), [aff_all, sel], [gate_all])
            dve(lambda: V.tensor_copy(selb[:, :, :], sel[:, :, :]), [sel], [selb])
            k = 0
            for (t0, t1, off) in ((0, 2, float(CAP_L)), (2, NT, 0.0)):
                for t in range(t0, t1):
                    ps = pss[k % 2]
                    k += 1
                    for c in range(t0, t):
                        mm(ps[:, :], onesb[:, :], selb[:, c, :], c == t0, False, [onesb, selb], [ps])
                    mm(ps[:, :], ustrb[:, :], selb[:, t, :], t == t0, True, [ustrb, selb], [ps])
                    dve(lambda ps=ps, off=off: V.tensor_scalar(tmp[:, :], ps[:, :], off + 1.0, None, ALU.add), [ps], [tmp])
                    dve(lambda t=t: V.tensor_tensor(tmp[:, :], tmp[:, :], sel[:, t, :], ALU.mult), [tmp, sel], [tmp])
                    dve(lambda t=t: V.tensor_scalar(sidx_all[:, t, :], tmp[:, :], -1.0, None, ALU.add), [tmp], [sidx_all])
            for t in range(NT):
                rows = slice(t * 128, (t + 1) * 128)
                mm(pst[0][:, :], sidx_all[:, t, :], identf[:, :], True, True, [sidx_all, identf], [pst[0]])
                dve(lambda rows=rows: V.tensor_copy(sidxT[:, rows], pst[0][:, :]), [pst[0]], [sidxT])
                mm(pst[1][:, :], gate_all[:, t, :], identf[:, :], True, True, [gate_all, identf], [pst[1]])
                act(lambda rows=rows: A.copy(gateT[:, rows], pst[1][:, :]), [pst[1]], [gateT])
            lo_ = tk.sb(es, "loTI", [16, T], F32)
            hib = tk.sb(es, "hibI", [16, T], BF16)
            lob = tk.sb(es, "lobI", [16, T], BF16)
            gtb = tk.sb(es, "gtbI", [16, T], BF16)
            dve(lambda: V.tensor_scalar(sidxT[:, :], sidxT[:, :], 1.0, None, ALU.add), [sidxT], [sidxT])
            cb = tk.sb(es, "cbI", [16, T], BF16)
            dve(lambda: V.tensor_scalar(hib[:, :], sidxT[:, :], 256.0, None, ALU.min), [sidxT], [hib])
            dve(lambda: V.tensor_scalar(lo_[:, :], sidxT[:, :], 256.0, None, ALU.min), [sidxT], [lo_])
            dve(lambda: V.tensor_tensor(sidxT[:, :], sidxT[:, :], lo_[:, :], ALU.subtract), [sidxT, lo_], [sidxT])
            dve(lambda: V.tensor_scalar(lob[:, :], sidxT[:, :], 256.0, None, ALU.min), [sidxT], [lob])
            dve(lambda: V.tensor_scalar(lo_[:, :], sidxT[:, :], 256.0, None, ALU.min), [sidxT], [lo_])
            dve(lambda: V.tensor_tensor(cb[:, :], sidxT[:, :], lo_[:, :], ALU.subtract), [sidxT, lo_], [cb])
            act(lambda: A.copy(gtb[:, :], gateT[:, :]), [gateT], [gtb])
            tk.dma("sp", shl_d[0:16, :], hib[:, :], [hib], [shl_d])
            tk.dma("sp", shl_d[16:32, :], lob[:, :], [lob], [shl_d])
            tk.dma("sp", shl_d[32:48, :], cb[:, :], [cb], [shl_d])
            tk.dma("sp", gtb_d[:, :], gtb[:, :], [gtb], [gtb_d])
        tk.barrier()

    def stage_J(l):
        with ExitStack() as es:
            h2b = tk.sb(es, "h2bJ", [128, NT, 512], BF16)
            P = [tk.sb(es, f"PJ{i}", [128, 512], BF16) for i in range(3)]
            psG = [tk.ps(es, f"psGJ{i}", [128, 512]) for i in range(4)]
            psC = tk.ps(es, "psCJ", [128, 4, 32])
            xg = [tk.sb(es, f"xgJ{i}", [128, 4, NSL], BF16) for i in range(2)]
            k = 0
            for db in range(4):
                tk.dma("sp", h2b[:, :, :], h2_d[:, db * 512:(db + 1) * 512].rearrange("(t p) d -> p t d", p=128), [h2_d], [h2b])
                for e in range(NE):
                    xo = xg[e % 2]
                    for t in range(NT):
                        p = P[k % 3]
                        k += 1
                        if t < 2:
                            dve(lambda p=p, t=t, e=e: V.tensor_scalar(p[:, 0:32], iota5[:, 0:32], float(CAP_L), sidx_all[:, t, e:e + 1],
                                                                      ALU.add, ALU.is_equal), [iota5, sidx_all], [p])
                            for j in range(4):
                                mm(psC[:, j, :], h2b[:, t, j * 128:(j + 1) * 128], p[:, 0:32], t == 0, t == 1, [h2b, p], [psC])
                        else:
                            dve(lambda p=p, t=t, e=e: V.tensor_scalar(p[:, :], iota5[:, :], sidx_all[:, t, e:e + 1], None, ALU.is_equal),
                                [iota5, sidx_all], [p])
                            for j in range(4):
                                mm(psG[j][:, :], h2b[:, t, j * 128:(j + 1) * 128], p[:, :], t == 2, t == NT - 1, [h2b, p], [psG[j]])
                    for j in range(4):
                        evac(xo[:, j, 0:CAP_L], psG[j][:, :], [psG[j]], [xo])
                    evac(xo[:, :, CAP_L:NSL], psC[:, :, :], [psC], [xo])
                    tk.dma("sp", xg_d[e * D + db * 512:e * D + (db + 1) * 512, :].rearrange("(j p) s -> p j s", p=128), xo[:, :, :], [xo], [xg_d])
        tk.barrier()

    def stage_K(l):
        with ExitStack() as es:
            wgs = [tk.sb(es, f"wgK{i}", [128, 16, FF], BF16) for i in range(2)]
            wus = [tk.sb(es, f"wuK{i}", [128, 16, FF], BF16) for i in range(2)]
            wd = tk.sb(es, "wdK", [128, 8, D], BF16)
            xg0 = tk.sb(es, "xgK", [128, 16, NSL], BF16)
            gT = tk.sb(es, "gTK", [128, 8, NSL], BF16)
            sa = [tk.sb(es, f"saK{i}", [128, NSL], F32) for i in range(2)]
            psA = [tk.ps(es, f"psAK{i}", [128, 512]) for i in range(2)]
            psU = [tk.ps(es, f"psUK{i}", [128, 512]) for i in range(2)]
            psA2 = tk.ps(es, "psA2K", [128, 2, 32])
            psY = [tk.ps(es, f"psYK{i}", [128, 512]) for i in range(2)]
            yo = [tk.sb(es, f"yoK{i}", [128, 512], BF16) for i in range(2)]
            ky = 0

            def load_gu(e):
                r0 = (l * NE + e) * D
                for hf in range(2):
                    fs = slice(hf * 512, (hf + 1) * 512)
                    tk.dma("pool", wgs[e % 2][:, :, fs], w_gate[r0:r0 + D, fs].rearrange("(c p) f -> p c f", p=128), [w_gate], [wgs[e % 2]])
                    tk.dma("pool", wus[e % 2][:, :, fs], w_up[r0:r0 + D, fs].rearrange("(c p) f -> p c f", p=128), [w_up], [wus[e % 2]])

            def load_d(e):
                r1 = (l * NE + e) * FF
                for nb in range(4):
                    cs = slice(nb * 512, (nb + 1) * 512)
                    tk.dma("pool", wd[:, :, cs], w_down[r1:r1 + FF, cs].rearrange("(c p) n -> p c n", p=128), [w_down], [wd])

            load_gu(0)
            load_d(0)
            for e in range(NE):
                wg, wu = wgs[e % 2], wus[e % 2]
                x_ = xg0
                tk.dma("sp", x_[:, :, :], xg_d[e * D:(e + 1) * D, :].rearrange("(c p) s -> p c s", p=128), [xg_d], [x_])
                if e + 1 < NE:
                    load_gu(e + 1)
                for fc in range(8):
                    pa, pu, s_ = psA[fc % 2], psU[fc % 2], sa[fc % 2]
                    fsl = slice(fc * 128, (fc + 1) * 128)
                    for c in range(16):
                        mm(pa[:, :], wg[:, c, fsl], x_[:, c, 0:CAP_L], c == 0, c == 15, [wg, x_], [pa])
                    for c in range(16):
                        mm(pu[:, :], wu[:, c, fsl], x_[:, c, 0:CAP_L], c == 0, c == 15, [wu, x_], [pu])
                    for c in range(16):
                        mm(psA2[:, 0, :], wg[:, c, fsl], x_[:, c, CAP_L:NSL], c == 0, c == 15, [wg, x_], [psA2])
                    for c in range(16):
                        mm(psA2[:, 1, :], wu[:, c, fsl], x_[:, c, CAP_L:NSL], c == 0, c == 15, [wu, x_], [psA2])
                    act(lambda: A.activation(s_[:, 0:CAP_L], pa[:, :], AF.Silu), [pa], [s_])
                    act(lambda: A.activation(s_[:, CAP_L:NSL], psA2[:, 0, :], AF.Silu), [psA2], [s_])
                    dve(lambda fc=fc: V.tensor_tensor(gT[:, fc, 0:CAP_L], s_[:, 0:CAP_L], pu[:, :], ALU.mult), [s_, pu], [gT])
                    dve(lambda fc=fc: V.tensor_tensor(gT[:, fc, CAP_L:NSL], s_[:, CAP_L:NSL], psA2[:, 1, :], ALU.mult), [s_, psA2], [gT])
                for stl in range(5):
                    s0 = stl * 128
                    m = 128 if stl < 4 else CAP_C
                    for nb in range(4):
                        ps, y_ = psY[ky % 2], yo[ky % 2]
                        ky += 1
                        for fc in range(8):
                            mm(ps[0:m, :], gT[:, fc, s0:s0 + m], wd[:, fc, nb * 512:(nb + 1) * 512], fc == 0, fc == 7, [gT, wd], [ps])
                        evac(y_[0:m, :], ps[0:m, :], [ps], [y_])
                        tk.dma("sp", y_d[e * NSL + s0:e * NSL + s0 + m, nb * 512:(nb + 1) * 512], y_[0:m, :], [y_], [y_d])
                if e + 1 < NE:
                    load_d(e + 1)
        tk.barrier()

    def stage_L(l, last):
        with ExitStack() as es:
            yb = tk.sb(es, "ybL", [128, NE, 5, 512], BF16)
            GT2 = [tk.sb(es, f"GT2{w}", [128, D], F32) for w in range(2)]
            psS = [tk.ps(es, f"psSL{i}", [128, 512]) for i in range(2)]
            psG = [tk.ps(es, f"psGL{i}", [128, 512]) for i in range(2)]
            psX = [tk.ps(es, f"psXL{i}", [128, 512]) for i in range(4)]
            eq = [tk.sb(es, f"eqL{i}", [128, 512], F32) for i in range(3)]
            ST = [tk.sb(es, f"STL{i}", [128, 512], BF16) for i in range(8)]
            xt = [tk.sb(es, f"xtL{i}", [128, 512], F32) for i in range(2)]
            tmp = [tk.sb(es, f"tmpL{i}", [128, 512], F32) for i in range(2)]
            eselHL = tk.sb(es, "eselHL", [48, NE, 128], BF16)
            eselG = tk.sb(es, "eselG", [16, NE, 128], BF16)
            shl = tk.sb(es, "shlL", [48, T], BF16)
            gtb = tk.sb(es, "gtbL", [16, T], BF16)
            iop1 = tk.sb(es, "iop1L", [128, 8], F32)
            tk.dma("pool", eselHL[:, :, :], t_esel[:, :].rearrange("k (e m) -> k e m", e=NE), [], [eselHL])
            dve(lambda: V.tensor_copy(eselG[:, :, :], identf[0:16, 0:16].unsqueeze(2).to_broadcast([16, 16, 128])), [identf], [eselG])
            dve(lambda: V.tensor_scalar(iop1[:, :], iotap[:, :], 1.0, None, ALU.add), [iotap], [iop1])
            tk.dma("sp", shl[:, :], shl_d[:, :], [shl_d], [shl])
            tk.dma("sp", gtb[:, :], gtb_d[:, :], [gtb_d], [gtb])
            for w in range(2):
                tk.dma("sp", GT2[w][:, :], modrow(l, w, 5), [mod_d], [GT2[w]])
            dve(lambda: V.memset(yb[:, :, :, :], 0.0), [], [yb])
            ks = 0
            kx = 0
            for db in range(4):
                dcs = slice(db * 512, (db + 1) * 512)
                for e in range(NE):
                    tk.dma("sp", yb[:, e, 0:4, :], y_d[e * NSL:e * NSL + CAP_L, dcs].rearrange("(c p) d -> p c d", p=128), [y_d], [yb])
                    tk.dma("sp", yb[0:CAP_C, e, 4, :], y_d[e * NSL + CAP_L:(e + 1) * NSL, dcs], [y_d], [yb])
                blocks = [(0, 256, True)] + [(256 + qb * 512, 512, False) for qb in range(8)]
                for (c0, ntok, isctx) in blocks:
                    if last and isctx:
                        continue
                    nsub = ntok // 128
                    chunks = [4] if isctx else [0, 1, 2, 3]
                    np_ = CAP_C if isctx else 128

                    def bc(e):
                        mm(psS[e % 2][:, 0:ntok], eselHL[:, e, :], shl[:, c0:c0 + ntok], True, True, [eselHL, shl], [psS[e % 2]])
                        mm(psG[e % 2][:, 0:ntok], eselG[:, e, :], gtb[:, c0:c0 + ntok], True, True, [eselG, gtb], [psG[e % 2]])
                    bc(0)
                    for e in range(NE):
                        if e + 1 < NE:
                            bc(e + 1)
                        pS, pG = psS[e % 2], psG[e % 2]
                        gq_ = eq[e % 3]
                        act(lambda: A.copy(gq_[0:np_, 0:ntok], pG[0:np_, 0:ntok]), [pG], [gq_])
                        sts = []
                        for ci, c in enumerate(chunks):
                            s_ = ST[ks % 8]
                            ks += 1
                            dve(lambda s_=s_, c=c: V.scalar_tensor_tensor(
                                s_[0:np_, 0:ntok], pS[0:np_, 0:ntok], iop1[0:np_, c:c + 1], gq_[0:np_, 0:ntok], ALU.is_equal, ALU.mult),
                                [pS, iop1, gq_], [s_])
                            sts.append(s_)
                        for ci, c in enumerate(chunks):
                            s_ = sts[ci]
                            first = (e == 0 and ci == 0)
                            lastm = (e == NE - 1 and ci == len(chunks) - 1)
                            for m in range(nsub):
                                mm(psX[m][:, :], s_[0:np_, m * 128:(m + 1) * 128], yb[0:np_, e, c, :], first, lastm, [s_, yb], [psX[m]])
                    w = 1 if isctx else 0
                    for m in range(nsub):
                        x_, t_ = xt[kx % 2], tmp[kx % 2]
                        kx += 1
                        rows = slice(c0 + m * 128, c0 + (m + 1) * 128)
                        tk.dma("sp", x_[:, :], xcur[rows, dcs], [xcur], [x_])
                        dve(lambda m=m, t_=t_, w=w: V.tensor_tensor(t_[:, :], psX[m][:, :], GT2[w][:, dcs], ALU.mult), [psX[m], GT2[w]], [t_])
                        dve(lambda x_=x_, t_=t_: V.tensor_tensor(x_[:, :], x_[:, :], t_[:, :], ALU.add), [x_, t_], [x_])
                        if last:
                            tk.dma("sp", out_d[c0 - LC + m * 128:c0 - LC + (m + 1) * 128, dcs], x_[:, :], [x_], [out_d])
                        else:
                            tk.dma("sp", xcur[rows, dcs], x_[:, :], [x_], [xcur])
        tk.barrier()

    tk.marks = []

    def mark(name):
        tk.marks.append((name, dict(tk.ccnt)))
    stage_adaln()
    mark("ada")
    for l in range(depth):
        stage_A(l)
        mark(f"A{l}")
        stage_B(l)
        mark(f"B{l}")
        stage_C(l)
        mark(f"C{l}")
        stage_D(l)
        mark(f"D{l}")
        stage_E(l)
        mark(f"E{l}")
        stage_F(l)
        mark(f"F{l}")
        stage_G(l)
        mark(f"G{l}")
        with ExitStack() as esHI:
            affT = tk.sb(esHI, "affT", [16, T], F32)
            stage_H(l, affT)
            mark(f"H{l}")
            stage_I(l, affT)
            mark(f"I{l}")
        stage_J(l)
        mark(f"J{l}")
        stage_K(l)
        mark(f"K{l}")
        stage_L(l, l == depth - 1)
        mark(f"L{l}")
    tk.barrier()
    top.close()
    return nc, tk


def _tables():
    f32 = np.float32
    bf = ml_dtypes.bfloat16
    tb = {}
    tb["t_ident"] = np.eye(128, dtype=f32)
    tb["t_ustr"] = np.triu(np.ones((128, 128), f32), 1)
    es_ = np.zeros((48, 16, 128), f32)
    for e in range(16):
        es_[e, e, :] = 1.0
        es_[16 + e, e, :] = 1.0
        es_[32 + e, e, :] = 1.0
    tb["t_esel"] = es_.reshape(48, 16 * 128)

    def rope_tab(pos, half):
        freqs = (f32(10000.0) ** (-np.arange(half, dtype=f32) / f32(half))).astype(f32)
        ang = (pos.astype(f32)[:, None] * freqs[None, :]).astype(f32)
        return np.cos(ang).astype(f32), np.sin(ang).astype(f32)

    rows = np.repeat(np.arange(S // 64), 64)
    cols = np.tile(np.arange(64), S // 64)
    cr, sr = rope_tab(rows, 32)
    cc_, sc_ = rope_tab(cols, 32)
    cosA = np.concatenate([cr, cr, cc_, cc_], axis=1)
    sinA = np.concatenate([-sr, sr, -sc_, sc_], axis=1)
    A_ = np.zeros((T, 256), f32)
    A_[:LC, :128] = 1.0
    A_[LC:, :128] = cosA
    A_[LC:, 128:] = sinA
    tb["t_ropeA"] = A_
    posF = np.arange(T)
    posB = np.concatenate([LC - 1 - np.arange(LC), LC + (S - 1 - np.arange(S))])
    for nm, pos in (("t_ropeF", posF), ("t_ropeB", posB)):
        c_, s_ = rope_tab(pos, 64)
        tb[nm] = np.concatenate([c_, c_, -s_, s_], axis=1).astype(f32)

    def dft(n):
        idx = np.arange(n, dtype=np.int64)
        m = (idx[:, None] * idx[None, :]) % n
        ang = 2.0 * np.pi * m.astype(np.float64) / n
        return np.cos(ang), np.sin(ang)
    c, s = dft(S)
    tb["t_CL"] = (c / np.sqrt(S)).astype(bf)
    tb["t_SL"] = (s / np.sqrt(S)).astype(bf)
    c, s = dft(LC)
    tb["t_CL2"] = (c / np.sqrt(LC)).astype(bf)
    tb["t_SL2"] = (s / np.sqrt(LC)).astype(bf)
    c, s = dft(128)
    tb["t_CC"] = np.concatenate([c / np.sqrt(128.0), -s / np.sqrt(128.0)], axis=1).astype(bf)
    j = np.arange(128, dtype=f32)[:, None]
    i = np.arange(128, dtype=f32)[None, :]
    Df = np.maximum(i - j, 0.0)
    Mf = (i >= j).astype(f32)
    Db = np.maximum(j - i, 0.0)
    Mb = (j >= i).astype(f32)
    rowf = np.broadcast_to(i + 1.0, (128, 128))
    rowb = np.broadcast_to(128.0 - i, (128, 128))
    colf = 127.0 - j
    colb = j + 0.0
    tb["t_ret"] = np.ascontiguousarray(np.concatenate([Df, Mf, Db, Mb, rowf, rowb, colf, colb], axis=1).astype(f32))
    return tb


_CACHE = {}


def kernel(x, c, ctx, c_ctx, w_ada, b_ada, norm_mix, norm_ffn, w_in, q_norm, k_norm,
           ret_log_decay, w_out, w_router, w_gate, w_up, w_down):
    depth = int(os.environ.get("MK_DEPTH", w_ada.shape[0]))
    ncores = int(os.environ.get("MK_CORES", x.shape[0]))
    f = np.ascontiguousarray
    tb = _tables()
    shared = {
        "w_ada": f(w_ada[:depth].reshape(depth * D, 6 * D)),
        "b_ada": f(b_ada[:depth]),
        "norm_mix": f(norm_mix[:depth]),
        "norm_ffn": f(norm_ffn[:depth]),
        "w_in": f(w_in[:depth].reshape(depth * D, INW)),
        "q_norm": f(q_norm[:depth]),
        "k_norm": f(k_norm[:depth]),
        "rld": f(ret_log_decay[:depth].reshape(depth, 8)),
        "w_out": f(w_out[:depth].reshape(depth * D, D)),
        "w_router": f(w_router[:depth].reshape(depth * D, NE)),
        "w_gate": f(w_gate[:depth].reshape(depth * NE * D, FF)),
        "w_up": f(w_up[:depth].reshape(depth * NE * D, FF)),
        "w_down": f(w_down[:depth].reshape(depth * NE * FF, D)),
    }
    shared.update(tb)
    in_maps = []
    for b in range(ncores):
        m = dict(shared)
        m["x"] = f(x[b])
        m["ctx"] = f(ctx[b])
        m["cvec"] = f(np.stack([c[b], c_ctx], axis=0))
        in_maps.append(m)
    nc, tk = build(depth)
    res = run_bass_kernel_spmd(nc, in_maps, core_ids=list(range(ncores)))
    out = np.stack([np.asarray(res.results[b]["out"]) for b in range(ncores)], axis=0)
    return out.astype(np.float32)
```
